# Optimizing a Trainium2 kernel written in Bass

```python
import math
import jax, jax.numpy as jnp
from jax import lax
import numpy as np

D_MODEL = 4096
BATCH = 4
SEQ = 2048
DEPTH = 1

GRID_W = 64
CTX_LEN = 256
N_MOD = 9
EPS = 1e-6
MACARON_W = 0.5

SSD_EXPAND = 2
SSD_D_INNER = SSD_EXPAND * D_MODEL
SSD_HEAD_DIM = 64
SSD_N_HEADS = SSD_D_INNER // SSD_HEAD_DIM
SSD_N_GROUPS = 8
SSD_D_STATE = 128
SSD_GN = SSD_N_GROUPS * SSD_D_STATE
SSD_CONV_DIM = SSD_D_INNER + 2 * SSD_GN
SSD_CONV = 4
SSD_CHUNK = 128

RG_WIDTH = (D_MODEL * 4 // 3) // 256 * 256
RG_N_BLOCKS = 16
RG_BLOCK = RG_WIDTH // RG_N_BLOCKS
RG_CONV = 4
RG_C = 8.0

D_FF = 11008

_S1 = SSD_D_INNER
_S2 = _S1 + SSD_CONV_DIM
_S3 = _S2 + 2 * SSD_N_HEADS
_S4 = _S3 + RG_WIDTH
_S5 = _S4 + RG_WIDTH
P_IN = _S5 + 2 * D_MODEL

kernel_name = 'hybrid_ssd_rglru_dit_block'


def rms_norm(x, g):
    xf = x.astype(jnp.float32)
    xf = xf * lax.rsqrt(jnp.mean(xf * xf, axis=-1, keepdims=True) + EPS)
    return xf.astype(x.dtype) * g


def mod_slot(mod, k):
    return mod[:, :, 3 * k], mod[:, :, 3 * k + 1], mod[:, :, 3 * k + 2]


def swiglu(u, w_up, w_down):
    gate, up = jnp.split(u @ w_up, 2, axis=-1)
    return (jax.nn.silu(gate) * up) @ w_down


def ffn_sublayer(h, mod, k, g_pre, g_post, w_up, w_down):
    shift, scale, gate = mod_slot(mod, k)
    u = rms_norm(h, g_pre) * (1.0 + scale) + shift
    return h + MACARON_W * gate * rms_norm(swiglu(u, w_up, w_down), g_post)


def centred_dwconv(x, w, b):
    k = w.shape[0]
    left = k // 2
    y = lax.conv_general_dilated(x, w[:, None, :], window_strides=(1,), padding=[(left, k - 1 - left)],
                                 dimension_numbers=('NWC', 'WIO', 'NWC'), feature_group_count=x.shape[-1])
    return y + b


def flip(t):
    return jnp.flip(t, axis=1)


def to_col_major(t, rows):
    b, s, ch = t.shape
    return t.reshape(b, rows, GRID_W, ch).transpose(0, 2, 1, 3).reshape(b, s, ch)


def to_row_major(t, rows):
    b, s, ch = t.shape
    return t.reshape(b, GRID_W, rows, ch).transpose(0, 2, 1, 3).reshape(b, s, ch)


def segsum(a):
    t = a.shape[-1]
    cs = jnp.cumsum(a, axis=-1)
    ss = cs[..., :, None] - cs[..., None, :]
    mask = jnp.tril(jnp.ones((t, t), dtype=bool))
    return jnp.where(mask, ss, -jnp.inf)


def ssd_scan(x, dt, a_log, bm, cm, h0, compute_y):
    bsz, t, h, p = x.shape
    g, n = bm.shape[2], bm.shape[3]
    e = h // g
    nc, cl = t // SSD_CHUNK, SSD_CHUNK
    da = (dt * -jnp.exp(a_log)).astype(jnp.float32)
    xc = (x * dt[..., None]).reshape(bsz, nc, cl, g, e, p)
    bc = bm.reshape(bsz, nc, cl, g, n)
    cc = cm.reshape(bsz, nc, cl, g, n)
    da = da.reshape(bsz, nc, cl, g, e).transpose(0, 3, 4, 1, 2)
    a_cs = jnp.cumsum(da, axis=-1)
    decay_states = jnp.exp(a_cs[..., -1:] - a_cs)
    states = jnp.einsum('bclgn,bgecl,bclgep->bcgepn', bc, decay_states, xc)
    states = jnp.concatenate([h0.reshape(bsz, 1, g, e, p, n), states], axis=1)
    chunk_decay = jnp.exp(segsum(jnp.pad(a_cs[..., -1], ((0, 0), (0, 0), (0, 0), (1, 0)))))
    new_states = jnp.einsum('bgezc,bcgepn->bzgepn', chunk_decay, states)
    final = new_states[:, -1].reshape(bsz, h, p, n)
    if not compute_y:
        return None, final
    prev_states = new_states[:, :-1]
    lmat = jnp.exp(segsum(da))
    cb = jnp.einsum('bclgn,bcsgn->bgcls', cc, bc)
    y_diag = jnp.einsum('bgcls,bgecls,bcsgep->bclgep', cb, lmat, xc)
    y_off = jnp.einsum('bclgn,bcgepn,bgecl->bclgep', cc, prev_states, jnp.exp(a_cs))
    y = (y_diag + y_off).reshape(bsz, t, h, p).astype(x.dtype)
    return y, final


def bidir_ssd(xs, dt_f, dt_b, bm, cm, a_log, h0_f, h0_b, compute_y):
    y_f, s_f = ssd_scan(xs, dt_f, a_log[0], bm, cm, h0_f, compute_y)
    y_b, s_b = ssd_scan(flip(xs), flip(dt_b), a_log[1], flip(bm), flip(cm), h0_b, compute_y)
    y = y_f + flip(y_b) if compute_y else None
    return y, s_f, s_b


def ssd_features(proj, p):
    z, xbc, dt_raw = proj[..., :_S1], proj[..., _S1:_S2], proj[..., _S2:_S3]
    xbc = jax.nn.silu(centred_dwconv(xbc, p['ssd_conv_w'], p['ssd_conv_b']))
    bsz, t, _ = xbc.shape
    xs = xbc[..., :SSD_D_INNER].reshape(bsz, t, SSD_N_HEADS, SSD_HEAD_DIM)
    bm = xbc[..., SSD_D_INNER:SSD_D_INNER + SSD_GN].reshape(bsz, t, SSD_N_GROUPS, SSD_D_STATE)
    cm = xbc[..., SSD_D_INNER + SSD_GN:].reshape(bsz, t, SSD_N_GROUPS, SSD_D_STATE)
    dt = jax.nn.softplus(dt_raw + p['ssd_dt_bias'])
    return z, xs, bm, cm, dt[..., :SSD_N_HEADS], dt[..., SSD_N_HEADS:]


def ssd_output(y, xs, z, p):
    bsz, t = y.shape[0], y.shape[1]
    y = (y + p['ssd_d'][:, None] * xs).reshape(bsz, t, SSD_D_INNER) * jax.nn.silu(z)
    y = rms_norm(y.reshape(bsz, t, SSD_N_GROUPS, SSD_D_INNER // SSD_N_GROUPS),
                 p['ssd_norm_g'].reshape(SSD_N_GROUPS, SSD_D_INNER // SSD_N_GROUPS))
    return y.reshape(bsz, t, SSD_D_INNER) @ p['w_ssd_out']


def rglru_coeffs(xr, w_a, b_a, w_x, b_x, lam):
    bsz, t, w = xr.shape
    xf = xr.astype(jnp.float32)
    xb = xf.reshape(bsz, t, RG_N_BLOCKS, RG_BLOCK)
    r = jax.nn.sigmoid(jnp.einsum('btki,kij->btkj', xb, w_a).reshape(bsz, t, w) + b_a)
    i = jax.nn.sigmoid(jnp.einsum('btki,kij->btkj', xb, w_x).reshape(bsz, t, w) + b_x)
    log_a = -RG_C * r * jax.nn.softplus(-lam)
    a = jnp.exp(log_a)
    b = jnp.sqrt(-jnp.expm1(2.0 * log_a)) * (i * xf)
    return a, b


def linear_scan(a, b, h0):
    b = b.at[:, 0].add(a[:, 0] * h0)
    def combine(left, right):
        return left[0] * right[0], right[0] * left[1] + right[1]
    _, h = lax.associative_scan(combine, (a, b), axis=1)
    return h


def bidir_rglru(xr, p, h0_f, h0_b, compute_y):
    a, b = rglru_coeffs(xr, p['rg_w_a'][0], p['rg_b_a'][0], p['rg_w_x'][0], p['rg_b_x'][0], p['rg_lam'][0])
    h_f = linear_scan(a, b, h0_f)
    a, b = rglru_coeffs(flip(xr), p['rg_w_a'][1], p['rg_b_a'][1], p['rg_w_x'][1], p['rg_b_x'][1], p['rg_lam'][1])
    h_b = linear_scan(a, b, h0_b)
    y = (h_f + flip(h_b)).astype(xr.dtype) if compute_y else None
    return y, h_f[:, -1], h_b[:, -1]


def merge_branches(gates_raw, y_ssd, y_rg, w_out):
    g_ssd, g_rg = jnp.split(jax.nn.sigmoid(gates_raw), 2, axis=-1)
    return (g_ssd * y_ssd + g_rg * y_rg) @ w_out


def token_mixer(u_ctx, u_lat, rows, p, with_ctx_out):
    pc = u_ctx @ p['w_in']
    pl = u_lat @ p['w_in']
    bsz = u_lat.shape[0]
    zc, xc, bc, cc, dfc, dbc = ssd_features(pc, p)
    zl, xl, bl, cl, dfl, dbl = ssd_features(pl, p)
    s0 = jnp.zeros((bsz, SSD_N_HEADS, SSD_HEAD_DIM, SSD_D_STATE), jnp.float32)
    yc, sf, sb = bidir_ssd(xc, dfc, dbc, bc, cc, p['ssd_a_log'], s0, s0, with_ctx_out)
    yl, _, _ = bidir_ssd(xl, dfl, dbl, bl, cl, p['ssd_a_log'], sf, sb, True)
    ssd_lat = ssd_output(yl, xl, zl, p)
    xr_c = centred_dwconv(pc[..., _S4:_S5], p['rg_conv_w'], p['rg_conv_b'])
    xr_l = centred_dwconv(to_col_major(pl[..., _S4:_S5], rows), p['rg_conv_w'], p['rg_conv_b'])
    r0 = jnp.zeros((bsz, RG_WIDTH), jnp.float32)
    hc, rf, rb = bidir_rglru(xr_c, p, r0, r0, with_ctx_out)
    hl, _, _ = bidir_rglru(xr_l, p, rf, rb, True)
    rg_lat = (jax.nn.gelu(pl[..., _S3:_S4]) * to_row_major(hl, rows)) @ p['w_rg_out']
    out_lat = merge_branches(pl[..., _S5:], ssd_lat, rg_lat, p['w_out'])
    if not with_ctx_out:
        return None, out_lat
    ssd_ctx = ssd_output(yc, xc, zc, p)
    rg_ctx = (jax.nn.gelu(pc[..., _S3:_S4]) * hc) @ p['w_rg_out']
    out_ctx = merge_branches(pc[..., _S5:], ssd_ctx, rg_ctx, p['w_out'])
    return out_ctx, out_lat


def setup_inputs(seed: int = 0) -> dict:
    key = jax.random.key(seed)
    ks = jax.random.split(key, 32)
    L = DEPTH
    H = SSD_N_HEADS
    def nrm(k, shape, scale):
        return jax.random.normal(k, shape, jnp.float32) * scale
    dt0 = jnp.exp(jax.random.uniform(ks[12], (L, 2 * H), jnp.float32, math.log(1e-3), math.log(1e-1)))
    u = jax.random.uniform(ks[22], (L, 2, RG_WIDTH), jnp.float32, 0.9, 0.999)
    s = u ** (1.0 / RG_C)
    return {
        'x': nrm(ks[0], (BATCH, SEQ, D_MODEL), 1.0),
        'c': nrm(ks[1], (BATCH, D_MODEL), 1.0),
        'ctx': nrm(ks[2], (BATCH, CTX_LEN, D_MODEL), 1.0),
        'c_ctx': nrm(ks[3], (D_MODEL,), 1.0),
        'w_ada': nrm(ks[4], (L, D_MODEL, N_MOD * D_MODEL), 0.5 * D_MODEL ** -0.5),
        'b_ada': nrm(ks[5], (L, N_MOD * D_MODEL), 0.02),
        'norm_g': 1.0 + nrm(ks[6], (L, 6, D_MODEL), 0.02),
        'ffn_w_up': nrm(ks[7], (L, 2, D_MODEL, 2 * D_FF), D_MODEL ** -0.5),
        'ffn_w_down': nrm(ks[8], (L, 2, D_FF, D_MODEL), D_FF ** -0.5),
        'w_in': nrm(ks[9], (L, D_MODEL, P_IN), D_MODEL ** -0.5),
        'ssd_conv_w': nrm(ks[10], (L, SSD_CONV, SSD_CONV_DIM), SSD_CONV ** -0.5),
        'ssd_conv_b': nrm(ks[11], (L, SSD_CONV_DIM), 0.02),
        'ssd_dt_bias': dt0 + jnp.log(-jnp.expm1(-dt0)),
        'ssd_a_log': jnp.log(jax.random.uniform(ks[13], (L, 2, H), jnp.float32, 1.0, 16.0)),
        'ssd_d': 1.0 + nrm(ks[14], (L, H), 0.02),
        'ssd_norm_g': 1.0 + nrm(ks[15], (L, SSD_D_INNER), 0.02),
        'w_ssd_out': nrm(ks[16], (L, SSD_D_INNER, D_MODEL), SSD_D_INNER ** -0.5),
        'rg_conv_w': nrm(ks[17], (L, RG_CONV, RG_WIDTH), RG_CONV ** -0.5),
        'rg_conv_b': nrm(ks[18], (L, RG_WIDTH), 0.02),
        'rg_w_a': nrm(ks[19], (L, 2, RG_N_BLOCKS, RG_BLOCK, RG_BLOCK), RG_BLOCK ** -0.5),
        'rg_b_a': nrm(ks[20], (L, 2, RG_WIDTH), 0.02),
        'rg_w_x': nrm(ks[21], (L, 2, RG_N_BLOCKS, RG_BLOCK, RG_BLOCK), RG_BLOCK ** -0.5),
        'rg_b_x': nrm(ks[23], (L, 2, RG_WIDTH), 0.02),
        'rg_lam': jnp.log(s) - jnp.log1p(-s),
        'w_rg_out': nrm(ks[24], (L, RG_WIDTH, D_MODEL), RG_WIDTH ** -0.5),
        'w_out': nrm(ks[25], (L, D_MODEL, D_MODEL), D_MODEL ** -0.5),
    }


def reference(x, c, ctx, c_ctx, w_ada, b_ada, norm_g, ffn_w_up, ffn_w_down, w_in,
              ssd_conv_w, ssd_conv_b, ssd_dt_bias, ssd_a_log, ssd_d, ssd_norm_g, w_ssd_out,
              rg_conv_w, rg_conv_b, rg_w_a, rg_b_a, rg_w_x, rg_b_x, rg_lam, w_rg_out, w_out):
    rows = x.shape[1] // GRID_W
    bsz = x.shape[0]
    h_lat, h_ctx = x, ctx
    sc, scc = jax.nn.silu(c), jax.nn.silu(c_ctx)
    for l in range(DEPTH):
        last = l == DEPTH - 1
        mod_lat = (sc @ w_ada[l] + b_ada[l]).reshape(bsz, 1, N_MOD, D_MODEL)
        mod_ctx = (scc @ w_ada[l] + b_ada[l]).reshape(1, 1, N_MOD, D_MODEL)
        g = norm_g[l]
        p = {
            'w_in': w_in[l], 'ssd_conv_w': ssd_conv_w[l], 'ssd_conv_b': ssd_conv_b[l],
            'ssd_dt_bias': ssd_dt_bias[l], 'ssd_a_log': ssd_a_log[l], 'ssd_d': ssd_d[l],
            'ssd_norm_g': ssd_norm_g[l], 'w_ssd_out': w_ssd_out[l],
            'rg_conv_w': rg_conv_w[l], 'rg_conv_b': rg_conv_b[l], 'rg_w_a': rg_w_a[l], 'rg_b_a': rg_b_a[l],
            'rg_w_x': rg_w_x[l], 'rg_b_x': rg_b_x[l], 'rg_lam': rg_lam[l], 'w_rg_out': w_rg_out[l],
            'w_out': w_out[l],
        }
        h_lat = ffn_sublayer(h_lat, mod_lat, 0, g[0], g[1], ffn_w_up[l, 0], ffn_w_down[l, 0])
        h_ctx = ffn_sublayer(h_ctx, mod_ctx, 0, g[0], g[1], ffn_w_up[l, 0], ffn_w_down[l, 0])
        sh_l, scl_l, gt_l = mod_slot(mod_lat, 1)
        sh_c, scl_c, gt_c = mod_slot(mod_ctx, 1)
        u_lat = rms_norm(h_lat, g[2]) * (1.0 + scl_l) + sh_l
        u_ctx = rms_norm(h_ctx, g[2]) * (1.0 + scl_c) + sh_c
        m_ctx, m_lat = token_mixer(u_ctx, u_lat, rows, p, not last)
        h_lat = h_lat + gt_l * rms_norm(m_lat, g[3])
        h_lat = ffn_sublayer(h_lat, mod_lat, 2, g[4], g[5], ffn_w_up[l, 1], ffn_w_down[l, 1])
        if not last:
            h_ctx = h_ctx + gt_c * rms_norm(m_ctx, g[3])
            h_ctx = ffn_sublayer(h_ctx, mod_ctx, 2, g[4], g[5], ffn_w_up[l, 1], ffn_w_down[l, 1])
    return h_lat
```

```python
import contextlib
import math
import os as _os
import numpy as np
import concourse.bass as bass
import concourse.mybir as mybir
from concourse.bass_utils import run_bass_kernel_spmd

F32 = mybir.dt.float32
BF16 = mybir.dt.bfloat16
ALU = mybir.AluOpType
AF = mybir.ActivationFunctionType
EPS = 1e-6
NEG = -30000.0
PRECAST = True
DEFER_ADA = True
CACHE_W = True


class Cfg:
    def __init__(self, D=4096, DFF=11008, SEQ=2048, CTX=256, HB=None):
        self.D, self.DFF, self.SEQ, self.CTX = D, DFF, SEQ, CTX
        self.KD = D // 128
        self.KF = DFF // 128
        self.NT = SEQ + CTX
        self.GW = 64
        self.ROWS = SEQ // 64
        self.DI = 2 * D
        self.H = self.DI // 64
        self.E = self.H // 8
        self.XC = self.DI // 128
        self.GN = 1024
        self.RGW = (D * 4 // 3) // 256 * 256
        self.RGB = self.RGW // 16
        self.NBC = (self.RGB + 127) // 128
        self.RC = 16 * self.NBC
        self.S1 = self.DI
        self.S2 = self.S1 + self.DI + 2 * self.GN
        self.S3 = self.S2 + 2 * self.H
        self.S4 = self.S3 + self.RGW
        self.S5 = self.S4 + self.RGW
        self.PIN = self.S5 + 2 * D
        self.OC_Z = 0
        self.OC_X = self.OC_Z + self.XC
        self.OC_B = self.OC_X + self.XC
        self.OC_C = self.OC_B + 8
        self.OC_DT = self.OC_C + 8
        self.OC_RG = self.OC_DT + 2
        self.OC_RX = self.OC_RG + self.RC
        self.OC_G = self.OC_RX + self.RC
        self.NOC = self.OC_G + 2 * self.KD
        self.NCL = SEQ // 128
        self.NCC = CTX // 128
        self.NCH = self.NCL + self.NCC
        self.HB = HB or min(8, self.E)
        self.NQ = self.E // self.HB
        self.TP = 512
        self.NS = 512
        o = 0
        self.V = {}
        for nm, n in (("cw", (self.XC + 16) * 4), ("cb", self.XC + 16), ("dtb", 2), ("alog", 2), ("sng", self.XC),
                      ("rcw", self.RC * 4), ("rcb", self.RC), ("rba", 2 * self.RC), ("rbx", 2 * self.RC), ("rlam", 2 * self.RC)):
            self.V[nm] = o
            o += n
        self.NV = o


def KW(*a, **k):
    return (a, k)


def _call(e, meth, args, kw):
    try:
        return getattr(e, meth)(*args, **kw)
    except Exception:
        print("FAILED INSTR", meth, [str(a)[:200] for a in args], {k: str(v)[:200] for k, v in kw.items()})
        raise


class Buf:
    def __init__(self, name):
        self.name = name
        self.last_w = None
        self.readers = []
        self.dsem = None
        self.psum = False


class Prog:
    ENG = ('pe', 'act', 'dve', 'pool', 'sp')

    def __init__(self, nc, ndsem=56):
        self.nc = nc
        self.q = {e: [] for e in self.ENG}
        self.cnt = {}
        self.known = {e: {} for e in self.ENG}
        self.semh = {}
        self.free_d = []
        self.dbufs = []
        self.nins = 0
        for e in self.ENG:
            self._sem('e:' + e)
        self.free_d = {'pool': [], 'sp': [], 'act': []}
        for i in range(ndsem):
            k = 'd:%d' % i
            self._sem(k)
            self.free_d['pool' if i < 10 else 'sp'].append(k)

    def _sem(self, key):
        if key not in self.semh:
            self.semh[key] = self.nc.alloc_semaphore(key.replace(':', '_'))
            self.cnt[key] = 0
        return self.semh[key]

    def _wait(self, eng, tok):
        if tok is None:
            return
        key, val = tok
        if key[0] == 'd':
            val = self.cnt[key]
        elif key == 'e:' + eng:
            if eng == 'pe' or val > self.cnt[key]:
                return
        if self.known[eng].get(key, 0) >= val:
            return
        self.known[eng][key] = val
        h = self.semh[key]
        self.q[eng].append(lambda e, h=h, val=val: e.wait_ge(h, val))

    def _deps(self, eng, reads, writes):
        for b in reads:
            self._wait(eng, b.last_w)
            if b.psum:
                for t in b.readers:
                    if t[0] != 'e:' + eng:
                        self._wait(eng, t)
        for b in writes:
            self._wait(eng, b.last_w)
            for t in b.readers:
                self._wait(eng, t)

    @staticmethod
    def _compact(toks):
        best = {}
        for k, v in toks:
            if best.get(k, 0) < v:
                best[k] = v
        return list(best.items())

    def op(self, eng, meth, akw, reads=(), writes=(), inc=True):
        args, kw = akw
        self._deps(eng, reads, writes)
        self.nins += 1
        key = 'e:' + eng
        h = self.semh[key]
        if inc:
            self.cnt[key] += 1
            tok = (key, self.cnt[key])
            self.q[eng].append(lambda e, meth=meth, args=args, kw=kw, h=h: _call(e, meth, args, kw).then_inc(h, 1))
        else:
            tok = (key, self.cnt[key] + 1)
            self.q[eng].append(lambda e, meth=meth, args=args, kw=kw: _call(e, meth, args, kw))
        for b in reads:
            b.readers.append(tok)
            if len(b.readers) > 48:
                b.readers = self._compact(b.readers)
        for b in writes:
            b.last_w = tok
            b.readers = []
        return tok

    def dma(self, eng, out_ap, in_ap, dst, src, **kw):
        self._deps(eng, [src], [dst])
        self.nins += 1
        if dst.dsem is None:
            dst.dsem = self.free_d[eng].pop()
            dst.dq = eng
            self.dbufs.append(dst)
        key = dst.dsem
        h = self.semh[key]
        self.cnt[key] += 16
        tok = (key, self.cnt[key])
        self.q[eng].append(lambda e, h=h, out_ap=out_ap, in_ap=in_ap, kw=kw: e.dma_start(out=out_ap, in_=in_ap, **kw).then_inc(h, 16))
        src.readers.append(tok)
        if len(src.readers) > 48:
            src.readers = self._compact(src.readers)
        dst.last_w = tok
        dst.readers = []
        return tok

    def barrier(self):
        for eng in self.ENG:
            for key in list(self.cnt):
                if self.cnt[key] > 0:
                    self._wait(eng, (key, self.cnt[key]))
        for b in self.dbufs:
            self.free_d[b.dq].append(b.dsem)
            b.dsem = None
        self.dbufs = []

    def emit(self):
        nc = self.nc
        with nc.Block() as block:
            @block.tensor
            def _(e):
                for f in self.q['pe']:
                    f(e)

            @block.scalar
            def _(e):
                for f in self.q['act']:
                    f(e)

            @block.vector
            def _(e):
                for f in self.q['dve']:
                    f(e)

            @block.gpsimd
            def _(e):
                for f in self.q['pool']:
                    f(e)

            @block.sync
            def _(e):
                for f in self.q['sp']:
                    f(e)


def make_passes(C, nlat, nctx, TP):
    segs = [(0, nlat, 0)]
    if nctx:
        segs.append((C.SEQ, nctx, 1))
    passes, cur, room = [], [], TP
    for (s0, n, w) in segs:
        while n > 0:
            t = min(n, room)
            cur.append((s0, t, w))
            s0 += t
            n -= t
            room -= t
            if room == 0:
                passes.append(cur)
                cur, room = [], TP
    if cur:
        passes.append(cur)
    return passes


def slices_of(n, NS):
    out, c = [], 0
    while c < n:
        t = min(NS, n - c)
        out.append((c, t))
        c += t
    return out


def build(C, upto=99, dbg=()):
    nc = bass.Bass("TRN2", target_bir_lowering=False)
    P = Prog(nc)
    D, KD, KF, NT, SEQ, CTX, TP, NS = C.D, C.KD, C.KF, C.NT, C.SEQ, C.CTX, C.TP, C.NS
    H, E, XC, RC = C.H, C.E, C.XC, C.RC

    def din(name, shape, dt=F32):
        return nc.dram_tensor(name, list(shape), dt, kind="ExternalInput").ap()

    def dsc(name, shape, dt):
        return nc.dram_tensor(name, list(shape), dt).ap()

    dbg_out = {}

    def dbgout(name, shape):
        dbg_out[name] = nc.dram_tensor("dbg_" + name, list(shape), F32, kind="ExternalOutput").ap()
        return dbg_out[name]

    xT = din("xT", [D, NT])
    scT = din("scT", [128, KD * 2])
    w_ada_t = din("w_ada_t", [9 * KD, 128, KD * 128])
    b_adaT = din("b_adaT", [128, 9 * KD])
    normgT = din("normgT", [128, 6 * KD])
    wup_t = [din("wup%d_t" % i, [2 * KF, 128, KD * 128]) for i in range(2)]
    wdn_t = [din("wdn%d_t" % i, [KD, 128, KF * 128]) for i in range(2)]
    win_t = din("win_t", [C.NOC, 128, KD * 128])
    vecs = din("vecs", [128, C.NV])
    drow = din("drow", [1, 128])
    rgw_t = din("rgw_t", [64 * C.NBC, 128, C.NBC * 128])
    wso_t = din("wso_t", [KD, 128, XC * 128])
    wro_t = din("wro_t", [KD, 128, RC * 128])
    wo_t = din("wo_t", [KD, 128, KD * 128])
    outT = nc.dram_tensor("outT", [D, SEQ], F32, kind="ExternalOutput").ap()

    h1T = dsc("h1T", [D, NT], F32)
    u1T = dsc("u1T", [D, NT], BF16)
    pz = dsc("pz", [C.DI, SEQ], BF16)
    x_tm = dsc("x_tm", [NT, C.DI], BF16)
    pB = dsc("pB", [C.GN, NT], BF16)
    pC = dsc("pC", [C.GN, NT], BF16)
    B_tm = dsc("B_tm", [NT, C.GN], BF16)
    pdt = dsc("pdt", [256, NT], F32)
    prg = dsc("prg", [RC * 128, SEQ], BF16)
    prx = dsc("prx", [RC * 128, NT], F32)
    pgt = dsc("pgt", [2 * D, SEQ], BF16)
    ynT = dsc("ynT", [C.DI, SEQ], BF16)
    rgyT = dsc("rgyT", [RC * 128, SEQ], BF16)
    sbin = dsc("sbin", [8, C.NCL, 128, E * 64], BF16)
    wupc = [dsc("wupc%d" % i, [KF, 128, 2 * KD * 128], BF16) for i in range(2)]
    wdnc = [dsc("wdnc%d" % i, [KD, 128, KF * 128], BF16) for i in range(2)]
    wsoc = dsc("wsoc", [KD, 128, XC * 128], BF16)
    wroc = dsc("wroc", [KD, 128, RC * 128], BF16)
    woc = dsc("woc", [KD, 128, KD * 128], BF16)
    h2T = dsc("h2T", [D, SEQ], F32)
    u2T = dsc("u2T", [D, SEQ], BF16)

    B_w = Buf("weights")
    B_xT, B_h1T, B_u1T, B_outT = Buf("xT"), Buf("h1T"), Buf("u1T"), Buf("outT")
    B_pz, B_xtm, B_pB, B_pC, B_Btm, B_pdt = Buf("pz"), Buf("x_tm"), Buf("pB"), Buf("pC"), Buf("B_tm"), Buf("pdt")
    B_prg, B_prx, B_pgt, B_ynT, B_rgyT, B_sbin = Buf("prg"), Buf("prx"), Buf("pgt"), Buf("ynT"), Buf("rgyT"), Buf("sbin")
    B_h2T, B_u2T = Buf("h2T"), Buf("u2T")
    B_cache = {k_: Buf(k_) for k_ in ("wupc0", "wdnc0", "wupc1", "wdnc1", "wsoc", "wroc", "woc")}

    with contextlib.ExitStack() as glob:
        def sbuf(stack, name, shape, dt=F32):
            t = stack.enter_context(nc.sbuf_tensor(name, list(shape), dt))
            return t, Buf(name)

        def psum(stack, name, shape, dt=F32):
            t = stack.enter_context(nc.psum_tensor(name, list(shape), dt))
            b = Buf(name)
            b.psum = True
            return t, b

        modT, B_mod = sbuf(glob, "modT", [128, 9 * KD, 2])
        ngT, B_ng = sbuf(glob, "ngT", [128, 6 * KD])
        ones_bf, B_ones = sbuf(glob, "ones_bf", [128, 128], BF16)
        ones32, B_ones32 = sbuf(glob, "ones32", [128, 128])
        ident32, B_id32 = sbuf(glob, "ident32", [128, 128])
        identb, B_idb = sbuf(glob, "identb", [128, 128], BF16)
        vec, B_vec = sbuf(glob, "vec", [128, C.NV])
        cf = {}
        for nm in ("s1_0", "s2_0", "co_0", "s1_1", "s2_1", "co_1", "s1_2", "s2_2", "co_2"):
            cf[nm] = sbuf(glob, "cf_" + nm, [128, KD, 2])
        P.dma('sp', ngT[:], normgT[:, :], B_ng, B_w)
        P.dma('sp', vec[:], vecs[:, :], B_vec, B_w)
        P.op('dve', 'memset', KW(ones_bf[:], 1.0), writes=[B_ones])
        P.op('dve', 'memset', KW(ones32[:], 1.0), writes=[B_ones32])
        P.op('pool', 'memset', KW(ident32[:], 1.0), writes=[B_id32])
        P.op('pool', 'affine_select', KW(out=ident32[:], in_=ident32[:], pattern=[[1, 128]], base=0, channel_multiplier=-1,
                                         compare_op=ALU.is_equal, fill=0.0), reads=[B_id32], writes=[B_id32])
        P.op('dve', 'tensor_copy', KW(out=identb[:], in_=ident32[:]), reads=[B_id32], writes=[B_idb])

        def V(nm, j=0, n=1):
            o = C.V[nm] + j
            return vec[:, o:o + n]

        scb, B_scb = sbuf(glob, "scb", [128, KD, 2], BF16)
        badT, B_bad = sbuf(glob, "badT", [128, 9 * KD])
        NADA0 = 5 * KD if (upto >= 5 and DEFER_ADA) else 9 * KD

        def ada_chunk(oc, wt, B_wt, pt, B_pt):
            P.dma('pool', wt[:, 0:KD * 128], w_ada_t[oc], B_wt, B_w, max_dma_last_dim=4096)
            for kc in range(KD):
                P.op('pe', 'matmul', KW(pt[:, 0:2], lhsT=wt[:, kc * 128:(kc + 1) * 128], rhs=scb[:, kc, :],
                                        start=(kc == 0), stop=(kc == KD - 1)),
                     reads=[B_wt, B_scb], writes=[B_pt], inc=(kc == KD - 1))
            P.op('dve', 'tensor_scalar', KW(out=modT[:, oc, :], in0=pt[:, 0:2], scalar1=badT[:, oc:oc + 1], scalar2=None,
                                            op0=ALU.add), reads=[B_pt, B_bad], writes=[B_mod])

        def mslot(j):
            return modT[:, j * KD:(j + 1) * KD, :]

        def ng(j):
            return ngT[:, j * KD:(j + 1) * KD].unsqueeze(2).to_broadcast([128, KD, 2])

        def cf_tables(k, which):
            gpre, gpost = ((0, 1), (2, 3), (4, 5))[k]
            s1, B_s1 = cf["s1_%d" % k]
            s2, B_s2 = cf["s2_%d" % k]
            co, B_co = cf["co_%d" % k]
            if 's' in which:
                P.op('dve', 'scalar_tensor_tensor', KW(out=s1[:], in0=mslot(3 * k + 1), scalar=1.0, in1=ng(gpre), op0=ALU.add, op1=ALU.mult),
                     reads=[B_mod, B_ng], writes=[B_s1])
                P.op('dve', 'tensor_copy', KW(out=s2[:], in_=mslot(3 * k)), reads=[B_mod], writes=[B_s2])
            if 'c' in which:
                P.op('dve', 'scalar_tensor_tensor', KW(out=co[:], in0=mslot(3 * k + 2), scalar=(1.0 if k == 1 else 0.5), in1=ng(gpost),
                                                       op0=ALU.mult, op1=ALU.mult), reads=[B_mod, B_ng], writes=[B_co])

        with contextlib.ExitStack() as ph:
            sc32, B_sc32 = sbuf(ph, "sc32", [128, KD * 2])
            NSL = 3
            wsl = [sbuf(ph, "wada%d" % i, [128, KD * 128], BF16) for i in range(NSL)]
            pss = [psum(ph, "p0ps%d" % i, [128, 512]) for i in range(4)]
            P.dma('sp', sc32[:], scT[:, :], B_sc32, B_w)
            P.dma('sp', badT[:], b_adaT[:, :], B_bad, B_w)
            P.op('act', 'activation', KW(out=scb[:].rearrange("p k t -> p (k t)"), in_=sc32[:], func=AF.Silu), reads=[B_sc32], writes=[B_scb])
            for oc in range(NADA0):
                wt, B_wt = wsl[oc % NSL]
                pt, B_pt = pss[oc % 4]
                ada_chunk(oc, wt, B_wt, pt, B_pt)
            cf_tables(0, 'sc')
            cf_tables(1, 's')
            if NADA0 == 9 * KD:
                cf_tables(1, 'c')
                cf_tables(2, 'sc')
            P.barrier()
        if 'mod' in dbg:
            o = dbgout("mod", [128, 9 * KD * 2])
            P.dma('sp', o[:, :], modT[:].rearrange("p a b -> p (a b)"), Buf("dbgmod"), B_mod)

        def rms_rstd(nps, sq, rstd, B_rstd, src_chunk, src_bufs, nchunks, nd, T):
            sls = slices_of(T, 512)
            pts = [nps() for _ in sls]
            for kc in range(nchunks):
                s_, B_s = sq[kc % 2]
                P.op('act', 'activation', KW(out=s_[:, 0:T], in_=src_chunk(kc), func=AF.Square), reads=src_bufs, writes=[B_s])
                for (pt, B_pt), (c0, n) in zip(pts, sls):
                    P.op('pe', 'matmul', KW(pt[:, 0:n], lhsT=ones_bf[:], rhs=s_[:, c0:c0 + n], start=(kc == 0), stop=(kc == nchunks - 1)),
                         reads=[B_s, B_ones], writes=[B_pt])
            for (pt, B_pt), (c0, n) in zip(pts, sls):
                P.op('act', 'activation', KW(out=rstd[:, c0:c0 + n], in_=pt[:, 0:n], func=AF.Sqrt, scale=1.0 / nd, bias=EPS),
                     reads=[B_pt], writes=[B_rstd])
            P.op('dve', 'reciprocal', KW(out=rstd[:, 0:T], in_=rstd[:, 0:T]), reads=[B_rstd], writes=[B_rstd])

        def ffn_phase(tag, l, h_src, B_hsrc, passes, u_src, h_dst, B_hdst, u_dst, B_udst, kpre, knext):
            B_wupc, B_wdnc = B_cache["wupc%d" % l], B_cache["wdnc%d" % l]
            pre = PRECAST and l == 1 and upto >= 4
            with contextlib.ExitStack() as ph:
                uT, B_uT = sbuf(ph, tag + "uT", [128, KD, TP], BF16)
                gT, B_gT = sbuf(ph, tag + "gT", [128, max(KF, 2 * KD) * TP], BF16)
                WS = max(KF, 2 * KD) * 128
                wsl = [sbuf(ph, tag + "w%d" % i, [128, WS], BF16) for i in range(2)]
                tmp = [sbuf(ph, tag + "tmp%d" % i, [128, TP]) for i in range(2)]
                sq = [sbuf(ph, tag + "sq%d" % i, [128, TP], BF16) for i in range(2)]
                rstd, B_rstd = sbuf(ph, tag + "rstd", [128, TP])
                hch = [sbuf(ph, tag + "hch%d" % i, [128, TP]) for i in range(3)]
                pss = [psum(ph, tag + "ps%d" % i, [128, 512]) for i in range(8)]
                st = {'ps': 0, 'w': 0}
                hT32 = gT[:].bitcast(F32).rearrange("p (k t) -> p k t", t=TP)

                def nps():
                    r = pss[st['ps'] % 8]
                    st['ps'] += 1
                    return r

                def nw():
                    r = wsl[st['w'] % 2]
                    st['w'] += 1
                    return r

                def normmod(k, segs, T):
                    s1, B_s1 = cf["s1_%d" % k]
                    s2, B_s2 = cf["s2_%d" % k]
                    for kc in range(KD):
                        t_, B_t = tmp[kc % 2]
                        P.op('dve', 'tensor_tensor', KW(out=t_[:, 0:T], in0=hT32[:, kc, 0:T], in1=rstd[:, 0:T], op=ALU.mult),
                             reads=[B_gT, B_rstd], writes=[B_t])
                        c0 = 0
                        for (_, n, which) in segs:
                            P.op('act', 'activation', KW(out=uT[:, kc, c0:c0 + n], in_=t_[:, c0:c0 + n], func=AF.Identity,
                                                         scale=s1[:, kc, which:which + 1], bias=s2[:, kc, which:which + 1]),
                                 reads=[B_t, B_s1, B_s2], writes=[B_uT])
                            c0 += n

                for ip, segs in enumerate(passes):
                    T = sum(n for (_, n, _) in segs)
                    sls = slices_of(T, NS)
                    if u_src is None:
                        c0 = 0
                        for (s0, n, which) in segs:
                            P.dma('sp', hT32[:, 0:KD, c0:c0 + n], h_src[:, s0:s0 + n].rearrange("(k p) t -> p k t", p=128), B_gT, B_hsrc)
                            c0 += n
                        rms_rstd(nps, sq, rstd, B_rstd, lambda kc: hT32[:, kc, 0:T], [B_gT], KD, D, T)
                        normmod(kpre, segs, T)
                    else:
                        c0 = 0
                        for (s0, n, which) in segs:
                            P.dma('sp', uT[:, :, c0:c0 + n], u_src[0][:, s0:s0 + n].rearrange("(k p) t -> p k t", p=128), B_uT, u_src[1])
                            c0 += n
                    for fc in range(KF):
                        wt, B_wt = nw()
                        if (ip == 0 or not CACHE_W) and not pre:
                            P.dma('pool', wt[:, 0:KD * 128], wup_t[l][fc], B_wt, B_w, max_dma_last_dim=4096)
                            P.dma('pool', wt[:, KD * 128:2 * KD * 128], wup_t[l][KF + fc], B_wt, B_w, max_dma_last_dim=4096)
                            if CACHE_W and len(passes) > 1:
                                P.dma('sp', wupc[l][fc], wt[:, 0:2 * KD * 128], B_wupc, B_wt)
                        else:
                            P.dma('pool', wt[:, 0:2 * KD * 128], wupc[l][fc], B_wt, B_wupc)
                        for si, (c0, n) in enumerate(sls):
                            pg, B_pg = nps()
                            pu, B_pu = nps()
                            for kc in range(KD):
                                P.op('pe', 'matmul', KW(pg[:, 0:n], lhsT=wt[:, kc * 128:(kc + 1) * 128], rhs=uT[:, kc, c0:c0 + n],
                                                        start=(kc == 0), stop=(kc == KD - 1)), reads=[B_wt, B_uT], writes=[B_pg], inc=(kc == KD - 1))
                            for kc in range(KD):
                                P.op('pe', 'matmul', KW(pu[:, 0:n], lhsT=wt[:, (KD + kc) * 128:(KD + kc + 1) * 128], rhs=uT[:, kc, c0:c0 + n],
                                                        start=(kc == 0), stop=(kc == KD - 1)), reads=[B_wt, B_uT], writes=[B_pu], inc=(kc == KD - 1))
                            t_, B_t = tmp[si % 2]
                            P.op('act', 'activation', KW(out=t_[:, 0:n], in_=pg[:, 0:n], func=AF.Silu), reads=[B_pg], writes=[B_t])
                            P.op('dve', 'tensor_tensor', KW(out=gT[:, fc * TP + c0: fc * TP + c0 + n], in0=t_[:, 0:n], in1=pu[:, 0:n], op=ALU.mult),
                                 reads=[B_t, B_pu], writes=[B_gT])
                    for oc in range(KD):
                        wt, B_wt = nw()
                        if (ip == 0 or not CACHE_W) and not pre:
                            P.dma('pool', wt[:, 0:KF * 128], wdn_t[l][oc], B_wt, B_w, max_dma_last_dim=4096)
                            if CACHE_W and len(passes) > 1:
                                P.dma('sp', wdnc[l][oc], wt[:, 0:KF * 128], B_wdnc, B_wt)
                        else:
                            P.dma('pool', wt[:, 0:KF * 128], wdnc[l][oc], B_wt, B_wdnc)
                        for (c0, n) in sls:
                            pt, B_pt = nps()
                            for kc in range(KF):
                                P.op('pe', 'matmul', KW(pt[:, 0:n], lhsT=wt[:, kc * 128:(kc + 1) * 128], rhs=gT[:, kc * TP + c0: kc * TP + c0 + n],
                                                        start=(kc == 0), stop=(kc == KF - 1)), reads=[B_wt, B_gT], writes=[B_pt], inc=(kc == KF - 1))
                            P.op('act', 'activation', KW(out=uT[:, oc, c0:c0 + n], in_=pt[:, 0:n], func=AF.Copy), reads=[B_pt], writes=[B_uT])
                    rms_rstd(nps, sq, rstd, B_rstd, lambda kc: uT[:, kc, 0:T], [B_uT], KD, D, T)
                    co, B_co = cf["co_%d" % kpre]
                    for kc in range(KD):
                        hc_, B_hc = hch[kc % 3]
                        c0 = 0
                        for (s0, n, which) in segs:
                            P.dma('sp', hc_[:, c0:c0 + n], h_src[kc * 128:(kc + 1) * 128, s0:s0 + n], B_hc, B_hsrc)
                            c0 += n
                        t_, B_t = tmp[kc % 2]
                        P.op('dve', 'tensor_tensor', KW(out=t_[:, 0:T], in0=uT[:, kc, 0:T], in1=rstd[:, 0:T], op=ALU.mult), reads=[B_uT, B_rstd], writes=[B_t])
                        c0 = 0
                        for (s0, n, which) in segs:
                            P.op('dve', 'scalar_tensor_tensor', KW(out=hT32[:, kc, c0:c0 + n], in0=t_[:, c0:c0 + n], scalar=co[:, kc, which:which + 1],
                                                                   in1=hc_[:, c0:c0 + n], op0=ALU.mult, op1=ALU.add),
                                 reads=[B_t, B_co, B_hc], writes=[B_gT])
                            c0 += n
                    c0 = 0
                    for (s0, n, which) in segs:
                        P.dma('sp', h_dst[:, s0:s0 + n].rearrange("(k p) t -> p k t", p=128), hT32[:, 0:KD, c0:c0 + n], B_hdst, B_gT)
                        c0 += n
                    if knext is not None:
                        rms_rstd(nps, sq, rstd, B_rstd, lambda kc: hT32[:, kc, 0:T], [B_gT], KD, D, T)
                        normmod(knext, segs, T)
                        c0 = 0
                        for (s0, n, which) in segs:
                            P.dma('sp', u_dst[:, s0:s0 + n].rearrange("(k p) t -> p k t", p=128), uT[:, :, c0:c0 + n], B_udst, B_uT)
                            c0 += n
                P.barrier()

        if upto >= 1:
            ffn_phase("f1", 0, xT, B_xT, make_passes(C, SEQ, CTX, TP), None, h1T, B_h1T, u1T, B_u1T, 0, 1)
        if 'h1' in dbg:
            o = dbgout("h1T", [D, NT])
            P.dma('sp', o[:, :], h1T[:, :], Buf("dbgh1"), B_h1T)
            hb, B_hb = sbuf(glob, "dbg_u1", [128, NT], BF16)
            hf, B_hf = sbuf(glob, "dbg_u1f", [128, NT])
            o = dbgout("u1T", [D, NT])
            B_o = Buf("dbgu1")
            for kc in range(KD):
                P.dma('sp', hb[:], u1T[kc * 128:(kc + 1) * 128, :], B_hb, B_u1T)
                P.op('dve', 'tensor_copy', KW(out=hf[:], in_=hb[:]), reads=[B_hb], writes=[B_hf])
                P.dma('sp', o[kc * 128:(kc + 1) * 128, :], hf[:], B_o, B_hf)

        if upto >= 2:
            with contextlib.ExitStack() as ph:
                uA, B_uA = sbuf(ph, "p2u", [128, KD, NT], BF16)
                wsl = [sbuf(ph, "p2w%d" % i, [128, KD * 128], BF16) for i in range(2)]
                raw, B_raw = sbuf(ph, "p2raw", [128, NT])
                tm2, B_tm2 = sbuf(ph, "p2tmp", [128, NT])
                ob = [sbuf(ph, "p2ob%d" % i, [128, NT], BF16) for i in range(1)]
                xblk = [sbuf(ph, "p2xb%d" % i, [128, 8 * 128], BF16) for i in range(2)]
                pss = [psum(ph, "p2ps%d" % i, [128, 512]) for i in range(7)]
                psT, B_psT = psum(ph, "p2psT", [128, 1024], BF16)
                st = {'ps': 0, 'w': 0, 'ob': 0, 'xb': 0}
                B_dst = {}
                for kc in range(KD):
                    P.dma('sp', uA[:, kc, :], u1T[kc * 128:(kc + 1) * 128, :], B_uA, B_u1T)
                lat_sl = slices_of(SEQ, 512)
                all_sl = lat_sl + [(SEQ + c0, n) for (c0, n) in slices_of(CTX, 512)]
                segs_lc = [(0, SEQ), (SEQ, NT)]

                def conv(src, dst, wcol, bcol):
                    P.op('dve', 'tensor_scalar', KW(out=dst[:, 0:NT], in0=src[:, 0:NT], scalar1=V(wcol[0], wcol[1] + 2), scalar2=V(bcol[0], bcol[1]),
                                                    op0=ALU.mult, op1=ALU.add), reads=[src_b[0], B_vec], writes=[dst_b[0]])
                    for k in (0, 1, 3):
                        o_ = k - 2
                        for (a, e) in segs_lc:
                            d0, d1 = a + max(0, -o_), e - max(0, o_)
                            P.op('dve', 'scalar_tensor_tensor', KW(out=dst[:, d0:d1], in0=src[:, d0 + o_:d1 + o_], scalar=V(wcol[0], wcol[1] + k),
                                                                   in1=dst[:, d0:d1], op0=ALU.mult, op1=ALU.add),
                                 reads=[src_b[0], B_vec], writes=[dst_b[0]])
                src_b, dst_b = [None], [None]

                _skip = _os.environ.get('P2SKIP', '').split(',')
                pend = []
                for oc in range(C.NOC):
                    _ty = ('z' if oc < C.OC_X else 'x' if oc < C.OC_DT else 'dt' if oc < C.OC_RG else 'rg' if oc < C.OC_RX else 'rx' if oc < C.OC_G else 'g')
                    if _ty in _skip:
                        continue
                    lat_only = (oc < C.OC_X) or (C.OC_RG <= oc < C.OC_RX) or (oc >= C.OC_G)
                    sls = lat_sl if lat_only else all_sl
                    wt, B_wt = wsl[st['w'] % 2]
                    st['w'] += 1
                    P.dma('pool', wt[:], win_t[oc], B_wt, B_w, max_dma_last_dim=4096)
                    pts = []
                    for (c0, n) in sls:
                        pt, B_pt = pss[st['ps'] % 7]
                        st['ps'] += 1
                        for kc in range(KD):
                            P.op('pe', 'matmul', KW(pt[:, 0:n], lhsT=wt[:, kc * 128:(kc + 1) * 128], rhs=uA[:, kc, c0:c0 + n],
                                                    start=(kc == 0), stop=(kc == KD - 1)), reads=[B_wt, B_uA], writes=[B_pt], inc=(kc == KD - 1))
                        pts.append((pt, B_pt, c0, n))
                    while pend:
                        pend.pop(0)()
                    o_t, B_ot = ob[0]
                    if oc < C.OC_X or oc >= C.OC_G:
                        st['ob'] += 1
                        fn = AF.Silu if oc < C.OC_X else AF.Sigmoid
                        for (pt, B_pt, c0, n) in pts:
                            P.op('act', 'activation', KW(out=o_t[:, c0:c0 + n], in_=pt[:, 0:n], func=fn), reads=[B_pt], writes=[B_ot])
                        if oc < C.OC_X:
                            P.dma('sp', pz[oc * 128:(oc + 1) * 128, :], o_t[:, 0:SEQ], B_pz, B_ot)
                        else:
                            j = oc - C.OC_G
                            P.dma('sp', pgt[j * 128:(j + 1) * 128, :], o_t[:, 0:SEQ], B_pgt, B_ot)
                    elif oc < C.OC_DT:
                        st['ob'] += 1
                        j = oc - C.OC_X
                        for (pt, B_pt, c0, n) in pts:
                            P.op('act', 'activation', KW(out=raw[:, c0:c0 + n], in_=pt[:, 0:n], func=AF.Copy), reads=[B_pt], writes=[B_raw])
                        src_b[0], dst_b[0] = B_raw, B_tm2
                        conv(raw, tm2, ("cw", 4 * j), ("cb", j))
                        P.op('act', 'activation', KW(out=o_t[:, 0:NT], in_=tm2[:, 0:NT], func=AF.Silu), reads=[B_tm2], writes=[B_ot])
                        def tjob(j=j, o_t=o_t, B_ot=B_ot):
                            nchunk = NT // 128
                            for c8 in range(0, nchunk, 8):
                                m = min(8, nchunk - c8)
                                for ci in range(m):
                                    P.op('pe', 'transpose', KW(psT[:, ci * 128:(ci + 1) * 128], o_t[:, (c8 + ci) * 128:(c8 + ci + 1) * 128], identb[:]),
                                         reads=[B_ot, B_idb], writes=[B_psT], inc=(ci == m - 1))
                                xb_, B_xb = xblk[st['xb'] % 2]
                                st['xb'] += 1
                                P.op('dve', 'tensor_copy', KW(out=xb_[:, 0:m * 128], in_=psT[:, 0:m * 128]), reads=[B_psT], writes=[B_xb])
                                if j < XC:
                                    dstap = x_tm[c8 * 128:(c8 + m) * 128, j * 128:(j + 1) * 128].rearrange("(c p) j -> p c j", p=128)
                                    P.dma('sp', dstap, xb_[:, 0:m * 128].rearrange("p (c j) -> p c j", j=128), B_xtm, B_xb)
                                else:
                                    jj = j - XC
                                    dstap = B_tm[c8 * 128:(c8 + m) * 128, jj * 128:(jj + 1) * 128].rearrange("(c p) j -> p c j", p=128)
                                    P.dma('sp', dstap, xb_[:, 0:m * 128].rearrange("p (c j) -> p c j", j=128), B_Btm, B_xb)
                        if (j < XC or (XC <= j < XC + 8)) and 'xt' not in _skip:
                            pend.append(tjob)
                        if XC <= j < XC + 8:
                            jj = j - XC
                            P.dma('sp', pB[jj * 128:(jj + 1) * 128, :], o_t[:, 0:NT], B_pB, B_ot)
                        elif j >= XC + 8:
                            jj = j - XC - 8
                            P.dma('sp', pC[jj * 128:(jj + 1) * 128, :], o_t[:, 0:NT], B_pC, B_ot)
                    elif oc < C.OC_RG:
                        j = oc - C.OC_DT
                        for (pt, B_pt, c0, n) in pts:
                            P.op('act', 'activation', KW(out=raw[:, c0:c0 + n], in_=pt[:, 0:n], func=AF.Exp, bias=V("dtb", j), scale=1.0),
                                 reads=[B_pt, B_vec], writes=[B_raw])
                        P.op('act', 'activation', KW(out=tm2[:, 0:NT], in_=raw[:, 0:NT], func=AF.Ln, bias=1.0, scale=1.0), reads=[B_raw], writes=[B_tm2])
                        P.dma('sp', pdt[j * 128:(j + 1) * 128, :], tm2[:, 0:NT], B_pdt, B_tm2)
                    elif oc < C.OC_RX:
                        st['ob'] += 1
                        j = oc - C.OC_RG
                        for (pt, B_pt, c0, n) in pts:
                            P.op('act', 'activation', KW(out=raw[:, c0:c0 + n], in_=pt[:, 0:n], func=AF.Copy), reads=[B_pt], writes=[B_raw])
                        P.op('dve', 'tensor_tensor', KW(out=tm2[:, 0:SEQ], in0=raw[:, 0:SEQ], in1=raw[:, 0:SEQ], op=ALU.mult), reads=[B_raw], writes=[B_tm2])
                        P.op('dve', 'tensor_scalar', KW(out=tm2[:, 0:SEQ], in0=tm2[:, 0:SEQ], scalar1=0.044715, scalar2=1.0, op0=ALU.mult, op1=ALU.add),
                             reads=[B_tm2], writes=[B_tm2])
                        P.op('dve', 'tensor_tensor', KW(out=tm2[:, 0:SEQ], in0=tm2[:, 0:SEQ], in1=raw[:, 0:SEQ], op=ALU.mult), reads=[B_raw, B_tm2], writes=[B_tm2])
                        P.op('act', 'activation', KW(out=tm2[:, 0:SEQ], in_=tm2[:, 0:SEQ], func=AF.Sigmoid, scale=1.5957691216057308), reads=[B_tm2], writes=[B_tm2])
                        P.op('dve', 'tensor_tensor', KW(out=o_t[:, 0:SEQ], in0=tm2[:, 0:SEQ], in1=raw[:, 0:SEQ], op=ALU.mult), reads=[B_raw, B_tm2], writes=[B_ot])
                        P.dma('sp', prg[j * 128:(j + 1) * 128, :], o_t[:, 0:SEQ], B_prg, B_ot)
                    else:
                        j = oc - C.OC_RX
                        for (pt, B_pt, c0, n) in pts:
                            if c0 < SEQ:
                                r0 = c0 // C.GW
                                nr = n // C.GW
                                P.op('act', 'activation', KW(out=tm2[:, 0:SEQ].rearrange("p (c r) -> p r c", r=C.ROWS)[:, r0:r0 + nr, :],
                                                             in_=pt[:, 0:n].rearrange("p (r c) -> p r c", c=C.GW), func=AF.Copy),
                                     reads=[B_pt], writes=[B_tm2])
                            else:
                                P.op('act', 'activation', KW(out=tm2[:, c0:c0 + n], in_=pt[:, 0:n], func=AF.Copy), reads=[B_pt], writes=[B_tm2])
                        src_b[0], dst_b[0] = B_tm2, B_raw
                        conv(tm2, raw, ("rcw", 4 * j), ("rcb", j))
                        P.dma('sp', prx[j * 128:(j + 1) * 128, :], raw[:, 0:NT], B_prx, B_raw)
                while pend:
                    pend.pop(0)()
                P.barrier()
        if 'p2' in dbg:
            tb, B_tb = sbuf(glob, "dbg_p2b", [128, NT], BF16)
            tf, B_tf = sbuf(glob, "dbg_p2f", [128, NT])
            for (nm, src, B_src, rows, cols, isbf) in (("pz", pz, B_pz, C.DI, SEQ, 1), ("pB", pB, B_pB, C.GN, NT, 1), ("pC", pC, B_pC, C.GN, NT, 1),
                                                      ("pdt", pdt, B_pdt, 256, NT, 0), ("prg", prg, B_prg, RC * 128, SEQ, 1),
                                                      ("prx", prx, B_prx, RC * 128, NT, 0), ("pgt", pgt, B_pgt, 2 * D, SEQ, 1)):
                o = dbgout(nm, [rows, cols])
                B_o = Buf("dbg" + nm)
                for kc in range(rows // 128):
                    if isbf:
                        P.dma('sp', tb[:, 0:cols], src[kc * 128:(kc + 1) * 128, :], B_tb, B_src)
                        P.op('dve', 'tensor_copy', KW(out=tf[:, 0:cols], in_=tb[:, 0:cols]), reads=[B_tb], writes=[B_tf])
                    else:
                        P.dma('sp', tf[:, 0:cols], src[kc * 128:(kc + 1) * 128, :], B_tf, B_src)
                    P.dma('sp', o[kc * 128:(kc + 1) * 128, :], tf[:, 0:cols], B_o, B_tf)
            xb2, B_xb2 = sbuf(glob, "dbg_xtb", [128, C.DI], BF16)
            xf2, B_xf2 = sbuf(glob, "dbg_xtf", [128, C.DI])
            o = dbgout("x_tm", [NT, C.DI])
            B_o = Buf("dbgxtm")
            for c in range(NT // 128):
                P.dma('sp', xb2[:], x_tm[c * 128:(c + 1) * 128, :], B_xb2, B_xtm)
                P.op('dve', 'tensor_copy', KW(out=xf2[:], in_=xb2[:]), reads=[B_xb2], writes=[B_xf2])
                P.dma('sp', o[c * 128:(c + 1) * 128, :], xf2[:], B_o, B_xf2)


        NCH, NCL, HB, NQ = C.NCH, C.NCL, C.HB, C.NQ
        EW = E * 64
        if upto >= 4:
            with contextlib.ExitStack() as ph:
                def tm(name, dt=F32):
                    return sbuf(ph, name, [128, NCH, 128], dt)
                lb_tm = [tm("lb_tm_f"), tm("lb_tm_b")]
                w_tm = [tm("w_tm_f", BF16), tm("w_tm_b", BF16)]
                etot = [tm("etot_f"), tm("etot_b")]
                csS = [[sbuf(ph, "csS_%d_%d" % (d_, i_), [128, NT], BF16) for i_ in range(2)] for d_ in range(2)]
                maskq, B_maskq = sbuf(ph, "maskq", [128, H // HB, HB])
                drw, B_drw = sbuf(ph, "drw", [128, 128])
                pA, B_pA = psum(ph, "p4A", [128, 1024])
                pB_, B_pB_ = psum(ph, "p4B", [128, 1024])
                pY, B_pY = psum(ph, "p4Y", [128, 512])
                pS, B_pS = psum(ph, "p4S", [128, 512])
                pM, B_pM = psum(ph, "p4M", [128, 512])
                maskE, B_maskE = sbuf(ph, "maskE", [128, 8, E])
                P.op('pool', 'memset', KW(maskE[:], 1.0), writes=[B_maskE])
                P.op('pool', 'affine_select', KW(out=maskE[:], in_=maskE[:], pattern=[[-E, 8], [-1, E]], base=0, channel_multiplier=1,
                                                 compare_op=ALU.is_equal, fill=0.0), reads=[B_maskE], writes=[B_maskE])
                P.op('pool', 'memset', KW(maskq[:], 1.0), writes=[B_maskq])
                P.op('pool', 'affine_select', KW(out=maskq[:], in_=maskq[:], pattern=[[-HB, H // HB], [-1, HB]], base=0, channel_multiplier=1,
                                                 compare_op=ALU.is_equal, fill=0.0), reads=[B_maskq], writes=[B_maskq])
                P.dma('sp', drw[:], drow[0:1, :].partition_broadcast(128), B_drw, B_w)
                amask = [sbuf(ph, "amask_f", [128, 128]), sbuf(ph, "amask_b", [128, 128])]
                P.op('pool', 'memset', KW(amask[0][0][:], 0.0), writes=[amask[0][1]])
                P.op('pool', 'affine_select', KW(out=amask[0][0][:], in_=amask[0][0][:], pattern=[[1, 128]], base=0, channel_multiplier=-1,
                                                 compare_op=ALU.is_ge, fill=NEG), reads=[amask[0][1]], writes=[amask[0][1]])
                P.op('pool', 'memset', KW(amask[1][0][:], NEG), writes=[amask[1][1]])
                P.op('pool', 'affine_select', KW(out=amask[1][0][:], in_=amask[1][0][:], pattern=[[1, 128]], base=0, channel_multiplier=-1,
                                                 compare_op=ALU.is_gt, fill=0.0), reads=[amask[1][1]], writes=[amask[1][1]])
                amask8 = [sbuf(ph, "amask8_%d" % d_, [128, HB, 128], BF16) for d_ in range(2)]
                for d_ in range(2):
                    P.op('dve', 'tensor_copy', KW(out=amask8[d_][0][:], in_=amask[d_][0][:].unsqueeze(1).to_broadcast([128, HB, 128])),
                         reads=[amask[d_][1]], writes=[amask8[d_][1]])
                with contextlib.ExitStack() as pp:
                    cs_tm = [sbuf(pp, "cs_tm_f", [128, NCH, 128]), sbuf(pp, "cs_tm_b", [128, NCH, 128])]
                    dt_tm = [sbuf(pp, "dt_tm_f", [128, NCH, 128], BF16), sbuf(pp, "dt_tm_b", [128, NCH, 128], BF16)]
                    dtT = [sbuf(pp, "dtT_f", [128, NT]), sbuf(pp, "dtT_b", [128, NT])]
                    daT = [sbuf(pp, "daT_f", [128, NT]), sbuf(pp, "daT_b", [128, NT])]
                    da_tm = [sbuf(pp, "da_tm_f", [128, NCH, 128]), sbuf(pp, "da_tm_b", [128, NCH, 128])]
                    aneg, B_aneg = sbuf(pp, "aneg", [128, 2])
                    csT = [sbuf(pp, "csT_f", [128, NT]), sbuf(pp, "csT_b", [128, NT])]
                    spl, B_spl = sbuf(pp, "spl", [128, NT])
                    P.op('act', 'activation', KW(out=aneg[:], in_=V("alog", 0, 2), func=AF.Exp), reads=[B_vec], writes=[B_aneg])
                    P.op('dve', 'tensor_scalar_mul', KW(out=aneg[:], in0=aneg[:], scalar1=-1.0), reads=[B_aneg], writes=[B_aneg])
                    for d in range(2):
                        t_, B_t = dtT[d]
                        a_, B_a = daT[d]
                        c_, B_c = csT[d]
                        P.dma('sp', t_[:], pdt[d * 128:(d + 1) * 128, :], B_t, B_pdt)
                        P.op('dve', 'tensor_scalar_mul', KW(out=a_[:], in0=t_[:], scalar1=aneg[:, d:d + 1]), reads=[B_t, B_aneg], writes=[B_a])
                        for c in range(NCH):
                            sl = slice(c * 128, (c + 1) * 128)
                            if d == 0:
                                P.op('dve', 'tensor_tensor_scan', KW(out=c_[:, sl], data0=ones32[:], data1=a_[:, sl], initial=0.0, op0=ALU.mult, op1=ALU.add),
                                     reads=[B_a, B_ones32], writes=[B_c])
                            else:
                                P.op('dve', 'tensor_tensor_scan', KW(out=c_[:, sl][:, ::-1], data0=ones32[:], data1=a_[:, sl][:, ::-1], initial=0.0,
                                                                     op0=ALU.mult, op1=ALU.add), reads=[B_a, B_ones32], writes=[B_c])
                        (h0_, B_h0), (h1_, B_h1) = csS[d]
                        P.op('act', 'activation', KW(out=h0_[:], in_=c_[:], func=AF.Copy), reads=[B_c], writes=[B_h0])
                        P.op('dve', 'tensor_tensor', KW(out=spl[:], in0=c_[:], in1=h0_[:], op=ALU.subtract), reads=[B_c, B_h0], writes=[B_spl])
                        P.op('act', 'activation', KW(out=h1_[:], in_=spl[:], func=AF.Copy), reads=[B_spl], writes=[B_h1])
                        for c in range(NCH):
                            sl = slice(c * 128, (c + 1) * 128)
                            for (src, B_src, (dst, B_dst)) in ((c_, B_c, cs_tm[d]), (t_, B_t, dt_tm[d]), (a_, B_a, da_tm[d])):
                                P.op('pe', 'transpose', KW(pM[:, 0:128], src[:, sl], ident32[:]), reads=[B_src, B_id32], writes=[B_pM])
                                P.op('act', 'activation', KW(out=dst[:, c, :], in_=pM[:, 0:128], func=AF.Copy), reads=[B_pM], writes=[B_dst])
                            P.op('pe', 'matmul', KW(pM[:, 128:256], lhsT=ones32[:], rhs=da_tm[d][0][:, c, :], start=True, stop=True),
                                 reads=[da_tm[d][1], B_ones32], writes=[B_pM])
                            e_, B_e = etot[d]
                            w_, B_w_ = w_tm[d]
                            P.op('act', 'activation', KW(out=e_[:, c, :], in_=pM[:, 128:256], func=AF.Exp), reads=[B_pM], writes=[B_e])
                            tw = spl[:, 0:128]
                            P.op('dve', 'tensor_tensor', KW(out=tw, in0=pM[:, 128:256], in1=cs_tm[d][0][:, c, :], op=ALU.subtract),
                                 reads=[B_pM, cs_tm[d][1]], writes=[B_spl])
                            P.op('act', 'activation', KW(out=tw, in_=tw, func=AF.Exp), reads=[B_spl], writes=[B_spl])
                            P.op('dve', 'tensor_tensor', KW(out=w_[:, c, :], in0=tw, in1=dt_tm[d][0][:, c, :], op=ALU.mult),
                                 reads=[B_spl, dt_tm[d][1]], writes=[B_w_])
                            lb_, B_lb = lb_tm[d]
                            P.op('act', 'activation', KW(out=lb_[:, c, :], in_=dt_tm[d][0][:, c, :], func=AF.Ln), reads=[dt_tm[d][1]], writes=[B_lb])
                            P.op('dve', 'tensor_tensor', KW(out=lb_[:, c, :], in0=lb_[:, c, :], in1=cs_tm[d][0][:, c, :], op=ALU.subtract),
                                 reads=[B_lb, cs_tm[d][1]], writes=[B_lb])
                    P.barrier()

                BTg, B_BTg = sbuf(ph, "BTg", [128, NT], BF16)
                CTg, B_CTg = sbuf(ph, "CTg", [128, NT], BF16)
                xtm = [sbuf(ph, "xtm%d" % i, [128, EW], BF16) for i in range(2)]
                btm = [sbuf(ph, "btm%d" % i, [128, 128], BF16) for i in range(2)]
                szc = [sbuf(ph, "szc%d" % i, [128, E // 2, 128], BF16) for i in range(2)]
                sbi = [sbuf(ph, "sbi%d" % i, [128, EW], BF16) for i in range(2)]
                xs, B_xs = sbuf(ph, "xs", [128, EW], BF16)
                S32 = [sbuf(ph, "S32f", [128, EW]), sbuf(ph, "S32b", [128, EW])]
                Sfb, B_Sfb = sbuf(ph, "Sfb", [128, EW], BF16)
                Sbb = [sbuf(ph, "Sbb%d" % i, [128, EW], BF16) for i in range(2)]
                DIg, B_DIg = sbuf(ph, "DIg", [128, E, 128], BF16)
                selg, B_selg = sbuf(ph, "selg", [128, E, 128], BF16)
                CBt, B_CBt = sbuf(ph, "CBt", [128, 128])
                Dd = [[sbuf(ph, "Dd%d_%d" % (p_, i), [128, HB, 128]) for i in range(2)] for p_ in range(2)]
                Xd = [[sbuf(ph, "Xd%d_%d" % (p_, i), [128, HB, 128], BF16) for i in range(2)] for p_ in range(2)]
                Mt = [sbuf(ph, "Mt%d" % p_, [128, HB, 128], BF16) for p_ in range(2)]
                Csd = [[sbuf(ph, "Cs%d_%d" % (p_, i), [128, HB, 128], BF16) for i in range(2)] for p_ in range(2)]
                yg, B_yg = sbuf(ph, "yg", [128, E // 2, 128])
                ysq, B_ysq = sbuf(ph, "ysq", [128, E // 2, 128], BF16)
                ynb, B_ynb = sbuf(ph, "ynb", [128, E // 2, 128], BF16)
                rs4, B_rs4 = sbuf(ph, "rs4", [128, 128])
                cnt4 = {'ld': 0, 'sb': 0}

                def bc_l(ap2):
                    return ap2.unsqueeze(2).to_broadcast([128, ap2.shape[1], 128])

                def bc_h(ap2, n):
                    return ap2.unsqueeze(1).to_broadcast([128, n, 128])

                def load_chunk(g, c):
                    i = cnt4['ld'] % 2
                    cnt4['ld'] += 1
                    x_, B_x = xtm[i]
                    b_, B_b = btm[i]
                    P.dma('sp', x_[:], x_tm[c * 128:(c + 1) * 128, g * EW:(g + 1) * EW], B_x, B_xtm)
                    P.dma('sp', b_[:], B_tm[c * 128:(c + 1) * 128, g * 128:(g + 1) * 128], B_b, B_Btm)
                    return x_, B_x, b_, B_b

                def state_update(g, d, c, x_, B_x, b_, B_b):
                    S, B_S = S32[d]
                    hs = slice(g * E, (g + 1) * E)
                    P.op('dve', 'tensor_tensor', KW(out=xs[:].rearrange("p (h q) -> p h q", q=64), in0=x_[:].rearrange("p (h q) -> p h q", q=64),
                                                    in1=w_tm[d][0][:, c, hs].unsqueeze(2).to_broadcast([128, E, 64]), op=ALU.mult),
                         reads=[B_x, w_tm[d][1]], writes=[B_xs])
                    for c0 in range(0, EW, 512):
                        n = min(512, EW - c0)
                        P.op('pe', 'matmul', KW(pS[:, 0:n], lhsT=b_[:], rhs=xs[:, c0:c0 + n], start=True, stop=True), reads=[B_b, B_xs], writes=[B_pS])
                        nh = n // 64
                        h0 = g * E + c0 // 64
                        P.op('dve', 'tensor_tensor', KW(out=S[:, c0:c0 + n].rearrange("p (h q) -> p h q", q=64),
                                                        in0=S[:, c0:c0 + n].rearrange("p (h q) -> p h q", q=64),
                                                        in1=etot[d][0][:, c, h0:h0 + nh].unsqueeze(2).to_broadcast([128, nh, 64]), op=ALU.mult),
                             reads=[B_S, etot[d][1]], writes=[B_S])
                        P.op('dve', 'tensor_tensor', KW(out=S[:, c0:c0 + n], in0=S[:, c0:c0 + n], in1=pS[:, 0:n], op=ALU.add), reads=[B_S, B_pS], writes=[B_S])

                if PRECAST:
                    for fc in range(KF):
                        P.dma('pool', wupc[1][fc][:, 0:KD * 128], wup_t[1][fc], B_cache["wupc1"], B_w, max_dma_last_dim=4096)
                        P.dma('pool', wupc[1][fc][:, KD * 128:2 * KD * 128], wup_t[1][KF + fc], B_cache["wupc1"], B_w, max_dma_last_dim=4096)
                    for oc in range(KD):
                        P.dma('pool', wdnc[1][oc], wdn_t[1][oc], B_cache["wdnc1"], B_w, max_dma_last_dim=4096)
                        P.dma('pool', wsoc[oc], wso_t[oc], B_cache["wsoc"], B_w, max_dma_last_dim=4096)
                        P.dma('pool', wroc[oc], wro_t[oc], B_cache["wroc"], B_w, max_dma_last_dim=4096)
                        P.dma('pool', woc[oc], wo_t[oc], B_cache["woc"], B_w, max_dma_last_dim=4096)
                _p4stop = _os.environ.get('P4STOP', '')
                _pe4 = _os.environ.get('P4POOL', 'pool')
                _p4d = _os.environ.get('P4D', '').split(',')
                _L = int(_os.environ.get('P4L', '99'))
                for g in range(8 if _p4stop != 'prep' else 0):
                    P.dma('sp', BTg[:], pB[g * 128:(g + 1) * 128, :], B_BTg, B_pB)
                    P.dma('sp', CTg[:], pC[g * 128:(g + 1) * 128, :], B_CTg, B_pC)
                    P.op('dve', 'tensor_tensor', KW(out=DIg[:], in0=bc_h(ident32[:], E), in1=bc_l(drw[:, g * E:(g + 1) * E]), op=ALU.mult),
                         reads=[B_id32, B_drw], writes=[B_DIg])
                    P.op('dve', 'tensor_copy', KW(out=selg[:], in_=bc_l(maskE[:, g, :])), reads=[B_maskE], writes=[B_selg])
                    for d in range(2):
                        P.op('dve', 'memset', KW(S32[d][0][:], 0.0), writes=[S32[d][1]])
                    for c in range(NCL, NCH):
                        state_update(g, 0, c, *load_chunk(g, c))
                    for c in range(NCH - 1, NCL - 1, -1):
                        state_update(g, 1, c, *load_chunk(g, c))
                    for c in range(NCL - 1, -1, -1) if _p4stop != 'ctx' else []:
                        sb_, B_sb = Sbb[cnt4['sb'] % 2]
                        cnt4['sb'] += 1
                        P.op('act', 'activation', KW(out=sb_[:], in_=S32[1][0][:], func=AF.Copy), reads=[S32[1][1]], writes=[B_sb])
                        P.dma('sp', sbin[g, c], sb_[:], B_sbin, B_sb)
                        state_update(g, 1, c, *load_chunk(g, c))
                    def stage_A(it):
                        c, q = divmod(it, NQ)
                        par = it % 2
                        sl = slice(c * 128, (c + 1) * 128)
                        Q = g * NQ + q
                        hs = slice(Q * HB, (Q + 1) * HB)
                        pps = ((pA, B_pA), (pB_, B_pB_))
                        for d in range(2):
                            pp_, B_pp = pps[d]
                            for hh in range(HB):
                                for i3 in range(2):
                                    P.op('pe', 'matmul', KW(pp_[:, hh * 128:(hh + 1) * 128], lhsT=selg[:, q * HB + hh, :], rhs=csS[d][i3][0][:, sl],
                                                            start=(i3 == 0 and hh % 4 == 0), stop=(i3 == 1), skip_group_check=True),
                                         reads=[B_selg, csS[d][i3][1]], writes=[B_pp], inc=(hh == HB - 1 and i3 == 1))
                        ppv = [pps[d][0][:, 0:HB * 128].rearrange("p (h l) -> p h l", l=128) for d in range(2)]
                        for d in range(2):
                            P.op('act', 'activation', KW(out=Xd[par][d][0][:], in_=ppv[d], func=AF.Exp), reads=[pps[d][1]], writes=[Xd[par][d][1]])
                        for d in range(2):
                            pp_, B_pp = pps[d]
                            af = amask8[d][0][:].rearrange("p h l -> p (h l)")
                            for c0 in range(0, HB * 128, 512):
                                n = min(512, HB * 128 - c0)
                                P.op('pe', 'matmul', KW(pp_[:, c0:c0 + n], lhsT=identb[:], rhs=af[:, c0:c0 + n], start=False, stop=True, skip_group_check=True),
                                     reads=[B_idb, amask8[d][1]], writes=[B_pp])
                        for hh in range(HB):
                            for d in range(2):
                                hcol = Q * HB + hh
                                P.op('act', 'activation', KW(out=Dd[par][d][0][:, hh, :], in_=pps[d][0][:, hh * 128:(hh + 1) * 128], func=AF.Exp,
                                                             bias=lb_tm[d][0][:, c, hcol:hcol + 1], scale=1.0),
                                     reads=[pps[d][1], lb_tm[d][1]], writes=[Dd[par][d][1]])
                        for d in range(2):
                            P.op('pool', 'tensor_tensor', KW(out=Csd[par][d][0][:], in0=Xd[par][d][0][:], in1=bc_h(CTg[:, sl], HB), op=ALU.mult),
                                 reads=[Xd[par][d][1], B_CTg], writes=[Csd[par][d][1]])

                    cur = {}

                    def stage_B(it):
                        c, q = divmod(it, NQ)
                        par = it % 2
                        sl = slice(c * 128, (c + 1) * 128)
                        if q == 0:
                            x_, B_x, b_, B_b = load_chunk(g, c)
                            i = c % 2
                            sz_, B_sz = szc[i]
                            si_, B_si = sbi[i]
                            P.dma('sp', sz_[:], pz[g * (E // 2) * 128:(g + 1) * (E // 2) * 128, sl].rearrange("(j p) t -> p j t", p=128), B_sz, B_pz)
                            P.dma('sp', si_[:], sbin[g, c], B_si, B_sbin)
                            P.op('act', 'activation', KW(out=Sfb[:], in_=S32[0][0][:], func=AF.Copy), reads=[S32[0][1]], writes=[B_Sfb])
                            P.op('pe', 'matmul', KW(pM[:, 0:128], lhsT=BTg[:, sl], rhs=CTg[:, sl], start=True, stop=True), reads=[B_BTg, B_CTg], writes=[B_pM])
                            P.op('act', 'activation', KW(out=CBt[:], in_=pM[:, 0:128], func=AF.Copy), reads=[B_pM], writes=[B_CBt])
                            cur['v'] = (x_, B_x, b_, B_b, sz_, B_sz, si_, B_si)
                        x_, B_x, b_, B_b, sz_, B_sz, si_, B_si = cur['v']
                        D0, B_D0 = Dd[par][0]
                        D1, B_D1 = Dd[par][1]
                        M_, B_M = Mt[par]
                        P.op('dve', 'tensor_tensor', KW(out=D0[:], in0=D0[:], in1=D1[:], op=ALU.add), reads=[B_D0, B_D1], writes=[B_D0])
                        P.op('dve', 'tensor_tensor', KW(out=M_[:], in0=D0[:], in1=bc_h(CBt[:], HB), op=ALU.mult), reads=[B_D0, B_CBt], writes=[B_M])
                        for hh in range(HB):
                            j, e = hh // 2, hh % 2
                            col = (q * HB + hh) * 64
                            o_ap = pY[64 * e:64 * e + 64, j * 128:(j + 1) * 128]
                            P.op('pe', 'matmul', KW(o_ap, lhsT=x_[:, col:col + 64], rhs=M_[:, hh, :], start=True, stop=False),
                                 reads=[B_x, B_M], writes=[B_pY], inc=False)
                            P.op('pe', 'matmul', KW(o_ap, lhsT=x_[:, col:col + 64], rhs=DIg[:, q * HB + hh, :], start=False, stop=False),
                                 reads=[B_x, B_DIg], writes=[B_pY], inc=False)
                            P.op('pe', 'matmul', KW(o_ap, lhsT=Sfb[:, col:col + 64], rhs=Csd[par][0][0][:, hh, :], start=False, stop=False),
                                 reads=[B_Sfb, Csd[par][0][1]], writes=[B_pY], inc=False)
                            P.op('pe', 'matmul', KW(o_ap, lhsT=si_[:, col:col + 64], rhs=Csd[par][1][0][:, hh, :], start=False, stop=True),
                                 reads=[B_si, Csd[par][1][1]], writes=[B_pY], inc=(hh == HB - 1))
                        j0 = q * (HB // 2)
                        P.op('dve', 'tensor_tensor', KW(out=yg[:, j0:j0 + HB // 2, :], in0=pY[:, 0:(HB // 2) * 128].rearrange("p (j l) -> p j l", l=128),
                                                        in1=sz_[:, j0:j0 + HB // 2, :], op=ALU.mult), reads=[B_pY, B_sz], writes=[B_yg])
                        if q == NQ - 1:
                            P.op('act', 'activation', KW(out=ysq[:], in_=yg[:], func=AF.Square), reads=[B_yg], writes=[B_ysq])
                            for j in range(E // 2):
                                P.op('pe', 'matmul', KW(pM[:, 256:384], lhsT=ones_bf[:], rhs=ysq[:, j, :], start=(j == 0), stop=(j == E // 2 - 1)),
                                     reads=[B_ysq, B_ones], writes=[B_pM], inc=(j == E // 2 - 1))
                            P.op('act', 'activation', KW(out=rs4[:], in_=pM[:, 256:384], func=AF.Sqrt, scale=1.0 / EW, bias=EPS), reads=[B_pM], writes=[B_rs4])
                            P.op('dve', 'reciprocal', KW(out=rs4[:], in_=rs4[:]), reads=[B_rs4], writes=[B_rs4])
                            for j in range(E // 2):
                                P.op('dve', 'scalar_tensor_tensor', KW(out=ynb[:, j, :], in0=yg[:, j, :], scalar=V("sng", g * (E // 2) + j), in1=rs4[:],
                                                                       op0=ALU.mult, op1=ALU.mult), reads=[B_yg, B_vec, B_rs4], writes=[B_ynb])
                            P.dma('sp', ynT[g * (E // 2) * 128:(g + 1) * (E // 2) * 128, sl].rearrange("(j p) t -> p j t", p=128), ynb[:], B_ynT, B_ynb)
                            state_update(g, 0, c, x_, B_x, b_, B_b)

                    NIT = NCL * NQ
                    stage_A(0)
                    if 'p4dbg' in dbg and g == 0:
                        dA, B_dA = sbuf(ph, "dbgsb_pA", [128, 1024])
                        P.op('dve', 'tensor_copy', KW(out=dA[:, 0:512], in_=pA[:, 0:512]), reads=[B_pA], writes=[B_dA])
                        P.op('dve', 'tensor_copy', KW(out=dA[:, 512:1024], in_=Dd[0][0][0][:].rearrange("p h l -> p (h l)")[:, 0:512]), reads=[Dd[0][0][1]], writes=[B_dA])
                        o = dbgout("pA", [128, 1024])
                        P.dma('sp', o[:, :], dA[:], Buf("dbgpA"), B_dA)
                    for it in range(NIT):
                        if it + 1 < NIT:
                            stage_A(it + 1)
                        stage_B(it)
                P.barrier()
        if 'yn' in dbg:
            tb, B_tb = sbuf(glob, "dbg_ynb", [128, SEQ], BF16)
            tf, B_tf = sbuf(glob, "dbg_ynf", [128, SEQ])
            o = dbgout("yn", [C.DI, SEQ])
            B_o = Buf("dbgyn")
            for kc in range(XC):
                P.dma('sp', tb[:], ynT[kc * 128:(kc + 1) * 128, :], B_tb, B_ynT)
                P.op('dve', 'tensor_copy', KW(out=tf[:], in_=tb[:]), reads=[B_tb], writes=[B_tf])
                P.dma('sp', o[kc * 128:(kc + 1) * 128, :], tf[:], B_o, B_tf)


        NBC = C.NBC
        if upto >= 5:
            with contextlib.ExitStack() as ph:
                xr, B_xr = sbuf(ph, "xr", [128, NBC, NT])
                xrb, B_xrb = sbuf(ph, "xrb", [128, NBC, NT], BF16)
                rrow, B_rrow = sbuf(ph, "rrow", [128, NT])
                irow, B_irow = sbuf(ph, "irow", [128, NT])
                arow, B_arow = sbuf(ph, "arow", [128, NT])
                brow, B_brow = sbuf(ph, "brow", [128, NT])
                hrow = [sbuf(ph, "hrow%d" % i, [128, NT]) for i in range(2)]
                gate, B_gate = sbuf(ph, "gate", [128, SEQ], BF16)
                orow, B_orow = sbuf(ph, "orow", [128, SEQ], BF16)
                wg = [sbuf(ph, "rgwt%d" % i, [128, NBC * 128], BF16) for i in range(4)]
                cc1, B_cc1 = sbuf(ph, "cc1", [128, 2 * RC])
                cc2, B_cc2 = sbuf(ph, "cc2", [128, 2 * RC])
                nba, B_nba = sbuf(ph, "nba", [128, 2 * RC])
                nbx, B_nbx = sbuf(ph, "nbx", [128, 2 * RC])
                pss = [psum(ph, "p5ps%d" % i, [128, 512]) for i in range(8)]
                st5 = {'ps': 0, 'w': 0}
                P.op('act', 'activation', KW(out=cc1[:], in_=V("rlam", 0, 2 * RC), func=AF.Exp, scale=-1.0), reads=[B_vec], writes=[B_cc1])
                P.op('act', 'activation', KW(out=cc1[:], in_=cc1[:], func=AF.Ln, bias=1.0, scale=1.0), reads=[B_cc1], writes=[B_cc1])
                P.op('dve', 'tensor_scalar_mul', KW(out=cc2[:], in0=cc1[:], scalar1=-16.0), reads=[B_cc1], writes=[B_cc2])
                P.op('dve', 'tensor_scalar_mul', KW(out=cc1[:], in0=cc1[:], scalar1=-8.0), reads=[B_cc1], writes=[B_cc1])
                P.op('dve', 'tensor_scalar_mul', KW(out=nba[:], in0=V("rba", 0, 2 * RC), scalar1=-1.0), reads=[B_vec], writes=[B_nba])
                P.op('dve', 'tensor_scalar_mul', KW(out=nbx[:], in0=V("rbx", 0, 2 * RC), scalar1=-1.0), reads=[B_vec], writes=[B_nbx])
                tsl = slices_of(SEQ, 512) + [(SEQ + c0, n) for (c0, n) in slices_of(CTX, 512)]
                ada_todo = list(range(NADA0, 9 * KD))
                ada_per = -(-len(ada_todo) // (16 * NBC)) if ada_todo else 0
                if ada_todo:
                    wada5 = [sbuf(ph, "wada5_%d" % i, [128, KD * 128], BF16) for i in range(3)]
                for k in range(16):
                    for ic in range(NBC):
                        P.dma('sp', xr[:, ic, :], prx[(k * NBC + ic) * 128:(k * NBC + ic + 1) * 128, :], B_xr, B_prx)
                    P.op('act', 'activation', KW(out=xrb[:].rearrange("p a t -> p (a t)"), in_=xr[:].rearrange("p a t -> p (a t)"), func=AF.Copy),
                         reads=[B_xr], writes=[B_xrb])
                    for jc in range(NBC):
                        ch = k * NBC + jc
                        P.dma('sp', gate[:], prg[ch * 128:(ch + 1) * 128, :], B_gate, B_prg)
                        for d in range(2):
                            col = d * RC + ch
                            wa, B_wa = wg[st5['w'] % 4]
                            wx, B_wx = wg[(st5['w'] + 1) % 4]
                            st5['w'] += 2
                            P.dma('pool', wa[:], rgw_t[((d * 2 + 0) * 16 + k) * NBC + jc], B_wa, B_w, max_dma_last_dim=4096)
                            P.dma('pool', wx[:], rgw_t[((d * 2 + 1) * 16 + k) * NBC + jc], B_wx, B_w, max_dma_last_dim=4096)
                            for (c0, n) in tsl:
                                pa, B_pa = pss[st5['ps'] % 8]
                                px_, B_px = pss[(st5['ps'] + 1) % 8]
                                st5['ps'] += 2
                                for (w_, B_w_, p_, B_p) in ((wa, B_wa, pa, B_pa), (wx, B_wx, px_, B_px)):
                                    for ic in range(NBC):
                                        P.op('pe', 'matmul', KW(p_[:, 0:n], lhsT=w_[:, ic * 128:(ic + 1) * 128], rhs=xrb[:, ic, c0:c0 + n],
                                                                start=(ic == 0), stop=(ic == NBC - 1)), reads=[B_w_, B_xrb], writes=[B_p], inc=(ic == NBC - 1))
                                P.op('act', 'activation', KW(out=rrow[:, c0:c0 + n], in_=pa[:, 0:n], func=AF.Sigmoid, bias=V("rba", col)),
                                     reads=[B_pa, B_vec], writes=[B_rrow])
                                P.op('act', 'activation', KW(out=irow[:, c0:c0 + n], in_=px_[:, 0:n], func=AF.Sigmoid, bias=V("rbx", col)),
                                     reads=[B_px, B_vec], writes=[B_irow])
                            P.op('act', 'activation', KW(out=arow[:], in_=rrow[:], func=AF.Exp, scale=cc1[:, col:col + 1]), reads=[B_rrow, B_cc1], writes=[B_arow])
                            P.op('act', 'activation', KW(out=brow[:], in_=rrow[:], func=AF.Exp, scale=cc2[:, col:col + 1]), reads=[B_rrow, B_cc2], writes=[B_brow])
                            P.op('act', 'activation', KW(out=brow[:], in_=brow[:], func=AF.Sqrt, scale=-1.0, bias=1.0), reads=[B_brow], writes=[B_brow])
                            P.op('dve', 'tensor_tensor', KW(out=brow[:], in0=brow[:], in1=irow[:], op=ALU.mult), reads=[B_brow, B_irow], writes=[B_brow])
                            P.op('dve', 'tensor_tensor', KW(out=brow[:], in0=brow[:], in1=xr[:, jc, :], op=ALU.mult), reads=[B_brow, B_xr], writes=[B_brow])
                            h_, B_h = hrow[d]
                            if d == 0:
                                P.op('dve', 'tensor_tensor_scan', KW(out=h_[:, SEQ:NT], data0=arow[:, SEQ:NT], data1=brow[:, SEQ:NT], initial=0.0,
                                                                     op0=ALU.mult, op1=ALU.add), reads=[B_arow, B_brow], writes=[B_h])
                                P.op('dve', 'tensor_tensor_scan', KW(out=h_[:, 0:SEQ], data0=arow[:, 0:SEQ], data1=brow[:, 0:SEQ], initial=h_[:, NT - 1:NT],
                                                                     op0=ALU.mult, op1=ALU.add), reads=[B_arow, B_brow, B_h], writes=[B_h])
                            else:
                                P.op('dve', 'tensor_tensor_scan', KW(out=h_[:, SEQ:NT][:, ::-1], data0=arow[:, SEQ:NT][:, ::-1], data1=brow[:, SEQ:NT][:, ::-1],
                                                                     initial=0.0, op0=ALU.mult, op1=ALU.add), reads=[B_arow, B_brow], writes=[B_h])
                                P.op('dve', 'tensor_tensor_scan', KW(out=h_[:, 0:SEQ][:, ::-1], data0=arow[:, 0:SEQ][:, ::-1], data1=brow[:, 0:SEQ][:, ::-1],
                                                                     initial=h_[:, SEQ:SEQ + 1], op0=ALU.mult, op1=ALU.add), reads=[B_arow, B_brow, B_h], writes=[B_h])
                        h0, B_h0 = hrow[0]
                        h1_, B_h1_ = hrow[1]
                        P.op('dve', 'tensor_tensor', KW(out=h0[:, 0:SEQ], in0=h0[:, 0:SEQ], in1=h1_[:, 0:SEQ], op=ALU.add), reads=[B_h0, B_h1_], writes=[B_h0])
                        P.op('dve', 'tensor_tensor', KW(out=orow[:].rearrange("p (r c) -> p r c", c=C.GW),
                                                        in0=h0[:, 0:SEQ].rearrange("p (c r) -> p r c", r=C.ROWS),
                                                        in1=gate[:].rearrange("p (r c) -> p r c", c=C.GW), op=ALU.mult),
                             reads=[B_h0, B_gate], writes=[B_orow])
                        P.dma('sp', rgyT[ch * 128:(ch + 1) * 128, :], orow[:], B_rgyT, B_orow)
                        for _ in range(ada_per):
                            if ada_todo:
                                oc_ = ada_todo.pop(0)
                                wt_, B_wt_ = wada5[oc_ % 3]
                                pt_, B_pt_ = pss[st5['ps'] % 8]
                                st5['ps'] += 1
                                ada_chunk(oc_, wt_, B_wt_, pt_, B_pt_)
                if NADA0 < 9 * KD:
                    cf_tables(1, 'c')
                    cf_tables(2, 'sc')
                P.barrier()
        if 'rgy' in dbg:
            tb, B_tb = sbuf(glob, "dbg_rgb", [128, SEQ], BF16)
            tf, B_tf = sbuf(glob, "dbg_rgf", [128, SEQ])
            o = dbgout("rgy", [RC * 128, SEQ])
            B_o = Buf("dbgrgy")
            for kc in range(RC):
                P.dma('sp', tb[:], rgyT[kc * 128:(kc + 1) * 128, :], B_tb, B_rgyT)
                P.op('dve', 'tensor_copy', KW(out=tf[:], in_=tb[:]), reads=[B_tb], writes=[B_tf])
                P.dma('sp', o[kc * 128:(kc + 1) * 128, :], tf[:], B_o, B_tf)


        T6 = 512
        if upto >= 6:
            with contextlib.ExitStack() as ph:
                NA = max(XC, RC)
                bufA, B_A = sbuf(ph, "p6A", [128, NA * T6], BF16)
                mix, B_mix = sbuf(ph, "p6mix", [128, KD, T6], BF16)
                mT, B_mT = sbuf(ph, "p6mT", [128, KD, T6], BF16)
                WS = max(XC, RC, KD) * 128
                wsl = [sbuf(ph, "p6w%d" % i, [128, WS], BF16) for i in range(2)]
                gch = [sbuf(ph, "p6g%d" % i, [128, T6], BF16) for i in range(2)]
                tmp = [sbuf(ph, "p6t%d" % i, [128, T6]) for i in range(2)]
                sq = [sbuf(ph, "p6sq%d" % i, [128, T6], BF16) for i in range(2)]
                hch = [sbuf(ph, "p6h%d" % i, [128, T6]) for i in range(3)]
                rstd, B_rstd = sbuf(ph, "p6rstd", [128, T6])
                pss = [psum(ph, "p6ps%d" % i, [128, 512]) for i in range(8)]
                st6 = {'ps': 0, 'w': 0, 'g': 0}
                A3 = bufA[:].rearrange("p (k t) -> p k t", t=T6)
                h2b = bufA[:].bitcast(F32).rearrange("p (k t) -> p k t", t=T6)

                def nps():
                    r = pss[st6['ps'] % 8]
                    st6['ps'] += 1
                    return r

                def nw():
                    r = wsl[st6['w'] % 2]
                    st6['w'] += 1
                    return r

                def ng_():
                    r = gch[st6['g'] % 2]
                    st6['g'] += 1
                    return r
                B_c6 = B_cache
                co, B_co = cf["co_1"]
                s1, B_s1 = cf["s1_2"]
                s2, B_s2 = cf["s2_2"]
                for t0 in range(0, SEQ, T6):
                    T = min(T6, SEQ - t0)
                    P.dma('sp', A3[:, 0:XC, 0:T], ynT[:, t0:t0 + T].rearrange("(k p) t -> p k t", p=128), B_A, B_ynT)
                    for dc in range(KD):
                        wt, B_wt = nw()
                        if (t0 == 0 or not CACHE_W) and not (PRECAST and upto >= 4):
                            P.dma('pool', wt[:, 0:XC * 128], wso_t[dc], B_wt, B_w, max_dma_last_dim=4096)
                            if CACHE_W and SEQ > T6:
                                P.dma('sp', wsoc[dc], wt[:, 0:XC * 128], B_c6["wsoc"], B_wt)
                        else:
                            P.dma('pool', wt[:, 0:XC * 128], wsoc[dc], B_wt, B_c6["wsoc"])
                        g_, B_g = ng_()
                        P.dma('sp', g_[:, 0:T], pgt[dc * 128:(dc + 1) * 128, t0:t0 + T], B_g, B_pgt)
                        pt, B_pt = nps()
                        for kc in range(XC):
                            P.op('pe', 'matmul', KW(pt[:, 0:T], lhsT=wt[:, kc * 128:(kc + 1) * 128], rhs=A3[:, kc, 0:T], start=(kc == 0), stop=(kc == XC - 1)),
                                 reads=[B_wt, B_A], writes=[B_pt], inc=(kc == XC - 1))
                        P.op('dve', 'tensor_tensor', KW(out=mix[:, dc, 0:T], in0=pt[:, 0:T], in1=g_[:, 0:T], op=ALU.mult), reads=[B_pt, B_g], writes=[B_mix])
                    P.dma('sp', A3[:, 0:RC, 0:T], rgyT[:, t0:t0 + T].rearrange("(k p) t -> p k t", p=128), B_A, B_rgyT)
                    for dc in range(KD):
                        wt, B_wt = nw()
                        if (t0 == 0 or not CACHE_W) and not (PRECAST and upto >= 4):
                            P.dma('pool', wt[:, 0:RC * 128], wro_t[dc], B_wt, B_w, max_dma_last_dim=4096)
                            if CACHE_W and SEQ > T6:
                                P.dma('sp', wroc[dc], wt[:, 0:RC * 128], B_c6["wroc"], B_wt)
                        else:
                            P.dma('pool', wt[:, 0:RC * 128], wroc[dc], B_wt, B_c6["wroc"])
                        g_, B_g = ng_()
                        P.dma('sp', g_[:, 0:T], pgt[(KD + dc) * 128:(KD + dc + 1) * 128, t0:t0 + T], B_g, B_pgt)
                        pt, B_pt = nps()
                        for kc in range(RC):
                            P.op('pe', 'matmul', KW(pt[:, 0:T], lhsT=wt[:, kc * 128:(kc + 1) * 128], rhs=A3[:, kc, 0:T], start=(kc == 0), stop=(kc == RC - 1)),
                                 reads=[B_wt, B_A], writes=[B_pt], inc=(kc == RC - 1))
                        t_, B_t = tmp[dc % 2]
                        P.op('dve', 'tensor_tensor', KW(out=t_[:, 0:T], in0=pt[:, 0:T], in1=g_[:, 0:T], op=ALU.mult), reads=[B_pt, B_g], writes=[B_t])
                        P.op('pool', 'tensor_tensor', KW(out=mix[:, dc, 0:T], in0=mix[:, dc, 0:T], in1=t_[:, 0:T], op=ALU.add), reads=[B_mix, B_t], writes=[B_mix])
                    for dc in range(KD):
                        wt, B_wt = nw()
                        if (t0 == 0 or not CACHE_W) and not (PRECAST and upto >= 4):
                            P.dma('pool', wt[:, 0:KD * 128], wo_t[dc], B_wt, B_w, max_dma_last_dim=4096)
                            if CACHE_W and SEQ > T6:
                                P.dma('sp', woc[dc], wt[:, 0:KD * 128], B_c6["woc"], B_wt)
                        else:
                            P.dma('pool', wt[:, 0:KD * 128], woc[dc], B_wt, B_c6["woc"])
                        pt, B_pt = nps()
                        for kc in range(KD):
                            P.op('pe', 'matmul', KW(pt[:, 0:T], lhsT=wt[:, kc * 128:(kc + 1) * 128], rhs=mix[:, kc, 0:T], start=(kc == 0), stop=(kc == KD - 1)),
                                 reads=[B_wt, B_mix], writes=[B_pt], inc=(kc == KD - 1))
                        P.op('act', 'activation', KW(out=mT[:, dc, 0:T], in_=pt[:, 0:T], func=AF.Copy), reads=[B_pt], writes=[B_mT])
                    rms_rstd(nps, sq, rstd, B_rstd, lambda kc: mT[:, kc, 0:T], [B_mT], KD, D, T)
                    for kc in range(KD):
                        hc_, B_hc = hch[kc % 3]
                        P.dma('sp', hc_[:, 0:T], h1T[kc * 128:(kc + 1) * 128, t0:t0 + T], B_hc, B_h1T)
                        t_, B_t = tmp[kc % 2]
                        P.op('dve', 'tensor_tensor', KW(out=t_[:, 0:T], in0=mT[:, kc, 0:T], in1=rstd[:, 0:T], op=ALU.mult), reads=[B_mT, B_rstd], writes=[B_t])
                        P.op('dve', 'scalar_tensor_tensor', KW(out=h2b[:, kc, 0:T], in0=t_[:, 0:T], scalar=co[:, kc, 0:1], in1=hc_[:, 0:T],
                                                               op0=ALU.mult, op1=ALU.add), reads=[B_t, B_co, B_hc], writes=[B_A])
                    P.dma('sp', h2T[:, t0:t0 + T].rearrange("(k p) t -> p k t", p=128), h2b[:, 0:KD, 0:T], B_h2T, B_A)
                    rms_rstd(nps, sq, rstd, B_rstd, lambda kc: h2b[:, kc, 0:T], [B_A], KD, D, T)
                    for kc in range(KD):
                        t_, B_t = tmp[kc % 2]
                        P.op('dve', 'tensor_tensor', KW(out=t_[:, 0:T], in0=h2b[:, kc, 0:T], in1=rstd[:, 0:T], op=ALU.mult), reads=[B_A, B_rstd], writes=[B_t])
                        P.op('act', 'activation', KW(out=mT[:, kc, 0:T], in_=t_[:, 0:T], func=AF.Identity, scale=s1[:, kc, 0:1], bias=s2[:, kc, 0:1]),
                             reads=[B_t, B_s1, B_s2], writes=[B_mT])
                    P.dma('sp', u2T[:, t0:t0 + T].rearrange("(k p) t -> p k t", p=128), mT[:, :, 0:T], B_u2T, B_mT)
                P.barrier()
        if 'h2' in dbg:
            o = dbgout("h2T", [D, SEQ])
            P.dma('sp', o[:, :], h2T[:, :], Buf("dbgh2"), B_h2T)

        if upto >= 7:
            ffn_phase("f2", 1, h2T, B_h2T, make_passes(C, SEQ, 0, TP), (u2T, B_u2T), outT, B_outT, None, None, 2, None)
        P.barrier()
        P.emit()
    return nc, dbg_out


def _tile_w(W):
    K, N = W.shape
    KC, OC = K // 128, N // 128
    return np.ascontiguousarray(W.reshape(KC, 128, OC, 128).transpose(2, 1, 0, 3)).reshape(OC, 128, KC * 128)


def _colvec(v, n=None):
    v = np.asarray(v, np.float32)
    return np.ascontiguousarray(v.reshape(-1, 128).T)


def _pad_rg(v, C, fill=0.0):
    v = np.asarray(v, np.float32)
    lead = v.shape[:-1]
    vb = v.reshape(lead + (16, C.RGB))
    out = np.full(lead + (16, C.NBC * 128), fill, np.float32)
    out[..., :C.RGB] = vb
    return out.reshape(lead + (C.RC * 128,))


def prep_shared(C, inp):
    S = {}
    S["w_ada_t"] = _tile_w(inp["w_ada"][0])
    S["b_adaT"] = _colvec(inp["b_ada"][0])
    S["normgT"] = np.ascontiguousarray(np.concatenate([_colvec(inp["norm_g"][0, j]) for j in range(6)], axis=1))
    for i in range(2):
        S["wup%d_t" % i] = _tile_w(inp["ffn_w_up"][0, i])
        S["wdn%d_t" % i] = _tile_w(inp["ffn_w_down"][0, i])
    w_in = inp["w_in"][0]
    D = C.D
    cols = []
    zpad = np.zeros((D, 1), np.float32)

    def take(idx):
        return w_in[:, idx]
    parts = []
    parts.append(w_in[:, 0:C.S1])
    parts.append(w_in[:, C.S1:C.S1 + C.DI])
    parts.append(w_in[:, C.S1 + C.DI:C.S1 + C.DI + C.GN])
    parts.append(w_in[:, C.S1 + C.DI + C.GN:C.S2])
    for d in range(2):
        blk = np.zeros((D, 128), np.float32)
        blk[:, :C.H] = w_in[:, C.S2 + d * C.H:C.S2 + (d + 1) * C.H]
        parts.append(blk)
    for s in (C.S3, C.S4):
        blk = np.zeros((D, 16, C.NBC * 128), np.float32)
        blk[:, :, :C.RGB] = w_in[:, s:s + C.RGW].reshape(D, 16, C.RGB)
        parts.append(blk.reshape(D, C.RC * 128))
    parts.append(w_in[:, C.S5:])
    S["win_t"] = _tile_w(np.concatenate(parts, axis=1))
    vec = np.zeros((128, C.NV), np.float32)
    cw = inp["ssd_conv_w"][0]
    nxc = C.XC + 16
    vec[:, C.V["cw"]:C.V["cw"] + nxc * 4] = cw.T.reshape(nxc, 128, 4).transpose(1, 0, 2).reshape(128, nxc * 4)
    vec[:, C.V["cb"]:C.V["cb"] + nxc] = _colvec(inp["ssd_conv_b"][0])
    for d in range(2):
        vec[:C.H, C.V["dtb"] + d] = inp["ssd_dt_bias"][0][d * C.H:(d + 1) * C.H]
        vec[:C.H, C.V["alog"] + d] = inp["ssd_a_log"][0, d]
    vec[:, C.V["sng"]:C.V["sng"] + C.XC] = _colvec(inp["ssd_norm_g"][0])
    rcw = _pad_rg(inp["rg_conv_w"][0], C)
    vec[:, C.V["rcw"]:C.V["rcw"] + C.RC * 4] = rcw.T.reshape(C.RC, 128, 4).transpose(1, 0, 2).reshape(128, C.RC * 4)
    vec[:, C.V["rcb"]:C.V["rcb"] + C.RC] = _colvec(_pad_rg(inp["rg_conv_b"][0], C))
    for d in range(2):
        vec[:, C.V["rba"] + d * C.RC:C.V["rba"] + (d + 1) * C.RC] = _colvec(_pad_rg(inp["rg_b_a"][0, d], C))
        vec[:, C.V["rbx"] + d * C.RC:C.V["rbx"] + (d + 1) * C.RC] = _colvec(_pad_rg(inp["rg_b_x"][0, d], C))
        vec[:, C.V["rlam"] + d * C.RC:C.V["rlam"] + (d + 1) * C.RC] = _colvec(_pad_rg(inp["rg_lam"][0, d], C))
    S["vecs"] = vec
    dr = np.zeros((1, 128), np.float32)
    dr[0, :C.H] = inp["ssd_d"][0]
    S["drow"] = dr
    NB = C.NBC
    rg = np.zeros((2, 2, 16, NB, 128, NB, 128), np.float32)
    for d in range(2):
        for ty, nm in enumerate(("rg_w_a", "rg_w_x")):
            w = np.zeros((16, NB * 128, NB * 128), np.float32)
            w[:, :C.RGB, :C.RGB] = inp[nm][0, d]
            rg[d, ty] = w.reshape(16, NB, 128, NB, 128).transpose(0, 3, 2, 1, 4)
    S["rgw_t"] = np.ascontiguousarray(rg.reshape(64 * NB, 128, NB * 128))
    S["wso_t"] = _tile_w(inp["w_ssd_out"][0])
    wro = np.zeros((16, NB * 128, D), np.float32)
    wro[:, :C.RGB, :] = inp["w_rg_out"][0].reshape(16, C.RGB, D)
    S["wro_t"] = _tile_w(wro.reshape(C.RC * 128, D))
    S["wo_t"] = _tile_w(inp["w_out"][0])
    return S


def prep_core(C, inp, b):
    m = {}
    m["xT"] = np.ascontiguousarray(np.concatenate([inp["x"][b].T, inp["ctx"][b].T], axis=1).astype(np.float32))
    sc = np.stack([_colvec(inp["c"][b]), _colvec(inp["c_ctx"])], axis=2)
    m["scT"] = np.ascontiguousarray(sc.reshape(128, C.KD * 2))
    return m


_CACHE = {}


def kernel(**inputs):
    C = Cfg()
    inp = {k: np.asarray(v) for k, v in inputs.items()}
    nb = inp["x"].shape[0]
    S = prep_shared(C, inp)
    if "nc" not in _CACHE:
        _CACHE["nc"] = build(C)[0]
    nc = _CACHE["nc"]
    in_maps = []
    for b in range(nb):
        m = dict(S)
        m.update(prep_core(C, inp, b))
        in_maps.append(m)
    res = run_bass_kernel_spmd(nc, in_maps, core_ids=list(range(nb)))
    out = np.stack([np.ascontiguousarray(r["outT"].T) for r in res.results], axis=0)
    return out.astype(np.float32)
```

```python
import contextlib
import math
import os as _os
import numpy as np
import concourse.bass as bass
import concourse.mybir as mybir
from concourse.bass_utils import run_bass_kernel_spmd

F32 = mybir.dt.float32
BF16 = mybir.dt.bfloat16
ALU = mybir.AluOpType
AF = mybir.ActivationFunctionType
EPS = 1e-6
NEG = -30000.0
PRECAST = True
DEFER_ADA = True
CACHE_W = True


class Cfg:
    def __init__(self, D=4096, DFF=11008, SEQ=2048, CTX=256, HB=None):
        self.D, self.DFF, self.SEQ, self.CTX = D, DFF, SEQ, CTX
        self.KD = D // 128
        self.KF = DFF // 128
        self.NT = SEQ + CTX
        self.GW = 64
        self.ROWS = SEQ // 64
        self.DI = 2 * D
        self.H = self.DI // 64
        self.E = self.H // 8
        self.XC = self.DI // 128
        self.GN = 1024
        self.RGW = (D * 4 // 3) // 256 * 256
        self.RGB = self.RGW // 16
        self.NBC = (self.RGB + 127) // 128
        self.RC = 16 * self.NBC
        self.S1 = self.DI
        self.S2 = self.S1 + self.DI + 2 * self.GN
        self.S3 = self.S2 + 2 * self.H
        self.S4 = self.S3 + self.RGW
        self.S5 = self.S4 + self.RGW
        self.PIN = self.S5 + 2 * D
        self.OC_Z = 0
        self.OC_X = self.OC_Z + self.XC
        self.OC_B = self.OC_X + self.XC
        self.OC_C = self.OC_B + 8
        self.OC_DT = self.OC_C + 8
        self.OC_RG = self.OC_DT + 2
        self.OC_RX = self.OC_RG + self.RC
        self.OC_G = self.OC_RX + self.RC
        self.NOC = self.OC_G + 2 * self.KD
        self.NCL = SEQ // 128
        self.NCC = CTX // 128
        self.NCH = self.NCL + self.NCC
        self.HB = HB or min(8, self.E)
        self.NQ = self.E // self.HB
        self.TP = 512
        self.NS = 512
        o = 0
        self.V = {}
        for nm, n in (("cw", (self.XC + 16) * 4), ("cb", self.XC + 16), ("dtb", 2), ("alog", 2), ("sng", self.XC),
                      ("rcw", self.RC * 4), ("rcb", self.RC), ("rba", 2 * self.RC), ("rbx", 2 * self.RC), ("rlam", 2 * self.RC)):
            self.V[nm] = o
            o += n
        self.NV = o


def KW(*a, **k):
    return (a, k)


def _call(e, meth, args, kw):
    try:
        return getattr(e, meth)(*args, **kw)
    except Exception:
        print("FAILED INSTR", meth, [str(a)[:200] for a in args], {k: str(v)[:200] for k, v in kw.items()})
        raise


class Buf:
    def __init__(self, name):
        self.name = name
        self.last_w = None
        self.readers = []
        self.dsem = None
        self.psum = False


class Prog:
    ENG = ('pe', 'act', 'dve', 'pool', 'sp')

    def __init__(self, nc, ndsem=56):
        self.nc = nc
        self.q = {e: [] for e in self.ENG}
        self.cnt = {}
        self.known = {e: {} for e in self.ENG}
        self.semh = {}
        self.free_d = []
        self.dbufs = []
        self.nins = 0
        for e in self.ENG:
            self._sem('e:' + e)
        self.free_d = {'pool': [], 'sp': [], 'act': []}
        for i in range(ndsem):
            k = 'd:%d' % i
            self._sem(k)
            self.free_d['pool' if i < 10 else 'sp'].append(k)

    def _sem(self, key):
        if key not in self.semh:
            self.semh[key] = self.nc.alloc_semaphore(key.replace(':', '_'))
            self.cnt[key] = 0
        return self.semh[key]

    def _wait(self, eng, tok):
        if tok is None:
            return
        key, val = tok
        if key[0] == 'd':
            val = self.cnt[key]
        elif key == 'e:' + eng:
            if eng == 'pe' or val > self.cnt[key]:
                return
        if self.known[eng].get(key, 0) >= val:
            return
        self.known[eng][key] = val
        h = self.semh[key]
        self.q[eng].append(lambda e, h=h, val=val: e.wait_ge(h, val))

    def _deps(self, eng, reads, writes):
        for b in reads:
            self._wait(eng, b.last_w)
            if b.psum:
                for t in b.readers:
                    if t[0] != 'e:' + eng:
                        self._wait(eng, t)
        for b in writes:
            self._wait(eng, b.last_w)
            for t in b.readers:
                self._wait(eng, t)

    @staticmethod
    def _compact(toks):
        best = {}
        for k, v in toks:
            if best.get(k, 0) < v:
                best[k] = v
        return list(best.items())

    def op(self, eng, meth, akw, reads=(), writes=(), inc=True):
        args, kw = akw
        self._deps(eng, reads, writes)
        self.nins += 1
        key = 'e:' + eng
        h = self.semh[key]
        if inc:
            self.cnt[key] += 1
            tok = (key, self.cnt[key])
            self.q[eng].append(lambda e, meth=meth, args=args, kw=kw, h=h: _call(e, meth, args, kw).then_inc(h, 1))
        else:
            tok = (key, self.cnt[key] + 1)
            self.q[eng].append(lambda e, meth=meth, args=args, kw=kw: _call(e, meth, args, kw))
        for b in reads:
            b.readers.append(tok)
            if len(b.readers) > 48:
                b.readers = self._compact(b.readers)
        for b in writes:
            b.last_w = tok
            b.readers = []
        return tok

    def dma(self, eng, out_ap, in_ap, dst, src, **kw):
        self._deps(eng, [src], [dst])
        self.nins += 1
        if dst.dsem is None:
            dst.dsem = self.free_d[eng].pop()
            dst.dq = eng
            self.dbufs.append(dst)
        key = dst.dsem
        h = self.semh[key]
        self.cnt[key] += 16
        tok = (key, self.cnt[key])
        self.q[eng].append(lambda e, h=h, out_ap=out_ap, in_ap=in_ap, kw=kw: e.dma_start(out=out_ap, in_=in_ap, **kw).then_inc(h, 16))
        src.readers.append(tok)
        if len(src.readers) > 48:
            src.readers = self._compact(src.readers)
        dst.last_w = tok
        dst.readers = []
        return tok

    def barrier(self):
        for eng in self.ENG:
            for key in list(self.cnt):
                if self.cnt[key] > 0:
                    self._wait(eng, (key, self.cnt[key]))
        for b in self.dbufs:
            self.free_d[b.dq].append(b.dsem)
            b.dsem = None
        self.dbufs = []

    def emit(self):
        nc = self.nc
        with nc.Block() as block:
            @block.tensor
            def _(e):
                for f in self.q['pe']:
                    f(e)

            @block.scalar
            def _(e):
                for f in self.q['act']:
                    f(e)

            @block.vector
            def _(e):
                for f in self.q['dve']:
                    f(e)

            @block.gpsimd
            def _(e):
                for f in self.q['pool']:
                    f(e)

            @block.sync
            def _(e):
                for f in self.q['sp']:
                    f(e)


def make_passes(C, nlat, nctx, TP):
    segs = [(0, nlat, 0)]
    if nctx:
        segs.append((C.SEQ, nctx, 1))
    passes, cur, room = [], [], TP
    for (s0, n, w) in segs:
        while n > 0:
            t = min(n, room)
            cur.append((s0, t, w))
            s0 += t
            n -= t
            room -= t
            if room == 0:
                passes.append(cur)
                cur, room = [], TP
    if cur:
        passes.append(cur)
    return passes


def slices_of(n, NS):
    out, c = [], 0
    while c < n:
        t = min(NS, n - c)
        out.append((c, t))
        c += t
    return out


def build(C, upto=99, dbg=()):
    nc = bass.Bass("TRN2", target_bir_lowering=False)
    P = Prog(nc)
    D, KD, KF, NT, SEQ, CTX, TP, NS = C.D, C.KD, C.KF, C.NT, C.SEQ, C.CTX, C.TP, C.NS
    H, E, XC, RC = C.H, C.E, C.XC, C.RC

    def din(name, shape, dt=F32):
        return nc.dram_tensor(name, list(shape), dt, kind="ExternalInput").ap()

    def dsc(name, shape, dt):
        return nc.dram_tensor(name, list(shape), dt).ap()

    dbg_out = {}

    def dbgout(name, shape):
        dbg_out[name] = nc.dram_tensor("dbg_" + name, list(shape), F32, kind="ExternalOutput").ap()
        return dbg_out[name]

    xT = din("xT", [D, NT])
    scT = din("scT", [128, KD * 2])
    w_ada_t = din("w_ada_t", [9 * KD, 128, KD * 128])
    b_adaT = din("b_adaT", [128, 9 * KD])
    normgT = din("normgT", [128, 6 * KD])
    wup_t = [din("wup%d_t" % i, [2 * KF, 128, KD * 128]) for i in range(2)]
    wdn_t = [din("wdn%d_t" % i, [KD, 128, KF * 128]) for i in range(2)]
    win_t = din("win_t", [C.NOC, 128, KD * 128])
    vecs = din("vecs", [128, C.NV])
    drow = din("drow", [1, 128])
    rgw_t = din("rgw_t", [64 * C.NBC, 128, C.NBC * 128])
    wso_t = din("wso_t", [KD, 128, XC * 128])
    wro_t = din("wro_t", [KD, 128, RC * 128])
    wo_t = din("wo_t", [KD, 128, KD * 128])
    outT = nc.dram_tensor("outT", [D, SEQ], F32, kind="ExternalOutput").ap()

    h1T = dsc("h1T", [D, NT], F32)
    u1T = dsc("u1T", [D, NT], BF16)
    pz = dsc("pz", [C.DI, SEQ], BF16)
    x_tm = dsc("x_tm", [NT, C.DI], BF16)
    pB = dsc("pB", [C.GN, NT], BF16)
    pC = dsc("pC", [C.GN, NT], BF16)
    B_tm = dsc("B_tm", [NT, C.GN], BF16)
    pdt = dsc("pdt", [256, NT], F32)
    prg = dsc("prg", [RC * 128, SEQ], BF16)
    prx = dsc("prx", [RC * 128, NT], F32)
    pgt = dsc("pgt", [2 * D, SEQ], BF16)
    ynT = dsc("ynT", [C.DI, SEQ], BF16)
    rgyT = dsc("rgyT", [RC * 128, SEQ], BF16)
    sbin = dsc("sbin", [8, C.NCL, 128, E * 64], BF16)
    wupc = [dsc("wupc%d" % i, [KF, 128, 2 * KD * 128], BF16) for i in range(2)]
    wdnc = [dsc("wdnc%d" % i, [KD, 128, KF * 128], BF16) for i in range(2)]
    wsoc = dsc("wsoc", [KD, 128, XC * 128], BF16)
    wroc = dsc("wroc", [KD, 128, RC * 128], BF16)
    woc = dsc("woc", [KD, 128, KD * 128], BF16)
    h2T = dsc("h2T", [D, SEQ], F32)
    u2T = dsc("u2T", [D, SEQ], BF16)

    B_w = Buf("weights")
    B_xT, B_h1T, B_u1T, B_outT = Buf("xT"), Buf("h1T"), Buf("u1T"), Buf("outT")
    B_pz, B_xtm, B_pB, B_pC, B_Btm, B_pdt = Buf("pz"), Buf("x_tm"), Buf("pB"), Buf("pC"), Buf("B_tm"), Buf("pdt")
    B_prg, B_prx, B_pgt, B_ynT, B_rgyT, B_sbin = Buf("prg"), Buf("prx"), Buf("pgt"), Buf("ynT"), Buf("rgyT"), Buf("sbin")
    B_h2T, B_u2T = Buf("h2T"), Buf("u2T")
    B_cache = {k_: Buf(k_) for k_ in ("wupc0", "wdnc0", "wupc1", "wdnc1", "wsoc", "wroc", "woc")}

    with contextlib.ExitStack() as glob:
        def sbuf(stack, name, shape, dt=F32):
            t = stack.enter_context(nc.sbuf_tensor(name, list(shape), dt))
            return t, Buf(name)

        def psum(stack, name, shape, dt=F32):
            t = stack.enter_context(nc.psum_tensor(name, list(shape), dt))
            b = Buf(name)
            b.psum = True
            return t, b

        modT, B_mod = sbuf(glob, "modT", [128, 9 * KD, 2])
        ngT, B_ng = sbuf(glob, "ngT", [128, 6 * KD])
        ones_bf, B_ones = sbuf(glob, "ones_bf", [128, 128], BF16)
        ones32, B_ones32 = sbuf(glob, "ones32", [128, 128])
        ident32, B_id32 = sbuf(glob, "ident32", [128, 128])
        identb, B_idb = sbuf(glob, "identb", [128, 128], BF16)
        vec, B_vec = sbuf(glob, "vec", [128, C.NV])
        cf = {}
        for nm in ("s1_0", "s2_0", "co_0", "s1_1", "s2_1", "co_1", "s1_2", "s2_2", "co_2"):
            cf[nm] = sbuf(glob, "cf_" + nm, [128, KD, 2])
        P.dma('sp', ngT[:], normgT[:, :], B_ng, B_w)
        P.dma('sp', vec[:], vecs[:, :], B_vec, B_w)
        P.op('dve', 'memset', KW(ones_bf[:], 1.0), writes=[B_ones])
        P.op('dve', 'memset', KW(ones32[:], 1.0), writes=[B_ones32])
        P.op('pool', 'memset', KW(ident32[:], 1.0), writes=[B_id32])
        P.op('pool', 'affine_select', KW(out=ident32[:], in_=ident32[:], pattern=[[1, 128]], base=0, channel_multiplier=-1,
                                         compare_op=ALU.is_equal, fill=0.0), reads=[B_id32], writes=[B_id32])
        P.op('dve', 'tensor_copy', KW(out=identb[:], in_=ident32[:]), reads=[B_id32], writes=[B_idb])

        def V(nm, j=0, n=1):
            o = C.V[nm] + j
            return vec[:, o:o + n]

        scb, B_scb = sbuf(glob, "scb", [128, KD, 2], BF16)
        badT, B_bad = sbuf(glob, "badT", [128, 9 * KD])
        NADA0 = 5 * KD if (upto >= 5 and DEFER_ADA) else 9 * KD

        def ada_chunk(oc, wt, B_wt, pt, B_pt):
            P.dma('pool', wt[:, 0:KD * 128], w_ada_t[oc], B_wt, B_w, max_dma_last_dim=4096)
            for kc in range(KD):
                P.op('pe', 'matmul', KW(pt[:, 0:2], lhsT=wt[:, kc * 128:(kc + 1) * 128], rhs=scb[:, kc, :],
                                        start=(kc == 0), stop=(kc == KD - 1)),
                     reads=[B_wt, B_scb], writes=[B_pt], inc=(kc == KD - 1))
            P.op('dve', 'tensor_scalar', KW(out=modT[:, oc, :], in0=pt[:, 0:2], scalar1=badT[:, oc:oc + 1], scalar2=None,
                                            op0=ALU.add), reads=[B_pt, B_bad], writes=[B_mod])

        def mslot(j):
            return modT[:, j * KD:(j + 1) * KD, :]

        def ng(j):
            return ngT[:, j * KD:(j + 1) * KD].unsqueeze(2).to_broadcast([128, KD, 2])

        def cf_tables(k, which):
            gpre, gpost = ((0, 1), (2, 3), (4, 5))[k]
            s1, B_s1 = cf["s1_%d" % k]
            s2, B_s2 = cf["s2_%d" % k]
            co, B_co = cf["co_%d" % k]
            if 's' in which:
                P.op('dve', 'scalar_tensor_tensor', KW(out=s1[:], in0=mslot(3 * k + 1), scalar=1.0, in1=ng(gpre), op0=ALU.add, op1=ALU.mult),
                     reads=[B_mod, B_ng], writes=[B_s1])
                P.op('dve', 'tensor_copy', KW(out=s2[:], in_=mslot(3 * k)), reads=[B_mod], writes=[B_s2])
            if 'c' in which:
                P.op('dve', 'scalar_tensor_tensor', KW(out=co[:], in0=mslot(3 * k + 2), scalar=(1.0 if k == 1 else 0.5), in1=ng(gpost),
                                                       op0=ALU.mult, op1=ALU.mult), reads=[B_mod, B_ng], writes=[B_co])

        with contextlib.ExitStack() as ph:
            sc32, B_sc32 = sbuf(ph, "sc32", [128, KD * 2])
            NSL = 3
            wsl = [sbuf(ph, "wada%d" % i, [128, KD * 128], BF16) for i in range(NSL)]
            pss = [psum(ph, "p0ps%d" % i, [128, 512]) for i in range(4)]
            P.dma('sp', sc32[:], scT[:, :], B_sc32, B_w)
            P.dma('sp', badT[:], b_adaT[:, :], B_bad, B_w)
            P.op('act', 'activation', KW(out=scb[:].rearrange("p k t -> p (k t)"), in_=sc32[:], func=AF.Silu), reads=[B_sc32], writes=[B_scb])
            for oc in range(NADA0):
                wt, B_wt = wsl[oc % NSL]
                pt, B_pt = pss[oc % 4]
                ada_chunk(oc, wt, B_wt, pt, B_pt)
            cf_tables(0, 'sc')
            cf_tables(1, 's')
            if NADA0 == 9 * KD:
                cf_tables(1, 'c')
                cf_tables(2, 'sc')
            P.barrier()
        if 'mod' in dbg:
            o = dbgout("mod", [128, 9 * KD * 2])
            P.dma('sp', o[:, :], modT[:].rearrange("p a b -> p (a b)"), Buf("dbgmod"), B_mod)

        def rms_rstd(nps, sq, rstd, B_rstd, src_chunk, src_bufs, nchunks, nd, T):
            sls = slices_of(T, 512)
            pts = [nps() for _ in sls]
            for kc in range(nchunks):
                s_, B_s = sq[kc % 2]
                P.op('act', 'activation', KW(out=s_[:, 0:T], in_=src_chunk(kc), func=AF.Square), reads=src_bufs, writes=[B_s])
                for (pt, B_pt), (c0, n) in zip(pts, sls):
                    P.op('pe', 'matmul', KW(pt[:, 0:n], lhsT=ones_bf[:], rhs=s_[:, c0:c0 + n], start=(kc == 0), stop=(kc == nchunks - 1)),
                         reads=[B_s, B_ones], writes=[B_pt])
            for (pt, B_pt), (c0, n) in zip(pts, sls):
                P.op('act', 'activation', KW(out=rstd[:, c0:c0 + n], in_=pt[:, 0:n], func=AF.Sqrt, scale=1.0 / nd, bias=EPS),
                     reads=[B_pt], writes=[B_rstd])
            P.op('dve', 'reciprocal', KW(out=rstd[:, 0:T], in_=rstd[:, 0:T]), reads=[B_rstd], writes=[B_rstd])

        def ffn_phase(tag, l, h_src, B_hsrc, passes, u_src, h_dst, B_hdst, u_dst, B_udst, kpre, knext):
            B_wupc, B_wdnc = B_cache["wupc%d" % l], B_cache["wdnc%d" % l]
            pre = PRECAST and l == 1 and upto >= 4
            with contextlib.ExitStack() as ph:
                uT, B_uT = sbuf(ph, tag + "uT", [128, KD, TP], BF16)
                gT, B_gT = sbuf(ph, tag + "gT", [128, max(KF, 2 * KD) * TP], BF16)
                WS = max(KF, 2 * KD) * 128
                wsl = [sbuf(ph, tag + "w%d" % i, [128, WS], BF16) for i in range(2)]
                tmp = [sbuf(ph, tag + "tmp%d" % i, [128, TP]) for i in range(2)]
                sq = [sbuf(ph, tag + "sq%d" % i, [128, TP], BF16) for i in range(2)]
                rstd, B_rstd = sbuf(ph, tag + "rstd", [128, TP])
                hch = [sbuf(ph, tag + "hch%d" % i, [128, TP]) for i in range(3)]
                pss = [psum(ph, tag + "ps%d" % i, [128, 512]) for i in range(8)]
                st = {'ps': 0, 'w': 0}
                hT32 = gT[:].bitcast(F32).rearrange("p (k t) -> p k t", t=TP)

                def nps():
                    r = pss[st['ps'] % 8]
                    st['ps'] += 1
                    return r

                def nw():
                    r = wsl[st['w'] % 2]
                    st['w'] += 1
                    return r

                def normmod(k, segs, T):
                    s1, B_s1 = cf["s1_%d" % k]
                    s2, B_s2 = cf["s2_%d" % k]
                    for kc in range(KD):
                        t_, B_t = tmp[kc % 2]
                        P.op('dve', 'tensor_tensor', KW(out=t_[:, 0:T], in0=hT32[:, kc, 0:T], in1=rstd[:, 0:T], op=ALU.mult),
                             reads=[B_gT, B_rstd], writes=[B_t])
                        c0 = 0
                        for (_, n, which) in segs:
                            P.op('act', 'activation', KW(out=uT[:, kc, c0:c0 + n], in_=t_[:, c0:c0 + n], func=AF.Identity,
                                                         scale=s1[:, kc, which:which + 1], bias=s2[:, kc, which:which + 1]),
                                 reads=[B_t, B_s1, B_s2], writes=[B_uT])
                            c0 += n

                for ip, segs in enumerate(passes):
                    T = sum(n for (_, n, _) in segs)
                    sls = slices_of(T, NS)
                    if u_src is None:
                        c0 = 0
                        for (s0, n, which) in segs:
                            P.dma('sp', hT32[:, 0:KD, c0:c0 + n], h_src[:, s0:s0 + n].rearrange("(k p) t -> p k t", p=128), B_gT, B_hsrc)
                            c0 += n
                        rms_rstd(nps, sq, rstd, B_rstd, lambda kc: hT32[:, kc, 0:T], [B_gT], KD, D, T)
                        normmod(kpre, segs, T)
                    else:
                        c0 = 0
                        for (s0, n, which) in segs:
                            P.dma('sp', uT[:, :, c0:c0 + n], u_src[0][:, s0:s0 + n].rearrange("(k p) t -> p k t", p=128), B_uT, u_src[1])
                            c0 += n
                    for fc in range(KF):
                        wt, B_wt = nw()
                        if (ip == 0 or not CACHE_W) and not pre:
                            P.dma('pool', wt[:, 0:KD * 128], wup_t[l][fc], B_wt, B_w, max_dma_last_dim=4096)
                            P.dma('pool', wt[:, KD * 128:2 * KD * 128], wup_t[l][KF + fc], B_wt, B_w, max_dma_last_dim=4096)
                            if CACHE_W and len(passes) > 1:
                                P.dma('sp', wupc[l][fc], wt[:, 0:2 * KD * 128], B_wupc, B_wt)
                        else:
                            P.dma('pool', wt[:, 0:2 * KD * 128], wupc[l][fc], B_wt, B_wupc)
                        for si, (c0, n) in enumerate(sls):
                            pg, B_pg = nps()
                            pu, B_pu = nps()
                            for kc in range(KD):
                                P.op('pe', 'matmul', KW(pg[:, 0:n], lhsT=wt[:, kc * 128:(kc + 1) * 128], rhs=uT[:, kc, c0:c0 + n],
                                                        start=(kc == 0), stop=(kc == KD - 1)), reads=[B_wt, B_uT], writes=[B_pg], inc=(kc == KD - 1))
                            for kc in range(KD):
                                P.op('pe', 'matmul', KW(pu[:, 0:n], lhsT=wt[:, (KD + kc) * 128:(KD + kc + 1) * 128], rhs=uT[:, kc, c0:c0 + n],
                                                        start=(kc == 0), stop=(kc == KD - 1)), reads=[B_wt, B_uT], writes=[B_pu], inc=(kc == KD - 1))
                            t_, B_t = tmp[si % 2]
                            P.op('act', 'activation', KW(out=t_[:, 0:n], in_=pg[:, 0:n], func=AF.Silu), reads=[B_pg], writes=[B_t])
                            P.op('dve', 'tensor_tensor', KW(out=gT[:, fc * TP + c0: fc * TP + c0 + n], in0=t_[:, 0:n], in1=pu[:, 0:n], op=ALU.mult),
                                 reads=[B_t, B_pu], writes=[B_gT])
                    for oc in range(KD):
                        wt, B_wt = nw()
                        if (ip == 0 or not CACHE_W) and not pre:
                            P.dma('pool', wt[:, 0:KF * 128], wdn_t[l][oc], B_wt, B_w, max_dma_last_dim=4096)
                            if CACHE_W and len(passes) > 1:
                                P.dma('sp', wdnc[l][oc], wt[:, 0:KF * 128], B_wdnc, B_wt)
                        else:
                            P.dma('pool', wt[:, 0:KF * 128], wdnc[l][oc], B_wt, B_wdnc)
                        for (c0, n) in sls:
                            pt, B_pt = nps()
                            for kc in range(KF):
                                P.op('pe', 'matmul', KW(pt[:, 0:n], lhsT=wt[:, kc * 128:(kc + 1) * 128], rhs=gT[:, kc * TP + c0: kc * TP + c0 + n],
                                                        start=(kc == 0), stop=(kc == KF - 1)), reads=[B_wt, B_gT], writes=[B_pt], inc=(kc == KF - 1))
                            P.op('act', 'activation', KW(out=uT[:, oc, c0:c0 + n], in_=pt[:, 0:n], func=AF.Copy), reads=[B_pt], writes=[B_uT])
                    rms_rstd(nps, sq, rstd, B_rstd, lambda kc: uT[:, kc, 0:T], [B_uT], KD, D, T)
                    co, B_co = cf["co_%d" % kpre]
                    for kc in range(KD):
                        hc_, B_hc = hch[kc % 3]
                        c0 = 0
                        for (s0, n, which) in segs:
                            P.dma('sp', hc_[:, c0:c0 + n], h_src[kc * 128:(kc + 1) * 128, s0:s0 + n], B_hc, B_hsrc)
                            c0 += n
                        t_, B_t = tmp[kc % 2]
                        P.op('dve', 'tensor_tensor', KW(out=t_[:, 0:T], in0=uT[:, kc, 0:T], in1=rstd[:, 0:T], op=ALU.mult), reads=[B_uT, B_rstd], writes=[B_t])
                        c0 = 0
                        for (s0, n, which) in segs:
                            P.op('dve', 'scalar_tensor_tensor', KW(out=hT32[:, kc, c0:c0 + n], in0=t_[:, c0:c0 + n], scalar=co[:, kc, which:which + 1],
                                                                   in1=hc_[:, c0:c0 + n], op0=ALU.mult, op1=ALU.add),
                                 reads=[B_t, B_co, B_hc], writes=[B_gT])
                            c0 += n
                    c0 = 0
                    for (s0, n, which) in segs:
                        P.dma('sp', h_dst[:, s0:s0 + n].rearrange("(k p) t -> p k t", p=128), hT32[:, 0:KD, c0:c0 + n], B_hdst, B_gT)
                        c0 += n
                    if knext is not None:
                        rms_rstd(nps, sq, rstd, B_rstd, lambda kc: hT32[:, kc, 0:T], [B_gT], KD, D, T)
                        normmod(knext, segs, T)
                        c0 = 0
                        for (s0, n, which) in segs:
                            P.dma('sp', u_dst[:, s0:s0 + n].rearrange("(k p) t -> p k t", p=128), uT[:, :, c0:c0 + n], B_udst, B_uT)
                            c0 += n
                P.barrier()

        if upto >= 1:
            ffn_phase("f1", 0, xT, B_xT, make_passes(C, SEQ, CTX, TP), None, h1T, B_h1T, u1T, B_u1T, 0, 1)
        if 'h1' in dbg:
            o = dbgout("h1T", [D, NT])
            P.dma('sp', o[:, :], h1T[:, :], Buf("dbgh1"), B_h1T)
            hb, B_hb = sbuf(glob, "dbg_u1", [128, NT], BF16)
            hf, B_hf = sbuf(glob, "dbg_u1f", [128, NT])
            o = dbgout("u1T", [D, NT])
            B_o = Buf("dbgu1")
            for kc in range(KD):
                P.dma('sp', hb[:], u1T[kc * 128:(kc + 1) * 128, :], B_hb, B_u1T)
                P.op('dve', 'tensor_copy', KW(out=hf[:], in_=hb[:]), reads=[B_hb], writes=[B_hf])
                P.dma('sp', o[kc * 128:(kc + 1) * 128, :], hf[:], B_o, B_hf)

        if upto >= 2:
            with contextlib.ExitStack() as ph:
                uA, B_uA = sbuf(ph, "p2u", [128, KD, NT], BF16)
                wsl = [sbuf(ph, "p2w%d" % i, [128, KD * 128], BF16) for i in range(2)]
                raw, B_raw = sbuf(ph, "p2raw", [128, NT])
                tm2, B_tm2 = sbuf(ph, "p2tmp", [128, NT])
                ob = [sbuf(ph, "p2ob%d" % i, [128, NT], BF16) for i in range(1)]
                xblk = [sbuf(ph, "p2xb%d" % i, [128, 8 * 128], BF16) for i in range(2)]
                pss = [psum(ph, "p2ps%d" % i, [128, 512]) for i in range(7)]
                psT, B_psT = psum(ph, "p2psT", [128, 1024], BF16)
                st = {'ps': 0, 'w': 0, 'ob': 0, 'xb': 0}
                B_dst = {}
                for kc in range(KD):
                    P.dma('sp', uA[:, kc, :], u1T[kc * 128:(kc + 1) * 128, :], B_uA, B_u1T)
                lat_sl = slices_of(SEQ, 512)
                all_sl = lat_sl + [(SEQ + c0, n) for (c0, n) in slices_of(CTX, 512)]
                segs_lc = [(0, SEQ), (SEQ, NT)]

                def conv(src, dst, wcol, bcol):
                    P.op('dve', 'tensor_scalar', KW(out=dst[:, 0:NT], in0=src[:, 0:NT], scalar1=V(wcol[0], wcol[1] + 2), scalar2=V(bcol[0], bcol[1]),
                                                    op0=ALU.mult, op1=ALU.add), reads=[src_b[0], B_vec], writes=[dst_b[0]])
                    for k in (0, 1, 3):
                        o_ = k - 2
                        for (a, e) in segs_lc:
                            d0, d1 = a + max(0, -o_), e - max(0, o_)
                            P.op('dve', 'scalar_tensor_tensor', KW(out=dst[:, d0:d1], in0=src[:, d0 + o_:d1 + o_], scalar=V(wcol[0], wcol[1] + k),
                                                                   in1=dst[:, d0:d1], op0=ALU.mult, op1=ALU.add),
                                 reads=[src_b[0], B_vec], writes=[dst_b[0]])
                src_b, dst_b = [None], [None]

                _skip = _os.environ.get('P2SKIP', '').split(',')
                pend = []
                for oc in range(C.NOC):
                    _ty = ('z' if oc < C.OC_X else 'x' if oc < C.OC_DT else 'dt' if oc < C.OC_RG else 'rg' if oc < C.OC_RX else 'rx' if oc < C.OC_G else 'g')
                    if _ty in _skip:
                        continue
                    lat_only = (oc < C.OC_X) or (C.OC_RG <= oc < C.OC_RX) or (oc >= C.OC_G)
                    sls = lat_sl if lat_only else all_sl
                    wt, B_wt = wsl[st['w'] % 2]
                    st['w'] += 1
                    P.dma('pool', wt[:], win_t[oc], B_wt, B_w, max_dma_last_dim=4096)
                    pts = []
                    for (c0, n) in sls:
                        pt, B_pt = pss[st['ps'] % 7]
                        st['ps'] += 1
                        for kc in range(KD):
                            P.op('pe', 'matmul', KW(pt[:, 0:n], lhsT=wt[:, kc * 128:(kc + 1) * 128], rhs=uA[:, kc, c0:c0 + n],
                                                    start=(kc == 0), stop=(kc == KD - 1)), reads=[B_wt, B_uA], writes=[B_pt], inc=(kc == KD - 1))
                        pts.append((pt, B_pt, c0, n))
                    while pend:
                        pend.pop(0)()
                    o_t, B_ot = ob[0]
                    if oc < C.OC_X or oc >= C.OC_G:
                        st['ob'] += 1
                        fn = AF.Silu if oc < C.OC_X else AF.Sigmoid
                        for (pt, B_pt, c0, n) in pts:
                            P.op('act', 'activation', KW(out=o_t[:, c0:c0 + n], in_=pt[:, 0:n], func=fn), reads=[B_pt], writes=[B_ot])
                        if oc < C.OC_X:
                            P.dma('sp', pz[oc * 128:(oc + 1) * 128, :], o_t[:, 0:SEQ], B_pz, B_ot)
                        else:
                            j = oc - C.OC_G
                            P.dma('sp', pgt[j * 128:(j + 1) * 128, :], o_t[:, 0:SEQ], B_pgt, B_ot)
                    elif oc < C.OC_DT:
                        st['ob'] += 1
                        j = oc - C.OC_X
                        for (pt, B_pt, c0, n) in pts:
                            P.op('act', 'activation', KW(out=raw[:, c0:c0 + n], in_=pt[:, 0:n], func=AF.Copy), reads=[B_pt], writes=[B_raw])
                        src_b[0], dst_b[0] = B_raw, B_tm2
                        conv(raw, tm2, ("cw", 4 * j), ("cb", j))
                        P.op('act', 'activation', KW(out=o_t[:, 0:NT], in_=tm2[:, 0:NT], func=AF.Silu), reads=[B_tm2], writes=[B_ot])
                        def tjob(j=j, o_t=o_t, B_ot=B_ot):
                            nchunk = NT // 128
                            for c8 in range(0, nchunk, 8):
                                m = min(8, nchunk - c8)
                                for ci in range(m):
                                    P.op('pe', 'transpose', KW(psT[:, ci * 128:(ci + 1) * 128], o_t[:, (c8 + ci) * 128:(c8 + ci + 1) * 128], identb[:]),
                                         reads=[B_ot, B_idb], writes=[B_psT], inc=(ci == m - 1))
                                xb_, B_xb = xblk[st['xb'] % 2]
                                st['xb'] += 1
                                P.op('dve', 'tensor_copy', KW(out=xb_[:, 0:m * 128], in_=psT[:, 0:m * 128]), reads=[B_psT], writes=[B_xb])
                                if j < XC:
                                    dstap = x_tm[c8 * 128:(c8 + m) * 128, j * 128:(j + 1) * 128].rearrange("(c p) j -> p c j", p=128)
                                    P.dma('sp', dstap, xb_[:, 0:m * 128].rearrange("p (c j) -> p c j", j=128), B_xtm, B_xb)
                                else:
                                    jj = j - XC
                                    dstap = B_tm[c8 * 128:(c8 + m) * 128, jj * 128:(jj + 1) * 128].rearrange("(c p) j -> p c j", p=128)
                                    P.dma('sp', dstap, xb_[:, 0:m * 128].rearrange("p (c j) -> p c j", j=128), B_Btm, B_xb)
                        if (j < XC or (XC <= j < XC + 8)) and 'xt' not in _skip:
                            pend.append(tjob)
                        if XC <= j < XC + 8:
                            jj = j - XC
                            P.dma('sp', pB[jj * 128:(jj + 1) * 128, :], o_t[:, 0:NT], B_pB, B_ot)
                        elif j >= XC + 8:
                            jj = j - XC - 8
                            P.dma('sp', pC[jj * 128:(jj + 1) * 128, :], o_t[:, 0:NT], B_pC, B_ot)
                    elif oc < C.OC_RG:
                        j = oc - C.OC_DT
                        for (pt, B_pt, c0, n) in pts:
                            P.op('act', 'activation', KW(out=raw[:, c0:c0 + n], in_=pt[:, 0:n], func=AF.Exp, bias=V("dtb", j), scale=1.0),
                                 reads=[B_pt, B_vec], writes=[B_raw])
                        P.op('act', 'activation', KW(out=tm2[:, 0:NT], in_=raw[:, 0:NT], func=AF.Ln, bias=1.0, scale=1.0), reads=[B_raw], writes=[B_tm2])
                        P.dma('sp', pdt[j * 128:(j + 1) * 128, :], tm2[:, 0:NT], B_pdt, B_tm2)
                    elif oc < C.OC_RX:
                        st['ob'] += 1
                        j = oc - C.OC_RG
                        for (pt, B_pt, c0, n) in pts:
                            P.op('act', 'activation', KW(out=raw[:, c0:c0 + n], in_=pt[:, 0:n], func=AF.Copy), reads=[B_pt], writes=[B_raw])
                        P.op('dve', 'tensor_tensor', KW(out=tm2[:, 0:SEQ], in0=raw[:, 0:SEQ], in1=raw[:, 0:SEQ], op=ALU.mult), reads=[B_raw], writes=[B_tm2])
                        P.op('dve', 'tensor_scalar', KW(out=tm2[:, 0:SEQ], in0=tm2[:, 0:SEQ], scalar1=0.044715, scalar2=1.0, op0=ALU.mult, op1=ALU.add),
                             reads=[B_tm2], writes=[B_tm2])
                        P.op('dve', 'tensor_tensor', KW(out=tm2[:, 0:SEQ], in0=tm2[:, 0:SEQ], in1=raw[:, 0:SEQ], op=ALU.mult), reads=[B_raw, B_tm2], writes=[B_tm2])
                        P.op('act', 'activation', KW(out=tm2[:, 0:SEQ], in_=tm2[:, 0:SEQ], func=AF.Sigmoid, scale=1.5957691216057308), reads=[B_tm2], writes=[B_tm2])
                        P.op('dve', 'tensor_tensor', KW(out=o_t[:, 0:SEQ], in0=tm2[:, 0:SEQ], in1=raw[:, 0:SEQ], op=ALU.mult), reads=[B_raw, B_tm2], writes=[B_ot])
                        P.dma('sp', prg[j * 128:(j + 1) * 128, :], o_t[:, 0:SEQ], B_prg, B_ot)
                    else:
                        j = oc - C.OC_RX
                        for (pt, B_pt, c0, n) in pts:
                            if c0 < SEQ:
                                r0 = c0 // C.GW
                                nr = n // C.GW
                                P.op('act', 'activation', KW(out=tm2[:, 0:SEQ].rearrange("p (c r) -> p r c", r=C.ROWS)[:, r0:r0 + nr, :],
                                                             in_=pt[:, 0:n].rearrange("p (r c) -> p r c", c=C.GW), func=AF.Copy),
                                     reads=[B_pt], writes=[B_tm2])
                            else:
                                P.op('act', 'activation', KW(out=tm2[:, c0:c0 + n], in_=pt[:, 0:n], func=AF.Copy), reads=[B_pt], writes=[B_tm2])
                        src_b[0], dst_b[0] = B_tm2, B_raw
                        conv(tm2, raw, ("rcw", 4 * j), ("rcb", j))
                        P.dma('sp', prx[j * 128:(j + 1) * 128, :], raw[:, 0:NT], B_prx, B_raw)
                while pend:
                    pend.pop(0)()
                P.barrier()
        if 'p2' in dbg:
            tb, B_tb = sbuf(glob, "dbg_p2b", [128, NT], BF16)
            tf, B_tf = sbuf(glob, "dbg_p2f", [128, NT])
            for (nm, src, B_src, rows, cols, isbf) in (("pz", pz, B_pz, C.DI, SEQ, 1), ("pB", pB, B_pB, C.GN, NT, 1), ("pC", pC, B_pC, C.GN, NT, 1),
                                                      ("pdt", pdt, B_pdt, 256, NT, 0), ("prg", prg, B_prg, RC * 128, SEQ, 1),
                                                      ("prx", prx, B_prx, RC * 128, NT, 0), ("pgt", pgt, B_pgt, 2 * D, SEQ, 1)):
                o = dbgout(nm, [rows, cols])
                B_o = Buf("dbg" + nm)
                for kc in range(rows // 128):
                    if isbf:
                        P.dma('sp', tb[:, 0:cols], src[kc * 128:(kc + 1) * 128, :], B_tb, B_src)
                        P.op('dve', 'tensor_copy', KW(out=tf[:, 0:cols], in_=tb[:, 0:cols]), reads=[B_tb], writes=[B_tf])
                    else:
                        P.dma('sp', tf[:, 0:cols], src[kc * 128:(kc + 1) * 128, :], B_tf, B_src)
                    P.dma('sp', o[kc * 128:(kc + 1) * 128, :], tf[:, 0:cols], B_o, B_tf)
            xb2, B_xb2 = sbuf(glob, "dbg_xtb", [128, C.DI], BF16)
            xf2, B_xf2 = sbuf(glob, "dbg_xtf", [128, C.DI])
            o = dbgout("x_tm", [NT, C.DI])
            B_o = Buf("dbgxtm")
            for c in range(NT // 128):
                P.dma('sp', xb2[:], x_tm[c * 128:(c + 1) * 128, :], B_xb2, B_xtm)
                P.op('dve', 'tensor_copy', KW(out=xf2[:], in_=xb2[:]), reads=[B_xb2], writes=[B_xf2])
                P.dma('sp', o[c * 128:(c + 1) * 128, :], xf2[:], B_o, B_xf2)


        NCH, NCL, HB, NQ = C.NCH, C.NCL, C.HB, C.NQ
        EW = E * 64
        if upto >= 4:
            with contextlib.ExitStack() as ph:
                def tm(name, dt=F32):
                    return sbuf(ph, name, [128, NCH, 128], dt)
                lb_tm = [tm("lb_tm_f"), tm("lb_tm_b")]
                w_tm = [tm("w_tm_f", BF16), tm("w_tm_b", BF16)]
                etot = [tm("etot_f"), tm("etot_b")]
                csS = [[sbuf(ph, "csS_%d_%d" % (d_, i_), [128, NT], BF16) for i_ in range(2)] for d_ in range(2)]
                maskq, B_maskq = sbuf(ph, "maskq", [128, H // HB, HB])
                drw, B_drw = sbuf(ph, "drw", [128, 128])
                pA, B_pA = psum(ph, "p4A", [128, 1024])
                pB_, B_pB_ = psum(ph, "p4B", [128, 1024])
                pY, B_pY = psum(ph, "p4Y", [128, 512])
                pS, B_pS = psum(ph, "p4S", [128, 512])
                pM, B_pM = psum(ph, "p4M", [128, 512])
                maskE, B_maskE = sbuf(ph, "maskE", [128, 8, E])
                P.op('pool', 'memset', KW(maskE[:], 1.0), writes=[B_maskE])
                P.op('pool', 'affine_select', KW(out=maskE[:], in_=maskE[:], pattern=[[-E, 8], [-1, E]], base=0, channel_multiplier=1,
                                                 compare_op=ALU.is_equal, fill=0.0), reads=[B_maskE], writes=[B_maskE])
                P.op('pool', 'memset', KW(maskq[:], 1.0), writes=[B_maskq])
                P.op('pool', 'affine_select', KW(out=maskq[:], in_=maskq[:], pattern=[[-HB, H // HB], [-1, HB]], base=0, channel_multiplier=1,
                                                 compare_op=ALU.is_equal, fill=0.0), reads=[B_maskq], writes=[B_maskq])
                P.dma('sp', drw[:], drow[0:1, :].partition_broadcast(128), B_drw, B_w)
                amask = [sbuf(ph, "amask_f", [128, 128]), sbuf(ph, "amask_b", [128, 128])]
                P.op('pool', 'memset', KW(amask[0][0][:], 0.0), writes=[amask[0][1]])
                P.op('pool', 'affine_select', KW(out=amask[0][0][:], in_=amask[0][0][:], pattern=[[1, 128]], base=0, channel_multiplier=-1,
                                                 compare_op=ALU.is_ge, fill=NEG), reads=[amask[0][1]], writes=[amask[0][1]])
                P.op('pool', 'memset', KW(amask[1][0][:], NEG), writes=[amask[1][1]])
                P.op('pool', 'affine_select', KW(out=amask[1][0][:], in_=amask[1][0][:], pattern=[[1, 128]], base=0, channel_multiplier=-1,
                                                 compare_op=ALU.is_gt, fill=0.0), reads=[amask[1][1]], writes=[amask[1][1]])
                amask8 = [sbuf(ph, "amask8_%d" % d_, [128, HB, 128], BF16) for d_ in range(2)]
                for d_ in range(2):
                    P.op('dve', 'tensor_copy', KW(out=amask8[d_][0][:], in_=amask[d_][0][:].unsqueeze(1).to_broadcast([128, HB, 128])),
                         reads=[amask[d_][1]], writes=[amask8[d_][1]])
                with contextlib.ExitStack() as pp:
                    cs_tm = [sbuf(pp, "cs_tm_f", [128, NCH, 128]), sbuf(pp, "cs_tm_b", [128, NCH, 128])]
                    dt_tm = [sbuf(pp, "dt_tm_f", [128, NCH, 128], BF16), sbuf(pp, "dt_tm_b", [128, NCH, 128], BF16)]
                    dtT = [sbuf(pp, "dtT_f", [128, NT]), sbuf(pp, "dtT_b", [128, NT])]
                    daT = [sbuf(pp, "daT_f", [128, NT]), sbuf(pp, "daT_b", [128, NT])]
                    da_tm = [sbuf(pp, "da_tm_f", [128, NCH, 128]), sbuf(pp, "da_tm_b", [128, NCH, 128])]
                    aneg, B_aneg = sbuf(pp, "aneg", [128, 2])
                    csT = [sbuf(pp, "csT_f", [128, NT]), sbuf(pp, "csT_b", [128, NT])]
                    spl, B_spl = sbuf(pp, "spl", [128, NT])
                    P.op('act', 'activation', KW(out=aneg[:], in_=V("alog", 0, 2), func=AF.Exp), reads=[B_vec], writes=[B_aneg])
                    P.op('dve', 'tensor_scalar_mul', KW(out=aneg[:], in0=aneg[:], scalar1=-1.0), reads=[B_aneg], writes=[B_aneg])
                    for d in range(2):
                        t_, B_t = dtT[d]
                        a_, B_a = daT[d]
                        c_, B_c = csT[d]
                        P.dma('sp', t_[:], pdt[d * 128:(d + 1) * 128, :], B_t, B_pdt)
                        P.op('dve', 'tensor_scalar_mul', KW(out=a_[:], in0=t_[:], scalar1=aneg[:, d:d + 1]), reads=[B_t, B_aneg], writes=[B_a])
                        for c in range(NCH):
                            sl = slice(c * 128, (c + 1) * 128)
                            if d == 0:
                                P.op('dve', 'tensor_tensor_scan', KW(out=c_[:, sl], data0=ones32[:], data1=a_[:, sl], initial=0.0, op0=ALU.mult, op1=ALU.add),
                                     reads=[B_a, B_ones32], writes=[B_c])
                            else:
                                P.op('dve', 'tensor_tensor_scan', KW(out=c_[:, sl][:, ::-1], data0=ones32[:], data1=a_[:, sl][:, ::-1], initial=0.0,
                                                                     op0=ALU.mult, op1=ALU.add), reads=[B_a, B_ones32], writes=[B_c])
                        (h0_, B_h0), (h1_, B_h1) = csS[d]
                        P.op('act', 'activation', KW(out=h0_[:], in_=c_[:], func=AF.Copy), reads=[B_c], writes=[B_h0])
                        P.op('dve', 'tensor_tensor', KW(out=spl[:], in0=c_[:], in1=h0_[:], op=ALU.subtract), reads=[B_c, B_h0], writes=[B_spl])
                        P.op('act', 'activation', KW(out=h1_[:], in_=spl[:], func=AF.Copy), reads=[B_spl], writes=[B_h1])
                        for c in range(NCH):
                            sl = slice(c * 128, (c + 1) * 128)
                            for (src, B_src, (dst, B_dst)) in ((c_, B_c, cs_tm[d]), (t_, B_t, dt_tm[d]), (a_, B_a, da_tm[d])):
                                P.op('pe', 'transpose', KW(pM[:, 0:128], src[:, sl], ident32[:]), reads=[B_src, B_id32], writes=[B_pM])
                                P.op('act', 'activation', KW(out=dst[:, c, :], in_=pM[:, 0:128], func=AF.Copy), reads=[B_pM], writes=[B_dst])
                            P.op('pe', 'matmul', KW(pM[:, 128:256], lhsT=ones32[:], rhs=da_tm[d][0][:, c, :], start=True, stop=True),
                                 reads=[da_tm[d][1], B_ones32], writes=[B_pM])
                            e_, B_e = etot[d]
                            w_, B_w_ = w_tm[d]
                            P.op('act', 'activation', KW(out=e_[:, c, :], in_=pM[:, 128:256], func=AF.Exp), reads=[B_pM], writes=[B_e])
                            tw = spl[:, 0:128]
                            P.op('dve', 'tensor_tensor', KW(out=tw, in0=pM[:, 128:256], in1=cs_tm[d][0][:, c, :], op=ALU.subtract),
                                 reads=[B_pM, cs_tm[d][1]], writes=[B_spl])
                            P.op('act', 'activation', KW(out=tw, in_=tw, func=AF.Exp), reads=[B_spl], writes=[B_spl])
                            P.op('dve', 'tensor_tensor', KW(out=w_[:, c, :], in0=tw, in1=dt_tm[d][0][:, c, :], op=ALU.mult),
                                 reads=[B_spl, dt_tm[d][1]], writes=[B_w_])
                            lb_, B_lb = lb_tm[d]
                            P.op('act', 'activation', KW(out=lb_[:, c, :], in_=dt_tm[d][0][:, c, :], func=AF.Ln), reads=[dt_tm[d][1]], writes=[B_lb])
                            P.op('dve', 'tensor_tensor', KW(out=lb_[:, c, :], in0=lb_[:, c, :], in1=cs_tm[d][0][:, c, :], op=ALU.subtract),
                                 reads=[B_lb, cs_tm[d][1]], writes=[B_lb])
                    P.barrier()

                BTg, B_BTg = sbuf(ph, "BTg", [128, NT], BF16)
                CTg, B_CTg = sbuf(ph, "CTg", [128, NT], BF16)
                xtm = [sbuf(ph, "xtm%d" % i, [128, EW], BF16) for i in range(2)]
                btm = [sbuf(ph, "btm%d" % i, [128, 128], BF16) for i in range(2)]
                szc = [sbuf(ph, "szc%d" % i, [128, E // 2, 128], BF16) for i in range(2)]
                sbi = [sbuf(ph, "sbi%d" % i, [128, EW], BF16) for i in range(2)]
                xs, B_xs = sbuf(ph, "xs", [128, EW], BF16)
                S32 = [sbuf(ph, "S32f", [128, EW]), sbuf(ph, "S32b", [128, EW])]
                Sfb, B_Sfb = sbuf(ph, "Sfb", [128, EW], BF16)
                Sbb = [sbuf(ph, "Sbb%d" % i, [128, EW], BF16) for i in range(2)]
                DIg, B_DIg = sbuf(ph, "DIg", [128, E, 128], BF16)
                selg, B_selg = sbuf(ph, "selg", [128, E, 128], BF16)
                CBt, B_CBt = sbuf(ph, "CBt", [128, 128])
                Dd = [[sbuf(ph, "Dd%d_%d" % (p_, i), [128, HB, 128]) for i in range(2)] for p_ in range(2)]
                Xd = [[sbuf(ph, "Xd%d_%d" % (p_, i), [128, HB, 128], BF16) for i in range(2)] for p_ in range(2)]
                Mt = [sbuf(ph, "Mt%d" % p_, [128, HB, 128], BF16) for p_ in range(2)]
                Csd = [[sbuf(ph, "Cs%d_%d" % (p_, i), [128, HB, 128], BF16) for i in range(2)] for p_ in range(2)]
                yg, B_yg = sbuf(ph, "yg", [128, E // 2, 128])
                ysq, B_ysq = sbuf(ph, "ysq", [128, E // 2, 128], BF16)
                ynb, B_ynb = sbuf(ph, "ynb", [128, E // 2, 128], BF16)
                rs4, B_rs4 = sbuf(ph, "rs4", [128, 128])
                cnt4 = {'ld': 0, 'sb': 0}

                def bc_l(ap2):
                    return ap2.unsqueeze(2).to_broadcast([128, ap2.shape[1], 128])

                def bc_h(ap2, n):
                    return ap2.unsqueeze(1).to_broadcast([128, n, 128])

                def load_chunk(g, c):
                    i = cnt4['ld'] % 2
                    cnt4['ld'] += 1
                    x_, B_x = xtm[i]
                    b_, B_b = btm[i]
                    P.dma('sp', x_[:], x_tm[c * 128:(c + 1) * 128, g * EW:(g + 1) * EW], B_x, B_xtm)
                    P.dma('sp', b_[:], B_tm[c * 128:(c + 1) * 128, g * 128:(g + 1) * 128], B_b, B_Btm)
                    return x_, B_x, b_, B_b

                def state_update(g, d, c, x_, B_x, b_, B_b):
                    S, B_S = S32[d]
                    hs = slice(g * E, (g + 1) * E)
                    P.op('dve', 'tensor_tensor', KW(out=xs[:].rearrange("p (h q) -> p h q", q=64), in0=x_[:].rearrange("p (h q) -> p h q", q=64),
                                                    in1=w_tm[d][0][:, c, hs].unsqueeze(2).to_broadcast([128, E, 64]), op=ALU.mult),
                         reads=[B_x, w_tm[d][1]], writes=[B_xs])
                    for c0 in range(0, EW, 512):
                        n = min(512, EW - c0)
                        P.op('pe', 'matmul', KW(pS[:, 0:n], lhsT=b_[:], rhs=xs[:, c0:c0 + n], start=True, stop=True), reads=[B_b, B_xs], writes=[B_pS])
                        nh = n // 64
                        h0 = g * E + c0 // 64
                        P.op('dve', 'tensor_tensor', KW(out=S[:, c0:c0 + n].rearrange("p (h q) -> p h q", q=64),
                                                        in0=S[:, c0:c0 + n].rearrange("p (h q) -> p h q", q=64),
                                                        in1=etot[d][0][:, c, h0:h0 + nh].unsqueeze(2).to_broadcast([128, nh, 64]), op=ALU.mult),
                             reads=[B_S, etot[d][1]], writes=[B_S])
                        P.op('dve', 'tensor_tensor', KW(out=S[:, c0:c0 + n], in0=S[:, c0:c0 + n], in1=pS[:, 0:n], op=ALU.add), reads=[B_S, B_pS], writes=[B_S])

                pre_jobs = []
                if PRECAST:
                    def _pj(out_ap, in_ap, key):
                        pre_jobs.append(lambda: P.dma('pool', out_ap, in_ap, B_cache[key], B_w, max_dma_last_dim=4096))
                    for oc in range(KD):
                        _pj(wsoc[oc], wso_t[oc], "wsoc")
                        _pj(wroc[oc], wro_t[oc], "wroc")
                        _pj(woc[oc], wo_t[oc], "woc")
                    for fc in range(KF):
                        _pj(wupc[1][fc][:, 0:KD * 128], wup_t[1][fc], "wupc1")
                        _pj(wupc[1][fc][:, KD * 128:2 * KD * 128], wup_t[1][KF + fc], "wupc1")
                    for oc in range(KD):
                        _pj(wdnc[1][oc], wdn_t[1][oc], "wdnc1")
                pre_per = -(-len(pre_jobs) // (8 * NCL * NQ)) if pre_jobs else 0
                _p4stop = _os.environ.get('P4STOP', '')
                _pe4 = _os.environ.get('P4POOL', 'pool')
                _p4d = _os.environ.get('P4D', '').split(',')
                _L = int(_os.environ.get('P4L', '99'))
                for g in range(8 if _p4stop != 'prep' else 0):
                    P.dma('sp', BTg[:], pB[g * 128:(g + 1) * 128, :], B_BTg, B_pB)
                    P.dma('sp', CTg[:], pC[g * 128:(g + 1) * 128, :], B_CTg, B_pC)
                    P.op('dve', 'tensor_tensor', KW(out=DIg[:], in0=bc_h(ident32[:], E), in1=bc_l(drw[:, g * E:(g + 1) * E]), op=ALU.mult),
                         reads=[B_id32, B_drw], writes=[B_DIg])
                    P.op('dve', 'tensor_copy', KW(out=selg[:], in_=bc_l(maskE[:, g, :])), reads=[B_maskE], writes=[B_selg])
                    for d in range(2):
                        P.op('dve', 'memset', KW(S32[d][0][:], 0.0), writes=[S32[d][1]])
                    for c in range(NCL, NCH):
                        state_update(g, 0, c, *load_chunk(g, c))
                    for c in range(NCH - 1, NCL - 1, -1):
                        state_update(g, 1, c, *load_chunk(g, c))
                    for c in range(NCL - 1, -1, -1) if _p4stop != 'ctx' else []:
                        sb_, B_sb = Sbb[cnt4['sb'] % 2]
                        cnt4['sb'] += 1
                        P.op('act', 'activation', KW(out=sb_[:], in_=S32[1][0][:], func=AF.Copy), reads=[S32[1][1]], writes=[B_sb])
                        P.dma('sp', sbin[g, c], sb_[:], B_sbin, B_sb)
                        state_update(g, 1, c, *load_chunk(g, c))
                    def stage_A(it):
                        c, q = divmod(it, NQ)
                        par = it % 2
                        sl = slice(c * 128, (c + 1) * 128)
                        Q = g * NQ + q
                        hs = slice(Q * HB, (Q + 1) * HB)
                        pps = ((pA, B_pA), (pB_, B_pB_))
                        for d in range(2):
                            pp_, B_pp = pps[d]
                            for hh in range(HB):
                                for i3 in range(2):
                                    P.op('pe', 'matmul', KW(pp_[:, hh * 128:(hh + 1) * 128], lhsT=selg[:, q * HB + hh, :], rhs=csS[d][i3][0][:, sl],
                                                            start=(i3 == 0 and hh % 4 == 0), stop=(i3 == 1), skip_group_check=True),
                                         reads=[B_selg, csS[d][i3][1]], writes=[B_pp], inc=(hh == HB - 1 and i3 == 1))
                        ppv = [pps[d][0][:, 0:HB * 128].rearrange("p (h l) -> p h l", l=128) for d in range(2)]
                        for d in range(2):
                            P.op('act', 'activation', KW(out=Xd[par][d][0][:], in_=ppv[d], func=AF.Exp), reads=[pps[d][1]], writes=[Xd[par][d][1]])
                        for d in range(2):
                            pp_, B_pp = pps[d]
                            af = amask8[d][0][:].rearrange("p h l -> p (h l)")
                            for c0 in range(0, HB * 128, 512):
                                n = min(512, HB * 128 - c0)
                                P.op('pe', 'matmul', KW(pp_[:, c0:c0 + n], lhsT=identb[:], rhs=af[:, c0:c0 + n], start=False, stop=True, skip_group_check=True),
                                     reads=[B_idb, amask8[d][1]], writes=[B_pp])
                        for hh in range(HB):
                            for d in range(2):
                                hcol = Q * HB + hh
                                P.op('act', 'activation', KW(out=Dd[par][d][0][:, hh, :], in_=pps[d][0][:, hh * 128:(hh + 1) * 128], func=AF.Exp,
                                                             bias=lb_tm[d][0][:, c, hcol:hcol + 1], scale=1.0),
                                     reads=[pps[d][1], lb_tm[d][1]], writes=[Dd[par][d][1]])
                        for d in range(2):
                            P.op('pool', 'tensor_tensor', KW(out=Csd[par][d][0][:], in0=Xd[par][d][0][:], in1=bc_h(CTg[:, sl], HB), op=ALU.mult),
                                 reads=[Xd[par][d][1], B_CTg], writes=[Csd[par][d][1]])
                        for _ in range(pre_per):
                            if pre_jobs:
                                pre_jobs.pop(0)()

                    cur = {}

                    def stage_B(it):
                        c, q = divmod(it, NQ)
                        par = it % 2
                        sl = slice(c * 128, (c + 1) * 128)
                        if q == 0:
                            x_, B_x, b_, B_b = load_chunk(g, c)
                            i = c % 2
                            sz_, B_sz = szc[i]
                            si_, B_si = sbi[i]
                            P.dma('sp', sz_[:], pz[g * (E // 2) * 128:(g + 1) * (E // 2) * 128, sl].rearrange("(j p) t -> p j t", p=128), B_sz, B_pz)
                            P.dma('sp', si_[:], sbin[g, c], B_si, B_sbin)
                            P.op('act', 'activation', KW(out=Sfb[:], in_=S32[0][0][:], func=AF.Copy), reads=[S32[0][1]], writes=[B_Sfb])
                            P.op('pe', 'matmul', KW(pM[:, 0:128], lhsT=BTg[:, sl], rhs=CTg[:, sl], start=True, stop=True), reads=[B_BTg, B_CTg], writes=[B_pM])
                            P.op('act', 'activation', KW(out=CBt[:], in_=pM[:, 0:128], func=AF.Copy), reads=[B_pM], writes=[B_CBt])
                            cur['v'] = (x_, B_x, b_, B_b, sz_, B_sz, si_, B_si)
                        x_, B_x, b_, B_b, sz_, B_sz, si_, B_si = cur['v']
                        D0, B_D0 = Dd[par][0]
                        D1, B_D1 = Dd[par][1]
                        M_, B_M = Mt[par]
                        P.op('dve', 'tensor_tensor', KW(out=D0[:], in0=D0[:], in1=D1[:], op=ALU.add), reads=[B_D0, B_D1], writes=[B_D0])
                        P.op('dve', 'tensor_tensor', KW(out=M_[:], in0=D0[:], in1=bc_h(CBt[:], HB), op=ALU.mult), reads=[B_D0, B_CBt], writes=[B_M])
                        for hh in range(HB):
                            j, e = hh // 2, hh % 2
                            col = (q * HB + hh) * 64
                            o_ap = pY[64 * e:64 * e + 64, j * 128:(j + 1) * 128]
                            P.op('pe', 'matmul', KW(o_ap, lhsT=x_[:, col:col + 64], rhs=M_[:, hh, :], start=True, stop=False),
                                 reads=[B_x, B_M], writes=[B_pY], inc=False)
                            P.op('pe', 'matmul', KW(o_ap, lhsT=x_[:, col:col + 64], rhs=DIg[:, q * HB + hh, :], start=False, stop=False),
                                 reads=[B_x, B_DIg], writes=[B_pY], inc=False)
                            P.op('pe', 'matmul', KW(o_ap, lhsT=Sfb[:, col:col + 64], rhs=Csd[par][0][0][:, hh, :], start=False, stop=False),
                                 reads=[B_Sfb, Csd[par][0][1]], writes=[B_pY], inc=False)
                            P.op('pe', 'matmul', KW(o_ap, lhsT=si_[:, col:col + 64], rhs=Csd[par][1][0][:, hh, :], start=False, stop=True),
                                 reads=[B_si, Csd[par][1][1]], writes=[B_pY], inc=(hh == HB - 1))
                        j0 = q * (HB // 2)
                        P.op('dve', 'tensor_tensor', KW(out=yg[:, j0:j0 + HB // 2, :], in0=pY[:, 0:(HB // 2) * 128].rearrange("p (j l) -> p j l", l=128),
                                                        in1=sz_[:, j0:j0 + HB // 2, :], op=ALU.mult), reads=[B_pY, B_sz], writes=[B_yg])
                        if q == NQ - 1:
                            P.op('act', 'activation', KW(out=ysq[:], in_=yg[:], func=AF.Square), reads=[B_yg], writes=[B_ysq])
                            for j in range(E // 2):
                                P.op('pe', 'matmul', KW(pM[:, 256:384], lhsT=ones_bf[:], rhs=ysq[:, j, :], start=(j == 0), stop=(j == E // 2 - 1)),
                                     reads=[B_ysq, B_ones], writes=[B_pM], inc=(j == E // 2 - 1))
                            P.op('act', 'activation', KW(out=rs4[:], in_=pM[:, 256:384], func=AF.Sqrt, scale=1.0 / EW, bias=EPS), reads=[B_pM], writes=[B_rs4])
                            P.op('dve', 'reciprocal', KW(out=rs4[:], in_=rs4[:]), reads=[B_rs4], writes=[B_rs4])
                            for j in range(E // 2):
                                P.op('dve', 'scalar_tensor_tensor', KW(out=ynb[:, j, :], in0=yg[:, j, :], scalar=V("sng", g * (E // 2) + j), in1=rs4[:],
                                                                       op0=ALU.mult, op1=ALU.mult), reads=[B_yg, B_vec, B_rs4], writes=[B_ynb])
                            P.dma('sp', ynT[g * (E // 2) * 128:(g + 1) * (E // 2) * 128, sl].rearrange("(j p) t -> p j t", p=128), ynb[:], B_ynT, B_ynb)
                            state_update(g, 0, c, x_, B_x, b_, B_b)

                    NIT = NCL * NQ
                    stage_A(0)
                    if 'p4dbg' in dbg and g == 0:
                        dA, B_dA = sbuf(ph, "dbgsb_pA", [128, 1024])
                        P.op('dve', 'tensor_copy', KW(out=dA[:, 0:512], in_=pA[:, 0:512]), reads=[B_pA], writes=[B_dA])
                        P.op('dve', 'tensor_copy', KW(out=dA[:, 512:1024], in_=Dd[0][0][0][:].rearrange("p h l -> p (h l)")[:, 0:512]), reads=[Dd[0][0][1]], writes=[B_dA])
                        o = dbgout("pA", [128, 1024])
                        P.dma('sp', o[:, :], dA[:], Buf("dbgpA"), B_dA)
                    for it in range(NIT):
                        if it + 1 < NIT:
                            stage_A(it + 1)
                        stage_B(it)
                while pre_jobs:
                    pre_jobs.pop(0)()
                P.barrier()
        if 'yn' in dbg:
            tb, B_tb = sbuf(glob, "dbg_ynb", [128, SEQ], BF16)
            tf, B_tf = sbuf(glob, "dbg_ynf", [128, SEQ])
            o = dbgout("yn", [C.DI, SEQ])
            B_o = Buf("dbgyn")
            for kc in range(XC):
                P.dma('sp', tb[:], ynT[kc * 128:(kc + 1) * 128, :], B_tb, B_ynT)
                P.op('dve', 'tensor_copy', KW(out=tf[:], in_=tb[:]), reads=[B_tb], writes=[B_tf])
                P.dma('sp', o[kc * 128:(kc + 1) * 128, :], tf[:], B_o, B_tf)


        NBC = C.NBC
        if upto >= 5:
            with contextlib.ExitStack() as ph:
                xr, B_xr = sbuf(ph, "xr", [128, NBC, NT])
                xrb, B_xrb = sbuf(ph, "xrb", [128, NBC, NT], BF16)
                rrow, B_rrow = sbuf(ph, "rrow", [128, NT])
                irow, B_irow = sbuf(ph, "irow", [128, NT])
                arow, B_arow = sbuf(ph, "arow", [128, NT])
                brow, B_brow = sbuf(ph, "brow", [128, NT])
                hrow = [sbuf(ph, "hrow%d" % i, [128, NT]) for i in range(2)]
                gate, B_gate = sbuf(ph, "gate", [128, SEQ], BF16)
                orow, B_orow = sbuf(ph, "orow", [128, SEQ], BF16)
                wg = [sbuf(ph, "rgwt%d" % i, [128, NBC * 128], BF16) for i in range(4)]
                cc1, B_cc1 = sbuf(ph, "cc1", [128, 2 * RC])
                cc2, B_cc2 = sbuf(ph, "cc2", [128, 2 * RC])
                nba, B_nba = sbuf(ph, "nba", [128, 2 * RC])
                nbx, B_nbx = sbuf(ph, "nbx", [128, 2 * RC])
                pss = [psum(ph, "p5ps%d" % i, [128, 512]) for i in range(8)]
                st5 = {'ps': 0, 'w': 0}
                P.op('act', 'activation', KW(out=cc1[:], in_=V("rlam", 0, 2 * RC), func=AF.Exp, scale=-1.0), reads=[B_vec], writes=[B_cc1])
                P.op('act', 'activation', KW(out=cc1[:], in_=cc1[:], func=AF.Ln, bias=1.0, scale=1.0), reads=[B_cc1], writes=[B_cc1])
                P.op('dve', 'tensor_scalar_mul', KW(out=cc2[:], in0=cc1[:], scalar1=-16.0), reads=[B_cc1], writes=[B_cc2])
                P.op('dve', 'tensor_scalar_mul', KW(out=cc1[:], in0=cc1[:], scalar1=-8.0), reads=[B_cc1], writes=[B_cc1])
                P.op('dve', 'tensor_scalar_mul', KW(out=nba[:], in0=V("rba", 0, 2 * RC), scalar1=-1.0), reads=[B_vec], writes=[B_nba])
                P.op('dve', 'tensor_scalar_mul', KW(out=nbx[:], in0=V("rbx", 0, 2 * RC), scalar1=-1.0), reads=[B_vec], writes=[B_nbx])
                tsl = slices_of(SEQ, 512) + [(SEQ + c0, n) for (c0, n) in slices_of(CTX, 512)]
                ada_todo = list(range(NADA0, 9 * KD))
                ada_per = -(-len(ada_todo) // (16 * NBC)) if ada_todo else 0
                if ada_todo:
                    wada5 = [sbuf(ph, "wada5_%d" % i, [128, KD * 128], BF16) for i in range(3)]
                for k in range(16):
                    for ic in range(NBC):
                        P.dma('sp', xr[:, ic, :], prx[(k * NBC + ic) * 128:(k * NBC + ic + 1) * 128, :], B_xr, B_prx)
                    P.op('act', 'activation', KW(out=xrb[:].rearrange("p a t -> p (a t)"), in_=xr[:].rearrange("p a t -> p (a t)"), func=AF.Copy),
                         reads=[B_xr], writes=[B_xrb])
                    for jc in range(NBC):
                        ch = k * NBC + jc
                        P.dma('sp', gate[:], prg[ch * 128:(ch + 1) * 128, :], B_gate, B_prg)
                        for d in range(2):
                            col = d * RC + ch
                            wa, B_wa = wg[st5['w'] % 4]
                            wx, B_wx = wg[(st5['w'] + 1) % 4]
                            st5['w'] += 2
                            P.dma('pool', wa[:], rgw_t[((d * 2 + 0) * 16 + k) * NBC + jc], B_wa, B_w, max_dma_last_dim=4096)
                            P.dma('pool', wx[:], rgw_t[((d * 2 + 1) * 16 + k) * NBC + jc], B_wx, B_w, max_dma_last_dim=4096)
                            for (c0, n) in tsl:
                                pa, B_pa = pss[st5['ps'] % 8]
                                px_, B_px = pss[(st5['ps'] + 1) % 8]
                                st5['ps'] += 2
                                for (w_, B_w_, p_, B_p) in ((wa, B_wa, pa, B_pa), (wx, B_wx, px_, B_px)):
                                    for ic in range(NBC):
                                        P.op('pe', 'matmul', KW(p_[:, 0:n], lhsT=w_[:, ic * 128:(ic + 1) * 128], rhs=xrb[:, ic, c0:c0 + n],
                                                                start=(ic == 0), stop=(ic == NBC - 1)), reads=[B_w_, B_xrb], writes=[B_p], inc=(ic == NBC - 1))
                                P.op('act', 'activation', KW(out=rrow[:, c0:c0 + n], in_=pa[:, 0:n], func=AF.Sigmoid, bias=V("rba", col)),
                                     reads=[B_pa, B_vec], writes=[B_rrow])
                                P.op('act', 'activation', KW(out=irow[:, c0:c0 + n], in_=px_[:, 0:n], func=AF.Sigmoid, bias=V("rbx", col)),
                                     reads=[B_px, B_vec], writes=[B_irow])
                            P.op('act', 'activation', KW(out=arow[:], in_=rrow[:], func=AF.Exp, scale=cc1[:, col:col + 1]), reads=[B_rrow, B_cc1], writes=[B_arow])
                            P.op('act', 'activation', KW(out=brow[:], in_=rrow[:], func=AF.Exp, scale=cc2[:, col:col + 1]), reads=[B_rrow, B_cc2], writes=[B_brow])
                            P.op('act', 'activation', KW(out=brow[:], in_=brow[:], func=AF.Sqrt, scale=-1.0, bias=1.0), reads=[B_brow], writes=[B_brow])
                            P.op('dve', 'tensor_tensor', KW(out=brow[:], in0=brow[:], in1=irow[:], op=ALU.mult), reads=[B_brow, B_irow], writes=[B_brow])
                            P.op('dve', 'tensor_tensor', KW(out=brow[:], in0=brow[:], in1=xr[:, jc, :], op=ALU.mult), reads=[B_brow, B_xr], writes=[B_brow])
                            h_, B_h = hrow[d]
                            if d == 0:
                                P.op('dve', 'tensor_tensor_scan', KW(out=h_[:, SEQ:NT], data0=arow[:, SEQ:NT], data1=brow[:, SEQ:NT], initial=0.0,
                                                                     op0=ALU.mult, op1=ALU.add), reads=[B_arow, B_brow], writes=[B_h])
                                P.op('dve', 'tensor_tensor_scan', KW(out=h_[:, 0:SEQ], data0=arow[:, 0:SEQ], data1=brow[:, 0:SEQ], initial=h_[:, NT - 1:NT],
                                                                     op0=ALU.mult, op1=ALU.add), reads=[B_arow, B_brow, B_h], writes=[B_h])
                            else:
                                P.op('dve', 'tensor_tensor_scan', KW(out=h_[:, SEQ:NT][:, ::-1], data0=arow[:, SEQ:NT][:, ::-1], data1=brow[:, SEQ:NT][:, ::-1],
                                                                     initial=0.0, op0=ALU.mult, op1=ALU.add), reads=[B_arow, B_brow], writes=[B_h])
                                P.op('dve', 'tensor_tensor_scan', KW(out=h_[:, 0:SEQ][:, ::-1], data0=arow[:, 0:SEQ][:, ::-1], data1=brow[:, 0:SEQ][:, ::-1],
                                                                     initial=h_[:, SEQ:SEQ + 1], op0=ALU.mult, op1=ALU.add), reads=[B_arow, B_brow, B_h], writes=[B_h])
                        h0, B_h0 = hrow[0]
                        h1_, B_h1_ = hrow[1]
                        P.op('dve', 'tensor_tensor', KW(out=h0[:, 0:SEQ], in0=h0[:, 0:SEQ], in1=h1_[:, 0:SEQ], op=ALU.add), reads=[B_h0, B_h1_], writes=[B_h0])
                        P.op('dve', 'tensor_tensor', KW(out=orow[:].rearrange("p (r c) -> p r c", c=C.GW),
                                                        in0=h0[:, 0:SEQ].rearrange("p (c r) -> p r c", r=C.ROWS),
                                                        in1=gate[:].rearrange("p (r c) -> p r c", c=C.GW), op=ALU.mult),
                             reads=[B_h0, B_gate], writes=[B_orow])
                        P.dma('sp', rgyT[ch * 128:(ch + 1) * 128, :], orow[:], B_rgyT, B_orow)
                        for _ in range(ada_per):
                            if ada_todo:
                                oc_ = ada_todo.pop(0)
                                wt_, B_wt_ = wada5[oc_ % 3]
                                pt_, B_pt_ = pss[st5['ps'] % 8]
                                st5['ps'] += 1
                                ada_chunk(oc_, wt_, B_wt_, pt_, B_pt_)
                if NADA0 < 9 * KD:
                    cf_tables(1, 'c')
                    cf_tables(2, 'sc')
                P.barrier()
        if 'rgy' in dbg:
            tb, B_tb = sbuf(glob, "dbg_rgb", [128, SEQ], BF16)
            tf, B_tf = sbuf(glob, "dbg_rgf", [128, SEQ])
            o = dbgout("rgy", [RC * 128, SEQ])
            B_o = Buf("dbgrgy")
            for kc in range(RC):
                P.dma('sp', tb[:], rgyT[kc * 128:(kc + 1) * 128, :], B_tb, B_rgyT)
                P.op('dve', 'tensor_copy', KW(out=tf[:], in_=tb[:]), reads=[B_tb], writes=[B_tf])
                P.dma('sp', o[kc * 128:(kc + 1) * 128, :], tf[:], B_o, B_tf)


        T6 = 512
        if upto >= 6:
            with contextlib.ExitStack() as ph:
                NA = max(XC, RC)
                bufA, B_A = sbuf(ph, "p6A", [128, NA * T6], BF16)
                mix, B_mix = sbuf(ph, "p6mix", [128, KD, T6], BF16)
                mT, B_mT = sbuf(ph, "p6mT", [128, KD, T6], BF16)
                WS = max(XC, RC, KD) * 128
                wsl = [sbuf(ph, "p6w%d" % i, [128, WS], BF16) for i in range(2)]
                gch = [sbuf(ph, "p6g%d" % i, [128, T6], BF16) for i in range(2)]
                tmp = [sbuf(ph, "p6t%d" % i, [128, T6]) for i in range(2)]
                sq = [sbuf(ph, "p6sq%d" % i, [128, T6], BF16) for i in range(2)]
                hch = [sbuf(ph, "p6h%d" % i, [128, T6]) for i in range(3)]
                rstd, B_rstd = sbuf(ph, "p6rstd", [128, T6])
                pss = [psum(ph, "p6ps%d" % i, [128, 512]) for i in range(8)]
                st6 = {'ps': 0, 'w': 0, 'g': 0}
                A3 = bufA[:].rearrange("p (k t) -> p k t", t=T6)
                h2b = bufA[:].bitcast(F32).rearrange("p (k t) -> p k t", t=T6)

                def nps():
                    r = pss[st6['ps'] % 8]
                    st6['ps'] += 1
                    return r

                def nw():
                    r = wsl[st6['w'] % 2]
                    st6['w'] += 1
                    return r

                def ng_():
                    r = gch[st6['g'] % 2]
                    st6['g'] += 1
                    return r
                B_c6 = B_cache
                co, B_co = cf["co_1"]
                s1, B_s1 = cf["s1_2"]
                s2, B_s2 = cf["s2_2"]
                for t0 in range(0, SEQ, T6):
                    T = min(T6, SEQ - t0)
                    P.dma('sp', A3[:, 0:XC, 0:T], ynT[:, t0:t0 + T].rearrange("(k p) t -> p k t", p=128), B_A, B_ynT)
                    for dc in range(KD):
                        wt, B_wt = nw()
                        if (t0 == 0 or not CACHE_W) and not (PRECAST and upto >= 4):
                            P.dma('pool', wt[:, 0:XC * 128], wso_t[dc], B_wt, B_w, max_dma_last_dim=4096)
                            if CACHE_W and SEQ > T6:
                                P.dma('sp', wsoc[dc], wt[:, 0:XC * 128], B_c6["wsoc"], B_wt)
                        else:
                            P.dma('pool', wt[:, 0:XC * 128], wsoc[dc], B_wt, B_c6["wsoc"])
                        g_, B_g = ng_()
                        P.dma('sp', g_[:, 0:T], pgt[dc * 128:(dc + 1) * 128, t0:t0 + T], B_g, B_pgt)
                        pt, B_pt = nps()
                        for kc in range(XC):
                            P.op('pe', 'matmul', KW(pt[:, 0:T], lhsT=wt[:, kc * 128:(kc + 1) * 128], rhs=A3[:, kc, 0:T], start=(kc == 0), stop=(kc == XC - 1)),
                                 reads=[B_wt, B_A], writes=[B_pt], inc=(kc == XC - 1))
                        P.op('dve', 'tensor_tensor', KW(out=mix[:, dc, 0:T], in0=pt[:, 0:T], in1=g_[:, 0:T], op=ALU.mult), reads=[B_pt, B_g], writes=[B_mix])
                    P.dma('sp', A3[:, 0:RC, 0:T], rgyT[:, t0:t0 + T].rearrange("(k p) t -> p k t", p=128), B_A, B_rgyT)
                    for dc in range(KD):
                        wt, B_wt = nw()
                        if (t0 == 0 or not CACHE_W) and not (PRECAST and upto >= 4):
                            P.dma('pool', wt[:, 0:RC * 128], wro_t[dc], B_wt, B_w, max_dma_last_dim=4096)
                            if CACHE_W and SEQ > T6:
                                P.dma('sp', wroc[dc], wt[:, 0:RC * 128], B_c6["wroc"], B_wt)
                        else:
                            P.dma('pool', wt[:, 0:RC * 128], wroc[dc], B_wt, B_c6["wroc"])
                        g_, B_g = ng_()
                        P.dma('sp', g_[:, 0:T], pgt[(KD + dc) * 128:(KD + dc + 1) * 128, t0:t0 + T], B_g, B_pgt)
                        pt, B_pt = nps()
                        for kc in range(RC):
                            P.op('pe', 'matmul', KW(pt[:, 0:T], lhsT=wt[:, kc * 128:(kc + 1) * 128], rhs=A3[:, kc, 0:T], start=(kc == 0), stop=(kc == RC - 1)),
                                 reads=[B_wt, B_A], writes=[B_pt], inc=(kc == RC - 1))
                        t_, B_t = tmp[dc % 2]
                        P.op('dve', 'tensor_tensor', KW(out=t_[:, 0:T], in0=pt[:, 0:T], in1=g_[:, 0:T], op=ALU.mult), reads=[B_pt, B_g], writes=[B_t])
                        P.op('pool', 'tensor_tensor', KW(out=mix[:, dc, 0:T], in0=mix[:, dc, 0:T], in1=t_[:, 0:T], op=ALU.add), reads=[B_mix, B_t], writes=[B_mix])
                    for dc in range(KD):
                        wt, B_wt = nw()
                        if (t0 == 0 or not CACHE_W) and not (PRECAST and upto >= 4):
                            P.dma('pool', wt[:, 0:KD * 128], wo_t[dc], B_wt, B_w, max_dma_last_dim=4096)
                            if CACHE_W and SEQ > T6:
                                P.dma('sp', woc[dc], wt[:, 0:KD * 128], B_c6["woc"], B_wt)
                        else:
                            P.dma('pool', wt[:, 0:KD * 128], woc[dc], B_wt, B_c6["woc"])
                        pt, B_pt = nps()
                        for kc in range(KD):
                            P.op('pe', 'matmul', KW(pt[:, 0:T], lhsT=wt[:, kc * 128:(kc + 1) * 128], rhs=mix[:, kc, 0:T], start=(kc == 0), stop=(kc == KD - 1)),
                                 reads=[B_wt, B_mix], writes=[B_pt], inc=(kc == KD - 1))
                        P.op('act', 'activation', KW(out=mT[:, dc, 0:T], in_=pt[:, 0:T], func=AF.Copy), reads=[B_pt], writes=[B_mT])
                    rms_rstd(nps, sq, rstd, B_rstd, lambda kc: mT[:, kc, 0:T], [B_mT], KD, D, T)
                    for kc in range(KD):
                        hc_, B_hc = hch[kc % 3]
                        P.dma('sp', hc_[:, 0:T], h1T[kc * 128:(kc + 1) * 128, t0:t0 + T], B_hc, B_h1T)
                        t_, B_t = tmp[kc % 2]
                        P.op('dve', 'tensor_tensor', KW(out=t_[:, 0:T], in0=mT[:, kc, 0:T], in1=rstd[:, 0:T], op=ALU.mult), reads=[B_mT, B_rstd], writes=[B_t])
                        P.op('dve', 'scalar_tensor_tensor', KW(out=h2b[:, kc, 0:T], in0=t_[:, 0:T], scalar=co[:, kc, 0:1], in1=hc_[:, 0:T],
                                                               op0=ALU.mult, op1=ALU.add), reads=[B_t, B_co, B_hc], writes=[B_A])
                    P.dma('sp', h2T[:, t0:t0 + T].rearrange("(k p) t -> p k t", p=128), h2b[:, 0:KD, 0:T], B_h2T, B_A)
                    rms_rstd(nps, sq, rstd, B_rstd, lambda kc: h2b[:, kc, 0:T], [B_A], KD, D, T)
                    for kc in range(KD):
                        t_, B_t = tmp[kc % 2]
                        P.op('dve', 'tensor_tensor', KW(out=t_[:, 0:T], in0=h2b[:, kc, 0:T], in1=rstd[:, 0:T], op=ALU.mult), reads=[B_A, B_rstd], writes=[B_t])
                        P.op('act', 'activation', KW(out=mT[:, kc, 0:T], in_=t_[:, 0:T], func=AF.Identity, scale=s1[:, kc, 0:1], bias=s2[:, kc, 0:1]),
                             reads=[B_t, B_s1, B_s2], writes=[B_mT])
                    P.dma('sp', u2T[:, t0:t0 + T].rearrange("(k p) t -> p k t", p=128), mT[:, :, 0:T], B_u2T, B_mT)
                P.barrier()
        if 'h2' in dbg:
            o = dbgout("h2T", [D, SEQ])
            P.dma('sp', o[:, :], h2T[:, :], Buf("dbgh2"), B_h2T)

        if upto >= 7:
            ffn_phase("f2", 1, h2T, B_h2T, make_passes(C, SEQ, 0, TP), (u2T, B_u2T), outT, B_outT, None, None, 2, None)
        P.barrier()
        P.emit()
    return nc, dbg_out


def _tile_w(W):
    K, N = W.shape
    KC, OC = K // 128, N // 128
    return np.ascontiguousarray(W.reshape(KC, 128, OC, 128).transpose(2, 1, 0, 3)).reshape(OC, 128, KC * 128)


def _colvec(v, n=None):
    v = np.asarray(v, np.float32)
    return np.ascontiguousarray(v.reshape(-1, 128).T)


def _pad_rg(v, C, fill=0.0):
    v = np.asarray(v, np.float32)
    lead = v.shape[:-1]
    vb = v.reshape(lead + (16, C.RGB))
    out = np.full(lead + (16, C.NBC * 128), fill, np.float32)
    out[..., :C.RGB] = vb
    return out.reshape(lead + (C.RC * 128,))


def prep_shared(C, inp):
    S = {}
    S["w_ada_t"] = _tile_w(inp["w_ada"][0])
    S["b_adaT"] = _colvec(inp["b_ada"][0])
    S["normgT"] = np.ascontiguousarray(np.concatenate([_colvec(inp["norm_g"][0, j]) for j in range(6)], axis=1))
    for i in range(2):
        S["wup%d_t" % i] = _tile_w(inp["ffn_w_up"][0, i])
        S["wdn%d_t" % i] = _tile_w(inp["ffn_w_down"][0, i])
    w_in = inp["w_in"][0]
    D = C.D
    cols = []
    zpad = np.zeros((D, 1), np.float32)

    def take(idx):
        return w_in[:, idx]
    parts = []
    parts.append(w_in[:, 0:C.S1])
    parts.append(w_in[:, C.S1:C.S1 + C.DI])
    parts.append(w_in[:, C.S1 + C.DI:C.S1 + C.DI + C.GN])
    parts.append(w_in[:, C.S1 + C.DI + C.GN:C.S2])
    for d in range(2):
        blk = np.zeros((D, 128), np.float32)
        blk[:, :C.H] = w_in[:, C.S2 + d * C.H:C.S2 + (d + 1) * C.H]
        parts.append(blk)
    for s in (C.S3, C.S4):
        blk = np.zeros((D, 16, C.NBC * 128), np.float32)
        blk[:, :, :C.RGB] = w_in[:, s:s + C.RGW].reshape(D, 16, C.RGB)
        parts.append(blk.reshape(D, C.RC * 128))
    parts.append(w_in[:, C.S5:])
    S["win_t"] = _tile_w(np.concatenate(parts, axis=1))
    vec = np.zeros((128, C.NV), np.float32)
    cw = inp["ssd_conv_w"][0]
    nxc = C.XC + 16
    vec[:, C.V["cw"]:C.V["cw"] + nxc * 4] = cw.T.reshape(nxc, 128, 4).transpose(1, 0, 2).reshape(128, nxc * 4)
    vec[:, C.V["cb"]:C.V["cb"] + nxc] = _colvec(inp["ssd_conv_b"][0])
    for d in range(2):
        vec[:C.H, C.V["dtb"] + d] = inp["ssd_dt_bias"][0][d * C.H:(d + 1) * C.H]
        vec[:C.H, C.V["alog"] + d] = inp["ssd_a_log"][0, d]
    vec[:, C.V["sng"]:C.V["sng"] + C.XC] = _colvec(inp["ssd_norm_g"][0])
    rcw = _pad_rg(inp["rg_conv_w"][0], C)
    vec[:, C.V["rcw"]:C.V["rcw"] + C.RC * 4] = rcw.T.reshape(C.RC, 128, 4).transpose(1, 0, 2).reshape(128, C.RC * 4)
    vec[:, C.V["rcb"]:C.V["rcb"] + C.RC] = _colvec(_pad_rg(inp["rg_conv_b"][0], C))
    for d in range(2):
        vec[:, C.V["rba"] + d * C.RC:C.V["rba"] + (d + 1) * C.RC] = _colvec(_pad_rg(inp["rg_b_a"][0, d], C))
        vec[:, C.V["rbx"] + d * C.RC:C.V["rbx"] + (d + 1) * C.RC] = _colvec(_pad_rg(inp["rg_b_x"][0, d], C))
        vec[:, C.V["rlam"] + d * C.RC:C.V["rlam"] + (d + 1) * C.RC] = _colvec(_pad_rg(inp["rg_lam"][0, d], C))
    S["vecs"] = vec
    dr = np.zeros((1, 128), np.float32)
    dr[0, :C.H] = inp["ssd_d"][0]
    S["drow"] = dr
    NB = C.NBC
    rg = np.zeros((2, 2, 16, NB, 128, NB, 128), np.float32)
    for d in range(2):
        for ty, nm in enumerate(("rg_w_a", "rg_w_x")):
            w = np.zeros((16, NB * 128, NB * 128), np.float32)
            w[:, :C.RGB, :C.RGB] = inp[nm][0, d]
            rg[d, ty] = w.reshape(16, NB, 128, NB, 128).transpose(0, 3, 2, 1, 4)
    S["rgw_t"] = np.ascontiguousarray(rg.reshape(64 * NB, 128, NB * 128))
    S["wso_t"] = _tile_w(inp["w_ssd_out"][0])
    wro = np.zeros((16, NB * 128, D), np.float32)
    wro[:, :C.RGB, :] = inp["w_rg_out"][0].reshape(16, C.RGB, D)
    S["wro_t"] = _tile_w(wro.reshape(C.RC * 128, D))
    S["wo_t"] = _tile_w(inp["w_out"][0])
    return S


def prep_core(C, inp, b):
    m = {}
    m["xT"] = np.ascontiguousarray(np.concatenate([inp["x"][b].T, inp["ctx"][b].T], axis=1).astype(np.float32))
    sc = np.stack([_colvec(inp["c"][b]), _colvec(inp["c_ctx"])], axis=2)
    m["scT"] = np.ascontiguousarray(sc.reshape(128, C.KD * 2))
    return m


_CACHE = {}


def kernel(**inputs):
    C = Cfg()
    inp = {k: np.asarray(v) for k, v in inputs.items()}
    nb = inp["x"].shape[0]
    S = prep_shared(C, inp)
    if "nc" not in _CACHE:
        _CACHE["nc"] = build(C)[0]
    nc = _CACHE["nc"]
    in_maps = []
    for b in range(nb):
        m = dict(S)
        m.update(prep_core(C, inp, b))
        in_maps.append(m)
    res = run_bass_kernel_spmd(nc, in_maps, core_ids=list(range(nb)))
    out = np.stack([np.ascontiguousarray(r["outT"].T) for r in res.results], axis=0)
    return out.astype(np.float32)
```

```python
import contextlib
import math
import os as _os
import numpy as np
import concourse.bass as bass
import concourse.mybir as mybir
from concourse.bass_utils import run_bass_kernel_spmd

F32 = mybir.dt.float32
BF16 = mybir.dt.bfloat16
ALU = mybir.AluOpType
AF = mybir.ActivationFunctionType
EPS = 1e-6
NEG = -30000.0
PRECAST = True
DEFER_ADA = True
CACHE_W = True


class Cfg:
    def __init__(self, D=4096, DFF=11008, SEQ=2048, CTX=256, HB=None):
        self.D, self.DFF, self.SEQ, self.CTX = D, DFF, SEQ, CTX
        self.KD = D // 128
        self.KF = DFF // 128
        self.NT = SEQ + CTX
        self.GW = 64
        self.ROWS = SEQ // 64
        self.DI = 2 * D
        self.H = self.DI // 64
        self.E = self.H // 8
        self.XC = self.DI // 128
        self.GN = 1024
        self.RGW = (D * 4 // 3) // 256 * 256
        self.RGB = self.RGW // 16
        self.NBC = (self.RGB + 127) // 128
        self.RC = 16 * self.NBC
        self.S1 = self.DI
        self.S2 = self.S1 + self.DI + 2 * self.GN
        self.S3 = self.S2 + 2 * self.H
        self.S4 = self.S3 + self.RGW
        self.S5 = self.S4 + self.RGW
        self.PIN = self.S5 + 2 * D
        self.OC_Z = 0
        self.OC_X = self.OC_Z + self.XC
        self.OC_B = self.OC_X + self.XC
        self.OC_C = self.OC_B + 8
        self.OC_DT = self.OC_C + 8
        self.OC_RG = self.OC_DT + 2
        self.OC_RX = self.OC_RG + self.RC
        self.OC_G = self.OC_RX + self.RC
        self.NOC = self.OC_G + 2 * self.KD
        self.NCL = SEQ // 128
        self.NCC = CTX // 128
        self.NCH = self.NCL + self.NCC
        self.HB = HB or min(8, self.E)
        self.NQ = self.E // self.HB
        self.TP = 512
        self.NS = 512
        o = 0
        self.V = {}
        for nm, n in (("cw", (self.XC + 16) * 4), ("cb", self.XC + 16), ("dtb", 2), ("alog", 2), ("sng", self.XC),
                      ("rcw", self.RC * 4), ("rcb", self.RC), ("rba", 2 * self.RC), ("rbx", 2 * self.RC), ("rlam", 2 * self.RC)):
            self.V[nm] = o
            o += n
        self.NV = o


def KW(*a, **k):
    return (a, k)


def _call(e, meth, args, kw):
    try:
        return getattr(e, meth)(*args, **kw)
    except Exception:
        print("FAILED INSTR", meth, [str(a)[:200] for a in args], {k: str(v)[:200] for k, v in kw.items()})
        raise


class Buf:
    def __init__(self, name):
        self.name = name
        self.last_w = None
        self.readers = []
        self.dsem = None
        self.psum = False


class Prog:
    ENG = ('pe', 'act', 'dve', 'pool', 'sp')

    def __init__(self, nc, ndsem=56):
        self.nc = nc
        self.q = {e: [] for e in self.ENG}
        self.cnt = {}
        self.known = {e: {} for e in self.ENG}
        self.semh = {}
        self.free_d = []
        self.dbufs = []
        self.nins = 0
        for e in self.ENG:
            self._sem('e:' + e)
        self.free_d = {'pool': [], 'sp': [], 'act': []}
        for i in range(ndsem):
            k = 'd:%d' % i
            self._sem(k)
            self.free_d['pool' if i < 10 else 'sp'].append(k)

    def _sem(self, key):
        if key not in self.semh:
            self.semh[key] = self.nc.alloc_semaphore(key.replace(':', '_'))
            self.cnt[key] = 0
        return self.semh[key]

    def _wait(self, eng, tok):
        if tok is None:
            return
        key, val = tok
        if key[0] == 'd':
            val = self.cnt[key]
        elif key == 'e:' + eng:
            if eng == 'pe' or val > self.cnt[key]:
                return
        if self.known[eng].get(key, 0) >= val:
            return
        self.known[eng][key] = val
        h = self.semh[key]
        self.q[eng].append(lambda e, h=h, val=val: e.wait_ge(h, val))

    def _deps(self, eng, reads, writes):
        for b in reads:
            self._wait(eng, b.last_w)
            if b.psum:
                for t in b.readers:
                    if t[0] != 'e:' + eng:
                        self._wait(eng, t)
        for b in writes:
            self._wait(eng, b.last_w)
            for t in b.readers:
                self._wait(eng, t)

    @staticmethod
    def _compact(toks):
        best = {}
        for k, v in toks:
            if best.get(k, 0) < v:
                best[k] = v
        return list(best.items())

    def op(self, eng, meth, akw, reads=(), writes=(), inc=True):
        args, kw = akw
        self._deps(eng, reads, writes)
        self.nins += 1
        key = 'e:' + eng
        h = self.semh[key]
        if inc:
            self.cnt[key] += 1
            tok = (key, self.cnt[key])
            self.q[eng].append(lambda e, meth=meth, args=args, kw=kw, h=h: _call(e, meth, args, kw).then_inc(h, 1))
        else:
            tok = (key, self.cnt[key] + 1)
            self.q[eng].append(lambda e, meth=meth, args=args, kw=kw: _call(e, meth, args, kw))
        for b in reads:
            b.readers.append(tok)
            if len(b.readers) > 48:
                b.readers = self._compact(b.readers)
        for b in writes:
            b.last_w = tok
            b.readers = []
        return tok

    def dma(self, eng, out_ap, in_ap, dst, src, **kw):
        self._deps(eng, [src], [dst])
        self.nins += 1
        if dst.dsem is None:
            dst.dsem = self.free_d[eng].pop()
            dst.dq = eng
            self.dbufs.append(dst)
        key = dst.dsem
        h = self.semh[key]
        self.cnt[key] += 16
        tok = (key, self.cnt[key])
        self.q[eng].append(lambda e, h=h, out_ap=out_ap, in_ap=in_ap, kw=kw: e.dma_start(out=out_ap, in_=in_ap, **kw).then_inc(h, 16))
        src.readers.append(tok)
        if len(src.readers) > 48:
            src.readers = self._compact(src.readers)
        dst.last_w = tok
        dst.readers = []
        return tok

    def barrier(self):
        for eng in self.ENG:
            for key in list(self.cnt):
                if self.cnt[key] > 0:
                    self._wait(eng, (key, self.cnt[key]))
        for b in self.dbufs:
            self.free_d[b.dq].append(b.dsem)
            b.dsem = None
        self.dbufs = []

    def emit(self):
        nc = self.nc
        with nc.Block() as block:
            @block.tensor
            def _(e):
                for f in self.q['pe']:
                    f(e)

            @block.scalar
            def _(e):
                for f in self.q['act']:
                    f(e)

            @block.vector
            def _(e):
                for f in self.q['dve']:
                    f(e)

            @block.gpsimd
            def _(e):
                for f in self.q['pool']:
                    f(e)

            @block.sync
            def _(e):
                for f in self.q['sp']:
                    f(e)


def make_passes(C, nlat, nctx, TP):
    segs = [(0, nlat, 0)]
    if nctx:
        segs.append((C.SEQ, nctx, 1))
    passes, cur, room = [], [], TP
    for (s0, n, w) in segs:
        while n > 0:
            t = min(n, room)
            cur.append((s0, t, w))
            s0 += t
            n -= t
            room -= t
            if room == 0:
                passes.append(cur)
                cur, room = [], TP
    if cur:
        passes.append(cur)
    return passes


def slices_of(n, NS):
    out, c = [], 0
    while c < n:
        t = min(NS, n - c)
        out.append((c, t))
        c += t
    return out


def build(C, upto=99, dbg=()):
    nc = bass.Bass("TRN2", target_bir_lowering=False)
    P = Prog(nc)
    D, KD, KF, NT, SEQ, CTX, TP, NS = C.D, C.KD, C.KF, C.NT, C.SEQ, C.CTX, C.TP, C.NS
    H, E, XC, RC = C.H, C.E, C.XC, C.RC

    def din(name, shape, dt=F32):
        return nc.dram_tensor(name, list(shape), dt, kind="ExternalInput").ap()

    def dsc(name, shape, dt):
        return nc.dram_tensor(name, list(shape), dt).ap()

    dbg_out = {}

    def dbgout(name, shape):
        dbg_out[name] = nc.dram_tensor("dbg_" + name, list(shape), F32, kind="ExternalOutput").ap()
        return dbg_out[name]

    xT = din("xT", [D, NT])
    scT = din("scT", [128, KD * 2])
    w_ada_t = din("w_ada_t", [9 * KD, 128, KD * 128])
    b_adaT = din("b_adaT", [128, 9 * KD])
    normgT = din("normgT", [128, 6 * KD])
    wup_t = [din("wup%d_t" % i, [2 * KF, 128, KD * 128]) for i in range(2)]
    wdn_t = [din("wdn%d_t" % i, [KD, 128, KF * 128]) for i in range(2)]
    win_t = din("win_t", [C.NOC, 128, KD * 128])
    vecs = din("vecs", [128, C.NV])
    drow = din("drow", [1, 128])
    rgw_t = din("rgw_t", [64 * C.NBC, 128, C.NBC * 128])
    wso_t = din("wso_t", [KD, 128, XC * 128])
    wro_t = din("wro_t", [KD, 128, RC * 128])
    wo_t = din("wo_t", [KD, 128, KD * 128])
    outT = nc.dram_tensor("outT", [D, SEQ], F32, kind="ExternalOutput").ap()

    h1T = dsc("h1T", [D, NT], F32)
    u1T = dsc("u1T", [D, NT], BF16)
    pz = dsc("pz", [C.DI, SEQ], BF16)
    x_tm = dsc("x_tm", [NT, C.DI], BF16)
    pB = dsc("pB", [C.GN, NT], BF16)
    pC = dsc("pC", [C.GN, NT], BF16)
    B_tm = dsc("B_tm", [NT, C.GN], BF16)
    pdt = dsc("pdt", [256, NT], F32)
    prg = dsc("prg", [RC * 128, SEQ], BF16)
    prx = dsc("prx", [RC * 128, NT], F32)
    pgt = dsc("pgt", [2 * D, SEQ], BF16)
    ynT = dsc("ynT", [C.DI, SEQ], BF16)
    rgyT = dsc("rgyT", [RC * 128, SEQ], BF16)
    sbin = dsc("sbin", [8, C.NCL, 128, E * 64], BF16)
    wupc = [dsc("wupc%d" % i, [KF, 128, 2 * KD * 128], BF16) for i in range(2)]
    wdnc = [dsc("wdnc%d" % i, [KD, 128, KF * 128], BF16) for i in range(2)]
    wsoc = dsc("wsoc", [KD, 128, XC * 128], BF16)
    wroc = dsc("wroc", [KD, 128, RC * 128], BF16)
    woc = dsc("woc", [KD, 128, KD * 128], BF16)
    h2T = dsc("h2T", [D, SEQ], F32)
    u2T = dsc("u2T", [D, SEQ], BF16)

    B_w = Buf("weights")
    B_xT, B_h1T, B_u1T, B_outT = Buf("xT"), Buf("h1T"), Buf("u1T"), Buf("outT")
    B_pz, B_xtm, B_pB, B_pC, B_Btm, B_pdt = Buf("pz"), Buf("x_tm"), Buf("pB"), Buf("pC"), Buf("B_tm"), Buf("pdt")
    B_prg, B_prx, B_pgt, B_ynT, B_rgyT, B_sbin = Buf("prg"), Buf("prx"), Buf("pgt"), Buf("ynT"), Buf("rgyT"), Buf("sbin")
    B_h2T, B_u2T = Buf("h2T"), Buf("u2T")
    B_cache = {k_: Buf(k_) for k_ in ("wupc0", "wdnc0", "wupc1", "wdnc1", "wsoc", "wroc", "woc")}

    with contextlib.ExitStack() as glob:
        def sbuf(stack, name, shape, dt=F32):
            t = stack.enter_context(nc.sbuf_tensor(name, list(shape), dt))
            return t, Buf(name)

        def psum(stack, name, shape, dt=F32):
            t = stack.enter_context(nc.psum_tensor(name, list(shape), dt))
            b = Buf(name)
            b.psum = True
            return t, b

        modT, B_mod = sbuf(glob, "modT", [128, 9 * KD, 2])
        ngT, B_ng = sbuf(glob, "ngT", [128, 6 * KD])
        ones_bf, B_ones = sbuf(glob, "ones_bf", [128, 128], BF16)
        ones32, B_ones32 = sbuf(glob, "ones32", [128, 128])
        ident32, B_id32 = sbuf(glob, "ident32", [128, 128])
        identb, B_idb = sbuf(glob, "identb", [128, 128], BF16)
        vec, B_vec = sbuf(glob, "vec", [128, C.NV])
        cf = {}
        for nm in ("s1_0", "s2_0", "co_0", "s1_1", "s2_1", "co_1", "s1_2", "s2_2", "co_2"):
            cf[nm] = sbuf(glob, "cf_" + nm, [128, KD, 2])
        P.dma('sp', ngT[:], normgT[:, :], B_ng, B_w)
        P.dma('sp', vec[:], vecs[:, :], B_vec, B_w)
        P.op('dve', 'memset', KW(ones_bf[:], 1.0), writes=[B_ones])
        P.op('dve', 'memset', KW(ones32[:], 1.0), writes=[B_ones32])
        P.op('pool', 'memset', KW(ident32[:], 1.0), writes=[B_id32])
        P.op('pool', 'affine_select', KW(out=ident32[:], in_=ident32[:], pattern=[[1, 128]], base=0, channel_multiplier=-1,
                                         compare_op=ALU.is_equal, fill=0.0), reads=[B_id32], writes=[B_id32])
        P.op('dve', 'tensor_copy', KW(out=identb[:], in_=ident32[:]), reads=[B_id32], writes=[B_idb])

        def V(nm, j=0, n=1):
            o = C.V[nm] + j
            return vec[:, o:o + n]

        scb, B_scb = sbuf(glob, "scb", [128, KD, 2], BF16)
        badT, B_bad = sbuf(glob, "badT", [128, 9 * KD])
        NADA0 = 5 * KD if (upto >= 5 and DEFER_ADA) else 9 * KD

        def ada_chunk(oc, wt, B_wt, pt, B_pt):
            P.dma('pool', wt[:, 0:KD * 128], w_ada_t[oc], B_wt, B_w, max_dma_last_dim=4096)
            for kc in range(KD):
                P.op('pe', 'matmul', KW(pt[:, 0:2], lhsT=wt[:, kc * 128:(kc + 1) * 128], rhs=scb[:, kc, :],
                                        start=(kc == 0), stop=(kc == KD - 1)),
                     reads=[B_wt, B_scb], writes=[B_pt], inc=(kc == KD - 1))
            P.op('dve', 'tensor_scalar', KW(out=modT[:, oc, :], in0=pt[:, 0:2], scalar1=badT[:, oc:oc + 1], scalar2=None,
                                            op0=ALU.add), reads=[B_pt, B_bad], writes=[B_mod])

        def mslot(j):
            return modT[:, j * KD:(j + 1) * KD, :]

        def ng(j):
            return ngT[:, j * KD:(j + 1) * KD].unsqueeze(2).to_broadcast([128, KD, 2])

        def cf_tables(k, which):
            gpre, gpost = ((0, 1), (2, 3), (4, 5))[k]
            s1, B_s1 = cf["s1_%d" % k]
            s2, B_s2 = cf["s2_%d" % k]
            co, B_co = cf["co_%d" % k]
            if 's' in which:
                P.op('dve', 'scalar_tensor_tensor', KW(out=s1[:], in0=mslot(3 * k + 1), scalar=1.0, in1=ng(gpre), op0=ALU.add, op1=ALU.mult),
                     reads=[B_mod, B_ng], writes=[B_s1])
                P.op('dve', 'tensor_copy', KW(out=s2[:], in_=mslot(3 * k)), reads=[B_mod], writes=[B_s2])
            if 'c' in which:
                P.op('dve', 'scalar_tensor_tensor', KW(out=co[:], in0=mslot(3 * k + 2), scalar=(1.0 if k == 1 else 0.5), in1=ng(gpost),
                                                       op0=ALU.mult, op1=ALU.mult), reads=[B_mod, B_ng], writes=[B_co])

        with contextlib.ExitStack() as ph:
            sc32, B_sc32 = sbuf(ph, "sc32", [128, KD * 2])
            NSL = 3
            wsl = [sbuf(ph, "wada%d" % i, [128, KD * 128], BF16) for i in range(NSL)]
            pss = [psum(ph, "p0ps%d" % i, [128, 512]) for i in range(4)]
            P.dma('sp', sc32[:], scT[:, :], B_sc32, B_w)
            P.dma('sp', badT[:], b_adaT[:, :], B_bad, B_w)
            P.op('act', 'activation', KW(out=scb[:].rearrange("p k t -> p (k t)"), in_=sc32[:], func=AF.Silu), reads=[B_sc32], writes=[B_scb])
            for oc in range(NADA0):
                wt, B_wt = wsl[oc % NSL]
                pt, B_pt = pss[oc % 4]
                ada_chunk(oc, wt, B_wt, pt, B_pt)
            cf_tables(0, 'sc')
            cf_tables(1, 's')
            if NADA0 == 9 * KD:
                cf_tables(1, 'c')
                cf_tables(2, 'sc')
            P.barrier()
        if 'mod' in dbg:
            o = dbgout("mod", [128, 9 * KD * 2])
            P.dma('sp', o[:, :], modT[:].rearrange("p a b -> p (a b)"), Buf("dbgmod"), B_mod)

        def rms_rstd(nps, sq, rstd, B_rstd, src_chunk, src_bufs, nchunks, nd, T):
            sls = slices_of(T, 512)
            pts = [nps() for _ in sls]
            for kc in range(nchunks):
                s_, B_s = sq[kc % 2]
                P.op('act', 'activation', KW(out=s_[:, 0:T], in_=src_chunk(kc), func=AF.Square), reads=src_bufs, writes=[B_s])
                for (pt, B_pt), (c0, n) in zip(pts, sls):
                    P.op('pe', 'matmul', KW(pt[:, 0:n], lhsT=ones_bf[:], rhs=s_[:, c0:c0 + n], start=(kc == 0), stop=(kc == nchunks - 1)),
                         reads=[B_s, B_ones], writes=[B_pt])
            for (pt, B_pt), (c0, n) in zip(pts, sls):
                P.op('act', 'activation', KW(out=rstd[:, c0:c0 + n], in_=pt[:, 0:n], func=AF.Sqrt, scale=1.0 / nd, bias=EPS),
                     reads=[B_pt], writes=[B_rstd])
            P.op('dve', 'reciprocal', KW(out=rstd[:, 0:T], in_=rstd[:, 0:T]), reads=[B_rstd], writes=[B_rstd])

        def ffn_phase(tag, l, h_src, B_hsrc, passes, u_src, h_dst, B_hdst, u_dst, B_udst, kpre, knext):
            B_wupc, B_wdnc = B_cache["wupc%d" % l], B_cache["wdnc%d" % l]
            pre = PRECAST and l == 1 and upto >= 4
            with contextlib.ExitStack() as ph:
                uT, B_uT = sbuf(ph, tag + "uT", [128, KD, TP], BF16)
                gT, B_gT = sbuf(ph, tag + "gT", [128, max(KF, 2 * KD) * TP], BF16)
                WS = max(KF, 2 * KD) * 128
                wsl = [sbuf(ph, tag + "w%d" % i, [128, WS], BF16) for i in range(2)]
                tmp = [sbuf(ph, tag + "tmp%d" % i, [128, TP]) for i in range(2)]
                sq = [sbuf(ph, tag + "sq%d" % i, [128, TP], BF16) for i in range(2)]
                rstd, B_rstd = sbuf(ph, tag + "rstd", [128, TP])
                hch = [sbuf(ph, tag + "hch%d" % i, [128, TP]) for i in range(3)]
                pss = [psum(ph, tag + "ps%d" % i, [128, 512]) for i in range(8)]
                st = {'ps': 0, 'w': 0}
                hT32 = gT[:].bitcast(F32).rearrange("p (k t) -> p k t", t=TP)

                def nps():
                    r = pss[st['ps'] % 8]
                    st['ps'] += 1
                    return r

                def nw():
                    r = wsl[st['w'] % 2]
                    st['w'] += 1
                    return r

                def normmod(k, segs, T):
                    s1, B_s1 = cf["s1_%d" % k]
                    s2, B_s2 = cf["s2_%d" % k]
                    for kc in range(KD):
                        t_, B_t = tmp[kc % 2]
                        P.op('dve', 'tensor_tensor', KW(out=t_[:, 0:T], in0=hT32[:, kc, 0:T], in1=rstd[:, 0:T], op=ALU.mult),
                             reads=[B_gT, B_rstd], writes=[B_t])
                        c0 = 0
                        for (_, n, which) in segs:
                            P.op('act', 'activation', KW(out=uT[:, kc, c0:c0 + n], in_=t_[:, c0:c0 + n], func=AF.Identity,
                                                         scale=s1[:, kc, which:which + 1], bias=s2[:, kc, which:which + 1]),
                                 reads=[B_t, B_s1, B_s2], writes=[B_uT])
                            c0 += n

                for ip, segs in enumerate(passes):
                    T = sum(n for (_, n, _) in segs)
                    sls = slices_of(T, NS)
                    if u_src is None:
                        c0 = 0
                        for (s0, n, which) in segs:
                            P.dma('sp', hT32[:, 0:KD, c0:c0 + n], h_src[:, s0:s0 + n].rearrange("(k p) t -> p k t", p=128), B_gT, B_hsrc)
                            c0 += n
                        rms_rstd(nps, sq, rstd, B_rstd, lambda kc: hT32[:, kc, 0:T], [B_gT], KD, D, T)
                        normmod(kpre, segs, T)
                    else:
                        c0 = 0
                        for (s0, n, which) in segs:
                            P.dma('sp', uT[:, :, c0:c0 + n], u_src[0][:, s0:s0 + n].rearrange("(k p) t -> p k t", p=128), B_uT, u_src[1])
                            c0 += n
                    for fc in range(KF):
                        wt, B_wt = nw()
                        if (ip == 0 or not CACHE_W) and not pre:
                            P.dma('pool', wt[:, 0:KD * 128], wup_t[l][fc], B_wt, B_w, max_dma_last_dim=4096)
                            P.dma('pool', wt[:, KD * 128:2 * KD * 128], wup_t[l][KF + fc], B_wt, B_w, max_dma_last_dim=4096)
                            if CACHE_W and len(passes) > 1:
                                P.dma('sp', wupc[l][fc], wt[:, 0:2 * KD * 128], B_wupc, B_wt)
                        else:
                            P.dma('pool', wt[:, 0:2 * KD * 128], wupc[l][fc], B_wt, B_wupc)
                        for si, (c0, n) in enumerate(sls):
                            pg, B_pg = nps()
                            pu, B_pu = nps()
                            for kc in range(KD):
                                P.op('pe', 'matmul', KW(pg[:, 0:n], lhsT=wt[:, kc * 128:(kc + 1) * 128], rhs=uT[:, kc, c0:c0 + n],
                                                        start=(kc == 0), stop=(kc == KD - 1)), reads=[B_wt, B_uT], writes=[B_pg], inc=(kc == KD - 1))
                            for kc in range(KD):
                                P.op('pe', 'matmul', KW(pu[:, 0:n], lhsT=wt[:, (KD + kc) * 128:(KD + kc + 1) * 128], rhs=uT[:, kc, c0:c0 + n],
                                                        start=(kc == 0), stop=(kc == KD - 1)), reads=[B_wt, B_uT], writes=[B_pu], inc=(kc == KD - 1))
                            t_, B_t = tmp[si % 2]
                            P.op('act', 'activation', KW(out=t_[:, 0:n], in_=pg[:, 0:n], func=AF.Silu), reads=[B_pg], writes=[B_t])
                            P.op('dve', 'tensor_tensor', KW(out=gT[:, fc * TP + c0: fc * TP + c0 + n], in0=t_[:, 0:n], in1=pu[:, 0:n], op=ALU.mult),
                                 reads=[B_t, B_pu], writes=[B_gT])
                    for oc in range(KD):
                        wt, B_wt = nw()
                        if (ip == 0 or not CACHE_W) and not pre:
                            P.dma('pool', wt[:, 0:KF * 128], wdn_t[l][oc], B_wt, B_w, max_dma_last_dim=4096)
                            if CACHE_W and len(passes) > 1:
                                P.dma('sp', wdnc[l][oc], wt[:, 0:KF * 128], B_wdnc, B_wt)
                        else:
                            P.dma('pool', wt[:, 0:KF * 128], wdnc[l][oc], B_wt, B_wdnc)
                        for (c0, n) in sls:
                            pt, B_pt = nps()
                            for kc in range(KF):
                                P.op('pe', 'matmul', KW(pt[:, 0:n], lhsT=wt[:, kc * 128:(kc + 1) * 128], rhs=gT[:, kc * TP + c0: kc * TP + c0 + n],
                                                        start=(kc == 0), stop=(kc == KF - 1)), reads=[B_wt, B_gT], writes=[B_pt], inc=(kc == KF - 1))
                            P.op('act', 'activation', KW(out=uT[:, oc, c0:c0 + n], in_=pt[:, 0:n], func=AF.Copy), reads=[B_pt], writes=[B_uT])
                    rms_rstd(nps, sq, rstd, B_rstd, lambda kc: uT[:, kc, 0:T], [B_uT], KD, D, T)
                    co, B_co = cf["co_%d" % kpre]
                    for kc in range(KD):
                        hc_, B_hc = hch[kc % 3]
                        c0 = 0
                        for (s0, n, which) in segs:
                            P.dma('sp', hc_[:, c0:c0 + n], h_src[kc * 128:(kc + 1) * 128, s0:s0 + n], B_hc, B_hsrc)
                            c0 += n
                        t_, B_t = tmp[kc % 2]
                        P.op('dve', 'tensor_tensor', KW(out=t_[:, 0:T], in0=uT[:, kc, 0:T], in1=rstd[:, 0:T], op=ALU.mult), reads=[B_uT, B_rstd], writes=[B_t])
                        c0 = 0
                        for (s0, n, which) in segs:
                            P.op('dve', 'scalar_tensor_tensor', KW(out=hT32[:, kc, c0:c0 + n], in0=t_[:, c0:c0 + n], scalar=co[:, kc, which:which + 1],
                                                                   in1=hc_[:, c0:c0 + n], op0=ALU.mult, op1=ALU.add),
                                 reads=[B_t, B_co, B_hc], writes=[B_gT])
                            c0 += n
                    c0 = 0
                    for (s0, n, which) in segs:
                        P.dma('sp', h_dst[:, s0:s0 + n].rearrange("(k p) t -> p k t", p=128), hT32[:, 0:KD, c0:c0 + n], B_hdst, B_gT)
                        c0 += n
                    if knext is not None:
                        rms_rstd(nps, sq, rstd, B_rstd, lambda kc: hT32[:, kc, 0:T], [B_gT], KD, D, T)
                        normmod(knext, segs, T)
                        c0 = 0
                        for (s0, n, which) in segs:
                            P.dma('sp', u_dst[:, s0:s0 + n].rearrange("(k p) t -> p k t", p=128), uT[:, :, c0:c0 + n], B_udst, B_uT)
                            c0 += n
                P.barrier()

        if upto >= 1:
            ffn_phase("f1", 0, xT, B_xT, make_passes(C, SEQ, CTX, TP), None, h1T, B_h1T, u1T, B_u1T, 0, 1)
        if 'h1' in dbg:
            o = dbgout("h1T", [D, NT])
            P.dma('sp', o[:, :], h1T[:, :], Buf("dbgh1"), B_h1T)
            hb, B_hb = sbuf(glob, "dbg_u1", [128, NT], BF16)
            hf, B_hf = sbuf(glob, "dbg_u1f", [128, NT])
            o = dbgout("u1T", [D, NT])
            B_o = Buf("dbgu1")
            for kc in range(KD):
                P.dma('sp', hb[:], u1T[kc * 128:(kc + 1) * 128, :], B_hb, B_u1T)
                P.op('dve', 'tensor_copy', KW(out=hf[:], in_=hb[:]), reads=[B_hb], writes=[B_hf])
                P.dma('sp', o[kc * 128:(kc + 1) * 128, :], hf[:], B_o, B_hf)

        if upto >= 2:
            with contextlib.ExitStack() as ph:
                uA, B_uA = sbuf(ph, "p2u", [128, KD, NT], BF16)
                wsl = [sbuf(ph, "p2w%d" % i, [128, KD * 128], BF16) for i in range(2)]
                raw, B_raw = sbuf(ph, "p2raw", [128, NT])
                tm2, B_tm2 = sbuf(ph, "p2tmp", [128, NT])
                ob = [sbuf(ph, "p2ob%d" % i, [128, NT], BF16) for i in range(1)]
                xblk = [sbuf(ph, "p2xb%d" % i, [128, 8 * 128], BF16) for i in range(2)]
                pss = [psum(ph, "p2ps%d" % i, [128, 512]) for i in range(7)]
                psT, B_psT = psum(ph, "p2psT", [128, 1024], BF16)
                st = {'ps': 0, 'w': 0, 'ob': 0, 'xb': 0}
                B_dst = {}
                for kc in range(KD):
                    P.dma('sp', uA[:, kc, :], u1T[kc * 128:(kc + 1) * 128, :], B_uA, B_u1T)
                lat_sl = slices_of(SEQ, 512)
                all_sl = lat_sl + [(SEQ + c0, n) for (c0, n) in slices_of(CTX, 512)]
                segs_lc = [(0, SEQ), (SEQ, NT)]

                def conv(src, dst, wcol, bcol):
                    P.op('dve', 'tensor_scalar', KW(out=dst[:, 0:NT], in0=src[:, 0:NT], scalar1=V(wcol[0], wcol[1] + 2), scalar2=V(bcol[0], bcol[1]),
                                                    op0=ALU.mult, op1=ALU.add), reads=[src_b[0], B_vec], writes=[dst_b[0]])
                    for k in (0, 1, 3):
                        o_ = k - 2
                        for (a, e) in segs_lc:
                            d0, d1 = a + max(0, -o_), e - max(0, o_)
                            P.op('dve', 'scalar_tensor_tensor', KW(out=dst[:, d0:d1], in0=src[:, d0 + o_:d1 + o_], scalar=V(wcol[0], wcol[1] + k),
                                                                   in1=dst[:, d0:d1], op0=ALU.mult, op1=ALU.add),
                                 reads=[src_b[0], B_vec, dst_b[0]], writes=[dst_b[0]])
                src_b, dst_b = [None], [None]

                _skip = _os.environ.get('P2SKIP', '').split(',')
                pend = []
                for oc in range(C.NOC):
                    _ty = ('z' if oc < C.OC_X else 'x' if oc < C.OC_DT else 'dt' if oc < C.OC_RG else 'rg' if oc < C.OC_RX else 'rx' if oc < C.OC_G else 'g')
                    if _ty in _skip:
                        continue
                    lat_only = (oc < C.OC_X) or (C.OC_RG <= oc < C.OC_RX) or (oc >= C.OC_G)
                    sls = lat_sl if lat_only else all_sl
                    wt, B_wt = wsl[st['w'] % 2]
                    st['w'] += 1
                    P.dma('pool', wt[:], win_t[oc], B_wt, B_w, max_dma_last_dim=4096)
                    pts = []
                    for (c0, n) in sls:
                        pt, B_pt = pss[st['ps'] % 7]
                        st['ps'] += 1
                        for kc in range(KD):
                            P.op('pe', 'matmul', KW(pt[:, 0:n], lhsT=wt[:, kc * 128:(kc + 1) * 128], rhs=uA[:, kc, c0:c0 + n],
                                                    start=(kc == 0), stop=(kc == KD - 1)), reads=[B_wt, B_uA], writes=[B_pt], inc=(kc == KD - 1))
                        pts.append((pt, B_pt, c0, n))
                    while pend:
                        pend.pop(0)()
                    o_t, B_ot = ob[0]
                    if oc < C.OC_X or oc >= C.OC_G:
                        st['ob'] += 1
                        fn = AF.Silu if oc < C.OC_X else AF.Sigmoid
                        for (pt, B_pt, c0, n) in pts:
                            P.op('act', 'activation', KW(out=o_t[:, c0:c0 + n], in_=pt[:, 0:n], func=fn), reads=[B_pt], writes=[B_ot])
                        if oc < C.OC_X:
                            P.dma('sp', pz[oc * 128:(oc + 1) * 128, :], o_t[:, 0:SEQ], B_pz, B_ot)
                        else:
                            j = oc - C.OC_G
                            P.dma('sp', pgt[j * 128:(j + 1) * 128, :], o_t[:, 0:SEQ], B_pgt, B_ot)
                    elif oc < C.OC_DT:
                        st['ob'] += 1
                        j = oc - C.OC_X
                        for (pt, B_pt, c0, n) in pts:
                            P.op('act', 'activation', KW(out=raw[:, c0:c0 + n], in_=pt[:, 0:n], func=AF.Copy), reads=[B_pt], writes=[B_raw])
                        src_b[0], dst_b[0] = B_raw, B_tm2
                        conv(raw, tm2, ("cw", 4 * j), ("cb", j))
                        P.op('act', 'activation', KW(out=o_t[:, 0:NT], in_=tm2[:, 0:NT], func=AF.Silu), reads=[B_tm2], writes=[B_ot])
                        def tjob(j=j, o_t=o_t, B_ot=B_ot):
                            nchunk = NT // 128
                            for c8 in range(0, nchunk, 8):
                                m = min(8, nchunk - c8)
                                for ci in range(m):
                                    P.op('pe', 'transpose', KW(psT[:, ci * 128:(ci + 1) * 128], o_t[:, (c8 + ci) * 128:(c8 + ci + 1) * 128], identb[:]),
                                         reads=[B_ot, B_idb], writes=[B_psT], inc=(ci == m - 1))
                                xb_, B_xb = xblk[st['xb'] % 2]
                                st['xb'] += 1
                                P.op('dve', 'tensor_copy', KW(out=xb_[:, 0:m * 128], in_=psT[:, 0:m * 128]), reads=[B_psT], writes=[B_xb])
                                if j < XC:
                                    dstap = x_tm[c8 * 128:(c8 + m) * 128, j * 128:(j + 1) * 128].rearrange("(c p) j -> p c j", p=128)
                                    P.dma('sp', dstap, xb_[:, 0:m * 128].rearrange("p (c j) -> p c j", j=128), B_xtm, B_xb)
                                else:
                                    jj = j - XC
                                    dstap = B_tm[c8 * 128:(c8 + m) * 128, jj * 128:(jj + 1) * 128].rearrange("(c p) j -> p c j", p=128)
                                    P.dma('sp', dstap, xb_[:, 0:m * 128].rearrange("p (c j) -> p c j", j=128), B_Btm, B_xb)
                        if (j < XC or (XC <= j < XC + 8)) and 'xt' not in _skip:
                            pend.append(tjob)
                        if XC <= j < XC + 8:
                            jj = j - XC
                            P.dma('sp', pB[jj * 128:(jj + 1) * 128, :], o_t[:, 0:NT], B_pB, B_ot)
                        elif j >= XC + 8:
                            jj = j - XC - 8
                            P.dma('sp', pC[jj * 128:(jj + 1) * 128, :], o_t[:, 0:NT], B_pC, B_ot)
                    elif oc < C.OC_RG:
                        j = oc - C.OC_DT
                        for (pt, B_pt, c0, n) in pts:
                            P.op('act', 'activation', KW(out=raw[:, c0:c0 + n], in_=pt[:, 0:n], func=AF.Exp, bias=V("dtb", j), scale=1.0),
                                 reads=[B_pt, B_vec], writes=[B_raw])
                        P.op('act', 'activation', KW(out=tm2[:, 0:NT], in_=raw[:, 0:NT], func=AF.Ln, bias=1.0, scale=1.0), reads=[B_raw], writes=[B_tm2])
                        P.dma('sp', pdt[j * 128:(j + 1) * 128, :], tm2[:, 0:NT], B_pdt, B_tm2)
                    elif oc < C.OC_RX:
                        st['ob'] += 1
                        j = oc - C.OC_RG
                        for (pt, B_pt, c0, n) in pts:
                            P.op('act', 'activation', KW(out=raw[:, c0:c0 + n], in_=pt[:, 0:n], func=AF.Copy), reads=[B_pt], writes=[B_raw])
                        P.op('dve', 'tensor_tensor', KW(out=tm2[:, 0:SEQ], in0=raw[:, 0:SEQ], in1=raw[:, 0:SEQ], op=ALU.mult), reads=[B_raw], writes=[B_tm2])
                        P.op('dve', 'tensor_scalar', KW(out=tm2[:, 0:SEQ], in0=tm2[:, 0:SEQ], scalar1=0.044715, scalar2=1.0, op0=ALU.mult, op1=ALU.add),
                             reads=[B_tm2], writes=[B_tm2])
                        P.op('dve', 'tensor_tensor', KW(out=tm2[:, 0:SEQ], in0=tm2[:, 0:SEQ], in1=raw[:, 0:SEQ], op=ALU.mult), reads=[B_raw, B_tm2], writes=[B_tm2])
                        P.op('act', 'activation', KW(out=tm2[:, 0:SEQ], in_=tm2[:, 0:SEQ], func=AF.Sigmoid, scale=1.5957691216057308), reads=[B_tm2], writes=[B_tm2])
                        P.op('dve', 'tensor_tensor', KW(out=o_t[:, 0:SEQ], in0=tm2[:, 0:SEQ], in1=raw[:, 0:SEQ], op=ALU.mult), reads=[B_raw, B_tm2], writes=[B_ot])
                        P.dma('sp', prg[j * 128:(j + 1) * 128, :], o_t[:, 0:SEQ], B_prg, B_ot)
                    else:
                        j = oc - C.OC_RX
                        for (pt, B_pt, c0, n) in pts:
                            if c0 < SEQ:
                                r0 = c0 // C.GW
                                nr = n // C.GW
                                P.op('act', 'activation', KW(out=tm2[:, 0:SEQ].rearrange("p (c r) -> p r c", r=C.ROWS)[:, r0:r0 + nr, :],
                                                             in_=pt[:, 0:n].rearrange("p (r c) -> p r c", c=C.GW), func=AF.Copy),
                                     reads=[B_pt], writes=[B_tm2])
                            else:
                                P.op('act', 'activation', KW(out=tm2[:, c0:c0 + n], in_=pt[:, 0:n], func=AF.Copy), reads=[B_pt], writes=[B_tm2])
                        src_b[0], dst_b[0] = B_tm2, B_raw
                        conv(tm2, raw, ("rcw", 4 * j), ("rcb", j))
                        P.dma('sp', prx[j * 128:(j + 1) * 128, :], raw[:, 0:NT], B_prx, B_raw)
                while pend:
                    pend.pop(0)()
                P.barrier()
        if 'p2' in dbg:
            tb, B_tb = sbuf(glob, "dbg_p2b", [128, NT], BF16)
            tf, B_tf = sbuf(glob, "dbg_p2f", [128, NT])
            for (nm, src, B_src, rows, cols, isbf) in (("pz", pz, B_pz, C.DI, SEQ, 1), ("pB", pB, B_pB, C.GN, NT, 1), ("pC", pC, B_pC, C.GN, NT, 1),
                                                      ("pdt", pdt, B_pdt, 256, NT, 0), ("prg", prg, B_prg, RC * 128, SEQ, 1),
                                                      ("prx", prx, B_prx, RC * 128, NT, 0), ("pgt", pgt, B_pgt, 2 * D, SEQ, 1)):
                o = dbgout(nm, [rows, cols])
                B_o = Buf("dbg" + nm)
                for kc in range(rows // 128):
                    if isbf:
                        P.dma('sp', tb[:, 0:cols], src[kc * 128:(kc + 1) * 128, :], B_tb, B_src)
                        P.op('dve', 'tensor_copy', KW(out=tf[:, 0:cols], in_=tb[:, 0:cols]), reads=[B_tb], writes=[B_tf])
                    else:
                        P.dma('sp', tf[:, 0:cols], src[kc * 128:(kc + 1) * 128, :], B_tf, B_src)
                    P.dma('sp', o[kc * 128:(kc + 1) * 128, :], tf[:, 0:cols], B_o, B_tf)
            xb2, B_xb2 = sbuf(glob, "dbg_xtb", [128, C.DI], BF16)
            xf2, B_xf2 = sbuf(glob, "dbg_xtf", [128, C.DI])
            o = dbgout("x_tm", [NT, C.DI])
            B_o = Buf("dbgxtm")
            for c in range(NT // 128):
                P.dma('sp', xb2[:], x_tm[c * 128:(c + 1) * 128, :], B_xb2, B_xtm)
                P.op('dve', 'tensor_copy', KW(out=xf2[:], in_=xb2[:]), reads=[B_xb2], writes=[B_xf2])
                P.dma('sp', o[c * 128:(c + 1) * 128, :], xf2[:], B_o, B_xf2)


        NCH, NCL, HB, NQ = C.NCH, C.NCL, C.HB, C.NQ
        EW = E * 64
        if upto >= 4:
            with contextlib.ExitStack() as ph:
                def tm(name, dt=F32):
                    return sbuf(ph, name, [128, NCH, 128], dt)
                lb_tm = [tm("lb_tm_f"), tm("lb_tm_b")]
                w_tm = [tm("w_tm_f", BF16), tm("w_tm_b", BF16)]
                etot = [tm("etot_f"), tm("etot_b")]
                csS = [[sbuf(ph, "csS_%d_%d" % (d_, i_), [128, NT], BF16) for i_ in range(2)] for d_ in range(2)]
                maskq, B_maskq = sbuf(ph, "maskq", [128, H // HB, HB])
                drw, B_drw = sbuf(ph, "drw", [128, 128])
                pA, B_pA = psum(ph, "p4A", [128, 1024])
                pB_, B_pB_ = psum(ph, "p4B", [128, 1024])
                pY, B_pY = psum(ph, "p4Y", [128, 512])
                pS, B_pS = psum(ph, "p4S", [128, 512])
                pM, B_pM = psum(ph, "p4M", [128, 512])
                maskE, B_maskE = sbuf(ph, "maskE", [128, 8, E])
                P.op('pool', 'memset', KW(maskE[:], 1.0), writes=[B_maskE])
                P.op('pool', 'affine_select', KW(out=maskE[:], in_=maskE[:], pattern=[[-E, 8], [-1, E]], base=0, channel_multiplier=1,
                                                 compare_op=ALU.is_equal, fill=0.0), reads=[B_maskE], writes=[B_maskE])
                P.op('pool', 'memset', KW(maskq[:], 1.0), writes=[B_maskq])
                P.op('pool', 'affine_select', KW(out=maskq[:], in_=maskq[:], pattern=[[-HB, H // HB], [-1, HB]], base=0, channel_multiplier=1,
                                                 compare_op=ALU.is_equal, fill=0.0), reads=[B_maskq], writes=[B_maskq])
                P.dma('sp', drw[:], drow[0:1, :].partition_broadcast(128), B_drw, B_w)
                amask = [sbuf(ph, "amask_f", [128, 128]), sbuf(ph, "amask_b", [128, 128])]
                P.op('pool', 'memset', KW(amask[0][0][:], 0.0), writes=[amask[0][1]])
                P.op('pool', 'affine_select', KW(out=amask[0][0][:], in_=amask[0][0][:], pattern=[[1, 128]], base=0, channel_multiplier=-1,
                                                 compare_op=ALU.is_ge, fill=NEG), reads=[amask[0][1]], writes=[amask[0][1]])
                P.op('pool', 'memset', KW(amask[1][0][:], NEG), writes=[amask[1][1]])
                P.op('pool', 'affine_select', KW(out=amask[1][0][:], in_=amask[1][0][:], pattern=[[1, 128]], base=0, channel_multiplier=-1,
                                                 compare_op=ALU.is_gt, fill=0.0), reads=[amask[1][1]], writes=[amask[1][1]])
                amask8 = [sbuf(ph, "amask8_%d" % d_, [128, HB, 128], BF16) for d_ in range(2)]
                for d_ in range(2):
                    P.op('dve', 'tensor_copy', KW(out=amask8[d_][0][:], in_=amask[d_][0][:].unsqueeze(1).to_broadcast([128, HB, 128])),
                         reads=[amask[d_][1]], writes=[amask8[d_][1]])
                with contextlib.ExitStack() as pp:
                    cs_tm = [sbuf(pp, "cs_tm_f", [128, NCH, 128]), sbuf(pp, "cs_tm_b", [128, NCH, 128])]
                    dt_tm = [sbuf(pp, "dt_tm_f", [128, NCH, 128], BF16), sbuf(pp, "dt_tm_b", [128, NCH, 128], BF16)]
                    dtT = [sbuf(pp, "dtT_f", [128, NT]), sbuf(pp, "dtT_b", [128, NT])]
                    daT = [sbuf(pp, "daT_f", [128, NT]), sbuf(pp, "daT_b", [128, NT])]
                    da_tm = [sbuf(pp, "da_tm_f", [128, NCH, 128]), sbuf(pp, "da_tm_b", [128, NCH, 128])]
                    aneg, B_aneg = sbuf(pp, "aneg", [128, 2])
                    csT = [sbuf(pp, "csT_f", [128, NT]), sbuf(pp, "csT_b", [128, NT])]
                    spl, B_spl = sbuf(pp, "spl", [128, NT])
                    P.op('act', 'activation', KW(out=aneg[:], in_=V("alog", 0, 2), func=AF.Exp), reads=[B_vec], writes=[B_aneg])
                    P.op('dve', 'tensor_scalar_mul', KW(out=aneg[:], in0=aneg[:], scalar1=-1.0), reads=[B_aneg], writes=[B_aneg])
                    for d in range(2):
                        t_, B_t = dtT[d]
                        a_, B_a = daT[d]
                        c_, B_c = csT[d]
                        P.dma('sp', t_[:], pdt[d * 128:(d + 1) * 128, :], B_t, B_pdt)
                        P.op('dve', 'tensor_scalar_mul', KW(out=a_[:], in0=t_[:], scalar1=aneg[:, d:d + 1]), reads=[B_t, B_aneg], writes=[B_a])
                        for c in range(NCH):
                            sl = slice(c * 128, (c + 1) * 128)
                            if d == 0:
                                P.op('dve', 'tensor_tensor_scan', KW(out=c_[:, sl], data0=ones32[:], data1=a_[:, sl], initial=0.0, op0=ALU.mult, op1=ALU.add),
                                     reads=[B_a, B_ones32], writes=[B_c])
                            else:
                                P.op('dve', 'tensor_tensor_scan', KW(out=c_[:, sl][:, ::-1], data0=ones32[:], data1=a_[:, sl][:, ::-1], initial=0.0,
                                                                     op0=ALU.mult, op1=ALU.add), reads=[B_a, B_ones32], writes=[B_c])
                        (h0_, B_h0), (h1_, B_h1) = csS[d]
                        P.op('act', 'activation', KW(out=h0_[:], in_=c_[:], func=AF.Copy), reads=[B_c], writes=[B_h0])
                        P.op('dve', 'tensor_tensor', KW(out=spl[:], in0=c_[:], in1=h0_[:], op=ALU.subtract), reads=[B_c, B_h0], writes=[B_spl])
                        P.op('act', 'activation', KW(out=h1_[:], in_=spl[:], func=AF.Copy), reads=[B_spl], writes=[B_h1])
                        for c in range(NCH):
                            sl = slice(c * 128, (c + 1) * 128)
                            for (src, B_src, (dst, B_dst)) in ((c_, B_c, cs_tm[d]), (t_, B_t, dt_tm[d]), (a_, B_a, da_tm[d])):
                                P.op('pe', 'transpose', KW(pM[:, 0:128], src[:, sl], ident32[:]), reads=[B_src, B_id32], writes=[B_pM])
                                P.op('act', 'activation', KW(out=dst[:, c, :], in_=pM[:, 0:128], func=AF.Copy), reads=[B_pM], writes=[B_dst])
                            P.op('pe', 'matmul', KW(pM[:, 128:256], lhsT=ones32[:], rhs=da_tm[d][0][:, c, :], start=True, stop=True),
                                 reads=[da_tm[d][1], B_ones32], writes=[B_pM])
                            e_, B_e = etot[d]
                            w_, B_w_ = w_tm[d]
                            P.op('act', 'activation', KW(out=e_[:, c, :], in_=pM[:, 128:256], func=AF.Exp), reads=[B_pM], writes=[B_e])
                            tw = spl[:, 0:128]
                            P.op('dve', 'tensor_tensor', KW(out=tw, in0=pM[:, 128:256], in1=cs_tm[d][0][:, c, :], op=ALU.subtract),
                                 reads=[B_pM, cs_tm[d][1]], writes=[B_spl])
                            P.op('act', 'activation', KW(out=tw, in_=tw, func=AF.Exp), reads=[B_spl], writes=[B_spl])
                            P.op('dve', 'tensor_tensor', KW(out=w_[:, c, :], in0=tw, in1=dt_tm[d][0][:, c, :], op=ALU.mult),
                                 reads=[B_spl, dt_tm[d][1]], writes=[B_w_])
                            lb_, B_lb = lb_tm[d]
                            P.op('act', 'activation', KW(out=lb_[:, c, :], in_=dt_tm[d][0][:, c, :], func=AF.Ln), reads=[dt_tm[d][1]], writes=[B_lb])
                            P.op('dve', 'tensor_tensor', KW(out=lb_[:, c, :], in0=lb_[:, c, :], in1=cs_tm[d][0][:, c, :], op=ALU.subtract),
                                 reads=[B_lb, cs_tm[d][1]], writes=[B_lb])
                    P.barrier()

                BTg, B_BTg = sbuf(ph, "BTg", [128, NT], BF16)
                CTg, B_CTg = sbuf(ph, "CTg", [128, NT], BF16)
                xtm = [sbuf(ph, "xtm%d" % i, [128, EW], BF16) for i in range(2)]
                btm = [sbuf(ph, "btm%d" % i, [128, 128], BF16) for i in range(2)]
                szc = [sbuf(ph, "szc%d" % i, [128, E // 2, 128], BF16) for i in range(2)]
                sbi = [sbuf(ph, "sbi%d" % i, [128, EW], BF16) for i in range(2)]
                xs, B_xs = sbuf(ph, "xs", [128, EW], BF16)
                S32 = [sbuf(ph, "S32f", [128, EW]), sbuf(ph, "S32b", [128, EW])]
                Sfb, B_Sfb = sbuf(ph, "Sfb", [128, EW], BF16)
                Sbb = [sbuf(ph, "Sbb%d" % i, [128, EW], BF16) for i in range(2)]
                DIg, B_DIg = sbuf(ph, "DIg", [128, E, 128], BF16)
                selg, B_selg = sbuf(ph, "selg", [128, E, 128], BF16)
                CBt, B_CBt = sbuf(ph, "CBt", [128, 128])
                Dd = [[sbuf(ph, "Dd%d_%d" % (p_, i), [128, HB, 128]) for i in range(2)] for p_ in range(2)]
                B_Dh = [[[Buf("Dh%d_%d_%d" % (p_, i, h_)) for h_ in range(HB)] for i in range(2)] for p_ in range(2)]
                Xd = [[sbuf(ph, "Xd%d_%d" % (p_, i), [128, HB, 128], BF16) for i in range(2)] for p_ in range(2)]
                Mt = [sbuf(ph, "Mt%d" % p_, [128, HB, 128], BF16) for p_ in range(2)]
                Csd = [[sbuf(ph, "Cs%d_%d" % (p_, i), [128, HB, 128], BF16) for i in range(2)] for p_ in range(2)]
                yg, B_yg = sbuf(ph, "yg", [128, E // 2, 128])
                ysq, B_ysq = sbuf(ph, "ysq", [128, E // 2, 128], BF16)
                ynb, B_ynb = sbuf(ph, "ynb", [128, E // 2, 128], BF16)
                rs4, B_rs4 = sbuf(ph, "rs4", [128, 128])
                cnt4 = {'ld': 0, 'sb': 0}

                def bc_l(ap2):
                    return ap2.unsqueeze(2).to_broadcast([128, ap2.shape[1], 128])

                def bc_h(ap2, n):
                    return ap2.unsqueeze(1).to_broadcast([128, n, 128])

                def load_chunk(g, c):
                    i = cnt4['ld'] % 2
                    cnt4['ld'] += 1
                    x_, B_x = xtm[i]
                    b_, B_b = btm[i]
                    P.dma('sp', x_[:], x_tm[c * 128:(c + 1) * 128, g * EW:(g + 1) * EW], B_x, B_xtm)
                    P.dma('sp', b_[:], B_tm[c * 128:(c + 1) * 128, g * 128:(g + 1) * 128], B_b, B_Btm)
                    return x_, B_x, b_, B_b

                def state_update(g, d, c, x_, B_x, b_, B_b):
                    S, B_S = S32[d]
                    hs = slice(g * E, (g + 1) * E)
                    P.op('dve', 'tensor_tensor', KW(out=xs[:].rearrange("p (h q) -> p h q", q=64), in0=x_[:].rearrange("p (h q) -> p h q", q=64),
                                                    in1=w_tm[d][0][:, c, hs].unsqueeze(2).to_broadcast([128, E, 64]), op=ALU.mult),
                         reads=[B_x, w_tm[d][1]], writes=[B_xs])
                    for c0 in range(0, EW, 512):
                        n = min(512, EW - c0)
                        P.op('pe', 'matmul', KW(pS[:, 0:n], lhsT=b_[:], rhs=xs[:, c0:c0 + n], start=True, stop=True), reads=[B_b, B_xs], writes=[B_pS])
                        nh = n // 64
                        h0 = g * E + c0 // 64
                        P.op('dve', 'tensor_tensor', KW(out=S[:, c0:c0 + n].rearrange("p (h q) -> p h q", q=64),
                                                        in0=S[:, c0:c0 + n].rearrange("p (h q) -> p h q", q=64),
                                                        in1=etot[d][0][:, c, h0:h0 + nh].unsqueeze(2).to_broadcast([128, nh, 64]), op=ALU.mult),
                             reads=[B_S, etot[d][1]], writes=[B_S])
                        P.op('dve', 'tensor_tensor', KW(out=S[:, c0:c0 + n], in0=S[:, c0:c0 + n], in1=pS[:, 0:n], op=ALU.add), reads=[B_S, B_pS], writes=[B_S])

                pre_jobs = []
                if PRECAST:
                    def _pj(out_ap, in_ap, key):
                        pre_jobs.append(lambda: P.dma('pool', out_ap, in_ap, B_cache[key], B_w, max_dma_last_dim=4096))
                    for oc in range(KD):
                        _pj(wsoc[oc], wso_t[oc], "wsoc")
                        _pj(wroc[oc], wro_t[oc], "wroc")
                        _pj(woc[oc], wo_t[oc], "woc")
                    for fc in range(KF):
                        _pj(wupc[1][fc][:, 0:KD * 128], wup_t[1][fc], "wupc1")
                        _pj(wupc[1][fc][:, KD * 128:2 * KD * 128], wup_t[1][KF + fc], "wupc1")
                    for oc in range(KD):
                        _pj(wdnc[1][oc], wdn_t[1][oc], "wdnc1")
                pre_per = -(-len(pre_jobs) // (8 * NCL * NQ)) if pre_jobs else 0
                _p4stop = _os.environ.get('P4STOP', '')
                _pe4 = _os.environ.get('P4POOL', 'pool')
                _p4d = _os.environ.get('P4D', '').split(',')
                _L = int(_os.environ.get('P4L', '99'))
                for g in range(8 if _p4stop != 'prep' else 0):
                    P.dma('sp', BTg[:], pB[g * 128:(g + 1) * 128, :], B_BTg, B_pB)
                    P.dma('sp', CTg[:], pC[g * 128:(g + 1) * 128, :], B_CTg, B_pC)
                    P.op('dve', 'tensor_tensor', KW(out=DIg[:], in0=bc_h(ident32[:], E), in1=bc_l(drw[:, g * E:(g + 1) * E]), op=ALU.mult),
                         reads=[B_id32, B_drw], writes=[B_DIg])
                    P.op('dve', 'tensor_copy', KW(out=selg[:], in_=bc_l(maskE[:, g, :])), reads=[B_maskE], writes=[B_selg])
                    for d in range(2):
                        P.op('dve', 'memset', KW(S32[d][0][:], 0.0), writes=[S32[d][1]])
                    for c in range(NCL, NCH):
                        state_update(g, 0, c, *load_chunk(g, c))
                    for c in range(NCH - 1, NCL - 1, -1):
                        state_update(g, 1, c, *load_chunk(g, c))
                    for c in range(NCL - 1, -1, -1) if _p4stop != 'ctx' else []:
                        sb_, B_sb = Sbb[cnt4['sb'] % 2]
                        cnt4['sb'] += 1
                        P.op('act', 'activation', KW(out=sb_[:], in_=S32[1][0][:], func=AF.Copy), reads=[S32[1][1]], writes=[B_sb])
                        P.dma('sp', sbin[g, c], sb_[:], B_sbin, B_sb)
                        state_update(g, 1, c, *load_chunk(g, c))
                    def stage_A(it):
                        c, q = divmod(it, NQ)
                        par = it % 2
                        sl = slice(c * 128, (c + 1) * 128)
                        Q = g * NQ + q
                        hs = slice(Q * HB, (Q + 1) * HB)
                        pps = ((pA, B_pA), (pB_, B_pB_))
                        for d in range(2):
                            pp_, B_pp = pps[d]
                            for hh in range(HB):
                                for i3 in range(2):
                                    P.op('pe', 'matmul', KW(pp_[:, hh * 128:(hh + 1) * 128], lhsT=selg[:, q * HB + hh, :], rhs=csS[d][i3][0][:, sl],
                                                            start=(i3 == 0 and hh % 4 == 0), stop=(i3 == 1), skip_group_check=True),
                                         reads=[B_selg, csS[d][i3][1]], writes=[B_pp], inc=(hh == HB - 1 and i3 == 1))
                        ppv = [pps[d][0][:, 0:HB * 128].rearrange("p (h l) -> p h l", l=128) for d in range(2)]
                        for d in range(2):
                            P.op('act', 'activation', KW(out=Xd[par][d][0][:], in_=ppv[d], func=AF.Exp), reads=[pps[d][1]], writes=[Xd[par][d][1]])
                        for d in range(2):
                            pp_, B_pp = pps[d]
                            af = amask8[d][0][:].rearrange("p h l -> p (h l)")
                            for c0 in range(0, HB * 128, 512):
                                n = min(512, HB * 128 - c0)
                                P.op('pe', 'matmul', KW(pp_[:, c0:c0 + n], lhsT=identb[:], rhs=af[:, c0:c0 + n], start=False, stop=True, skip_group_check=True),
                                     reads=[B_idb, amask8[d][1]], writes=[B_pp])
                        for hh in range(HB):
                            for d in range(2):
                                hcol = Q * HB + hh
                                P.op('act', 'activation', KW(out=Dd[par][d][0][:, hh, :], in_=pps[d][0][:, hh * 128:(hh + 1) * 128], func=AF.Exp,
                                                             bias=lb_tm[d][0][:, c, hcol:hcol + 1], scale=1.0),
                                     reads=[pps[d][1], lb_tm[d][1]], writes=[B_Dh[par][d][hh]])
                        for d in range(2):
                            P.op('pool', 'tensor_tensor', KW(out=Csd[par][d][0][:], in0=Xd[par][d][0][:], in1=bc_h(CTg[:, sl], HB), op=ALU.mult),
                                 reads=[Xd[par][d][1], B_CTg], writes=[Csd[par][d][1]])
                        for _ in range(pre_per):
                            if pre_jobs:
                                pre_jobs.pop(0)()

                    cur = {}

                    def stage_B(it):
                        c, q = divmod(it, NQ)
                        par = it % 2
                        sl = slice(c * 128, (c + 1) * 128)
                        if q == 0:
                            x_, B_x, b_, B_b = load_chunk(g, c)
                            i = c % 2
                            sz_, B_sz = szc[i]
                            si_, B_si = sbi[i]
                            P.dma('sp', sz_[:], pz[g * (E // 2) * 128:(g + 1) * (E // 2) * 128, sl].rearrange("(j p) t -> p j t", p=128), B_sz, B_pz)
                            P.dma('sp', si_[:], sbin[g, c], B_si, B_sbin)
                            P.op('act', 'activation', KW(out=Sfb[:], in_=S32[0][0][:], func=AF.Copy), reads=[S32[0][1]], writes=[B_Sfb])
                            P.op('pe', 'matmul', KW(pM[:, 0:128], lhsT=BTg[:, sl], rhs=CTg[:, sl], start=True, stop=True), reads=[B_BTg, B_CTg], writes=[B_pM])
                            P.op('act', 'activation', KW(out=CBt[:], in_=pM[:, 0:128], func=AF.Copy), reads=[B_pM], writes=[B_CBt])
                            cur['v'] = (x_, B_x, b_, B_b, sz_, B_sz, si_, B_si)
                        x_, B_x, b_, B_b, sz_, B_sz, si_, B_si = cur['v']
                        D0, B_D0 = Dd[par][0]
                        D1, B_D1 = Dd[par][1]
                        M_, B_M = Mt[par]
                        P.op('dve', 'tensor_tensor', KW(out=D0[:], in0=D0[:], in1=D1[:], op=ALU.add), reads=B_Dh[par][0] + B_Dh[par][1], writes=B_Dh[par][0])
                        P.op('dve', 'tensor_tensor', KW(out=M_[:], in0=D0[:], in1=bc_h(CBt[:], HB), op=ALU.mult), reads=B_Dh[par][0] + [B_CBt], writes=[B_M])
                        for hh in range(HB):
                            j, e = hh // 2, hh % 2
                            col = (q * HB + hh) * 64
                            o_ap = pY[64 * e:64 * e + 64, j * 128:(j + 1) * 128]
                            P.op('pe', 'matmul', KW(o_ap, lhsT=x_[:, col:col + 64], rhs=M_[:, hh, :], start=True, stop=False),
                                 reads=[B_x, B_M], writes=[B_pY], inc=False)
                            P.op('pe', 'matmul', KW(o_ap, lhsT=x_[:, col:col + 64], rhs=DIg[:, q * HB + hh, :], start=False, stop=False),
                                 reads=[B_x, B_DIg], writes=[B_pY], inc=False)
                            P.op('pe', 'matmul', KW(o_ap, lhsT=Sfb[:, col:col + 64], rhs=Csd[par][0][0][:, hh, :], start=False, stop=False),
                                 reads=[B_Sfb, Csd[par][0][1]], writes=[B_pY], inc=False)
                            P.op('pe', 'matmul', KW(o_ap, lhsT=si_[:, col:col + 64], rhs=Csd[par][1][0][:, hh, :], start=False, stop=True),
                                 reads=[B_si, Csd[par][1][1]], writes=[B_pY], inc=(hh == HB - 1))
                        j0 = q * (HB // 2)
                        P.op('dve', 'tensor_tensor', KW(out=yg[:, j0:j0 + HB // 2, :], in0=pY[:, 0:(HB // 2) * 128].rearrange("p (j l) -> p j l", l=128),
                                                        in1=sz_[:, j0:j0 + HB // 2, :], op=ALU.mult), reads=[B_pY, B_sz], writes=[B_yg])
                        if q == NQ - 1:
                            P.op('act', 'activation', KW(out=ysq[:], in_=yg[:], func=AF.Square), reads=[B_yg], writes=[B_ysq])
                            for j in range(E // 2):
                                P.op('pe', 'matmul', KW(pM[:, 256:384], lhsT=ones_bf[:], rhs=ysq[:, j, :], start=(j == 0), stop=(j == E // 2 - 1)),
                                     reads=[B_ysq, B_ones], writes=[B_pM], inc=(j == E // 2 - 1))
                            P.op('act', 'activation', KW(out=rs4[:], in_=pM[:, 256:384], func=AF.Sqrt, scale=1.0 / EW, bias=EPS), reads=[B_pM], writes=[B_rs4])
                            P.op('dve', 'reciprocal', KW(out=rs4[:], in_=rs4[:]), reads=[B_rs4], writes=[B_rs4])
                            for j in range(E // 2):
                                P.op('dve', 'scalar_tensor_tensor', KW(out=ynb[:, j, :], in0=yg[:, j, :], scalar=V("sng", g * (E // 2) + j), in1=rs4[:],
                                                                       op0=ALU.mult, op1=ALU.mult), reads=[B_yg, B_vec, B_rs4], writes=[B_ynb])
                            P.dma('sp', ynT[g * (E // 2) * 128:(g + 1) * (E // 2) * 128, sl].rearrange("(j p) t -> p j t", p=128), ynb[:], B_ynT, B_ynb)
                            state_update(g, 0, c, x_, B_x, b_, B_b)

                    NIT = NCL * NQ
                    stage_A(0)
                    if 'p4dbg' in dbg and g == 0:
                        dA, B_dA = sbuf(ph, "dbgsb_pA", [128, 1024])
                        P.op('dve', 'tensor_copy', KW(out=dA[:, 0:512], in_=pA[:, 0:512]), reads=[B_pA], writes=[B_dA])
                        P.op('dve', 'tensor_copy', KW(out=dA[:, 512:1024], in_=Dd[0][0][0][:].rearrange("p h l -> p (h l)")[:, 0:512]), reads=[Dd[0][0][1]], writes=[B_dA])
                        o = dbgout("pA", [128, 1024])
                        P.dma('sp', o[:, :], dA[:], Buf("dbgpA"), B_dA)
                    for it in range(NIT):
                        if it + 1 < NIT:
                            stage_A(it + 1)
                        stage_B(it)
                while pre_jobs:
                    pre_jobs.pop(0)()
                P.barrier()
        if 'yn' in dbg:
            tb, B_tb = sbuf(glob, "dbg_ynb", [128, SEQ], BF16)
            tf, B_tf = sbuf(glob, "dbg_ynf", [128, SEQ])
            o = dbgout("yn", [C.DI, SEQ])
            B_o = Buf("dbgyn")
            for kc in range(XC):
                P.dma('sp', tb[:], ynT[kc * 128:(kc + 1) * 128, :], B_tb, B_ynT)
                P.op('dve', 'tensor_copy', KW(out=tf[:], in_=tb[:]), reads=[B_tb], writes=[B_tf])
                P.dma('sp', o[kc * 128:(kc + 1) * 128, :], tf[:], B_o, B_tf)


        NBC = C.NBC
        if upto >= 5:
            with contextlib.ExitStack() as ph:
                xr, B_xr = sbuf(ph, "xr", [128, NBC, NT])
                xrb, B_xrb = sbuf(ph, "xrb", [128, NBC, NT], BF16)
                rrow, B_rrow = sbuf(ph, "rrow", [128, NT])
                irow, B_irow = sbuf(ph, "irow", [128, NT])
                arow, B_arow = sbuf(ph, "arow", [128, NT])
                brow, B_brow = sbuf(ph, "brow", [128, NT])
                hrow = [sbuf(ph, "hrow%d" % i, [128, NT]) for i in range(2)]
                gate, B_gate = sbuf(ph, "gate", [128, SEQ], BF16)
                orow, B_orow = sbuf(ph, "orow", [128, SEQ], BF16)
                wg = [sbuf(ph, "rgwt%d" % i, [128, NBC * 128], BF16) for i in range(4)]
                cc1, B_cc1 = sbuf(ph, "cc1", [128, 2 * RC])
                cc2, B_cc2 = sbuf(ph, "cc2", [128, 2 * RC])
                nba, B_nba = sbuf(ph, "nba", [128, 2 * RC])
                nbx, B_nbx = sbuf(ph, "nbx", [128, 2 * RC])
                pss = [psum(ph, "p5ps%d" % i, [128, 512]) for i in range(8)]
                st5 = {'ps': 0, 'w': 0}
                P.op('act', 'activation', KW(out=cc1[:], in_=V("rlam", 0, 2 * RC), func=AF.Exp, scale=-1.0), reads=[B_vec], writes=[B_cc1])
                P.op('act', 'activation', KW(out=cc1[:], in_=cc1[:], func=AF.Ln, bias=1.0, scale=1.0), reads=[B_cc1], writes=[B_cc1])
                P.op('dve', 'tensor_scalar_mul', KW(out=cc2[:], in0=cc1[:], scalar1=-16.0), reads=[B_cc1], writes=[B_cc2])
                P.op('dve', 'tensor_scalar_mul', KW(out=cc1[:], in0=cc1[:], scalar1=-8.0), reads=[B_cc1], writes=[B_cc1])
                P.op('dve', 'tensor_scalar_mul', KW(out=nba[:], in0=V("rba", 0, 2 * RC), scalar1=-1.0), reads=[B_vec], writes=[B_nba])
                P.op('dve', 'tensor_scalar_mul', KW(out=nbx[:], in0=V("rbx", 0, 2 * RC), scalar1=-1.0), reads=[B_vec], writes=[B_nbx])
                tsl = slices_of(SEQ, 512) + [(SEQ + c0, n) for (c0, n) in slices_of(CTX, 512)]
                ada_todo = list(range(NADA0, 9 * KD))
                ada_per = -(-len(ada_todo) // (16 * NBC)) if ada_todo else 0
                if ada_todo:
                    wada5 = [sbuf(ph, "wada5_%d" % i, [128, KD * 128], BF16) for i in range(3)]
                for k in range(16):
                    for ic in range(NBC):
                        P.dma('sp', xr[:, ic, :], prx[(k * NBC + ic) * 128:(k * NBC + ic + 1) * 128, :], B_xr, B_prx)
                    P.op('act', 'activation', KW(out=xrb[:].rearrange("p a t -> p (a t)"), in_=xr[:].rearrange("p a t -> p (a t)"), func=AF.Copy),
                         reads=[B_xr], writes=[B_xrb])
                    for jc in range(NBC):
                        ch = k * NBC + jc
                        P.dma('sp', gate[:], prg[ch * 128:(ch + 1) * 128, :], B_gate, B_prg)
                        for d in range(2):
                            col = d * RC + ch
                            wa, B_wa = wg[st5['w'] % 4]
                            wx, B_wx = wg[(st5['w'] + 1) % 4]
                            st5['w'] += 2
                            P.dma('pool', wa[:], rgw_t[((d * 2 + 0) * 16 + k) * NBC + jc], B_wa, B_w, max_dma_last_dim=4096)
                            P.dma('pool', wx[:], rgw_t[((d * 2 + 1) * 16 + k) * NBC + jc], B_wx, B_w, max_dma_last_dim=4096)
                            for (c0, n) in tsl:
                                pa, B_pa = pss[st5['ps'] % 8]
                                px_, B_px = pss[(st5['ps'] + 1) % 8]
                                st5['ps'] += 2
                                for (w_, B_w_, p_, B_p) in ((wa, B_wa, pa, B_pa), (wx, B_wx, px_, B_px)):
                                    for ic in range(NBC):
                                        P.op('pe', 'matmul', KW(p_[:, 0:n], lhsT=w_[:, ic * 128:(ic + 1) * 128], rhs=xrb[:, ic, c0:c0 + n],
                                                                start=(ic == 0), stop=(ic == NBC - 1)), reads=[B_w_, B_xrb], writes=[B_p], inc=(ic == NBC - 1))
                                P.op('act', 'activation', KW(out=rrow[:, c0:c0 + n], in_=pa[:, 0:n], func=AF.Sigmoid, bias=V("rba", col)),
                                     reads=[B_pa, B_vec], writes=[B_rrow])
                                P.op('act', 'activation', KW(out=irow[:, c0:c0 + n], in_=px_[:, 0:n], func=AF.Sigmoid, bias=V("rbx", col)),
                                     reads=[B_px, B_vec], writes=[B_irow])
                            P.op('act', 'activation', KW(out=arow[:], in_=rrow[:], func=AF.Exp, scale=cc1[:, col:col + 1]), reads=[B_rrow, B_cc1], writes=[B_arow])
                            P.op('act', 'activation', KW(out=brow[:], in_=rrow[:], func=AF.Exp, scale=cc2[:, col:col + 1]), reads=[B_rrow, B_cc2], writes=[B_brow])
                            P.op('act', 'activation', KW(out=brow[:], in_=brow[:], func=AF.Sqrt, scale=-1.0, bias=1.0), reads=[B_brow], writes=[B_brow])
                            P.op('dve', 'tensor_tensor', KW(out=brow[:], in0=brow[:], in1=irow[:], op=ALU.mult), reads=[B_brow, B_irow], writes=[B_brow])
                            P.op('dve', 'tensor_tensor', KW(out=brow[:], in0=brow[:], in1=xr[:, jc, :], op=ALU.mult), reads=[B_brow, B_xr], writes=[B_brow])
                            h_, B_h = hrow[d]
                            if d == 0:
                                P.op('dve', 'tensor_tensor_scan', KW(out=h_[:, SEQ:NT], data0=arow[:, SEQ:NT], data1=brow[:, SEQ:NT], initial=0.0,
                                                                     op0=ALU.mult, op1=ALU.add), reads=[B_arow, B_brow], writes=[B_h])
                                P.op('dve', 'tensor_tensor_scan', KW(out=h_[:, 0:SEQ], data0=arow[:, 0:SEQ], data1=brow[:, 0:SEQ], initial=h_[:, NT - 1:NT],
                                                                     op0=ALU.mult, op1=ALU.add), reads=[B_arow, B_brow, B_h], writes=[B_h])
                            else:
                                P.op('dve', 'tensor_tensor_scan', KW(out=h_[:, SEQ:NT][:, ::-1], data0=arow[:, SEQ:NT][:, ::-1], data1=brow[:, SEQ:NT][:, ::-1],
                                                                     initial=0.0, op0=ALU.mult, op1=ALU.add), reads=[B_arow, B_brow], writes=[B_h])
                                P.op('dve', 'tensor_tensor_scan', KW(out=h_[:, 0:SEQ][:, ::-1], data0=arow[:, 0:SEQ][:, ::-1], data1=brow[:, 0:SEQ][:, ::-1],
                                                                     initial=h_[:, SEQ:SEQ + 1], op0=ALU.mult, op1=ALU.add), reads=[B_arow, B_brow, B_h], writes=[B_h])
                        h0, B_h0 = hrow[0]
                        h1_, B_h1_ = hrow[1]
                        P.op('dve', 'tensor_tensor', KW(out=h0[:, 0:SEQ], in0=h0[:, 0:SEQ], in1=h1_[:, 0:SEQ], op=ALU.add), reads=[B_h0, B_h1_], writes=[B_h0])
                        P.op('dve', 'tensor_tensor', KW(out=orow[:].rearrange("p (r c) -> p r c", c=C.GW),
                                                        in0=h0[:, 0:SEQ].rearrange("p (c r) -> p r c", r=C.ROWS),
                                                        in1=gate[:].rearrange("p (r c) -> p r c", c=C.GW), op=ALU.mult),
                             reads=[B_h0, B_gate], writes=[B_orow])
                        P.dma('sp', rgyT[ch * 128:(ch + 1) * 128, :], orow[:], B_rgyT, B_orow)
                        for _ in range(ada_per):
                            if ada_todo:
                                oc_ = ada_todo.pop(0)
                                wt_, B_wt_ = wada5[oc_ % 3]
                                pt_, B_pt_ = pss[st5['ps'] % 8]
                                st5['ps'] += 1
                                ada_chunk(oc_, wt_, B_wt_, pt_, B_pt_)
                if NADA0 < 9 * KD:
                    cf_tables(1, 'c')
                    cf_tables(2, 'sc')
                P.barrier()
        if 'rgy' in dbg:
            tb, B_tb = sbuf(glob, "dbg_rgb", [128, SEQ], BF16)
            tf, B_tf = sbuf(glob, "dbg_rgf", [128, SEQ])
            o = dbgout("rgy", [RC * 128, SEQ])
            B_o = Buf("dbgrgy")
            for kc in range(RC):
                P.dma('sp', tb[:], rgyT[kc * 128:(kc + 1) * 128, :], B_tb, B_rgyT)
                P.op('dve', 'tensor_copy', KW(out=tf[:], in_=tb[:]), reads=[B_tb], writes=[B_tf])
                P.dma('sp', o[kc * 128:(kc + 1) * 128, :], tf[:], B_o, B_tf)


        T6 = 512
        if upto >= 6:
            with contextlib.ExitStack() as ph:
                NA = max(XC, RC)
                bufA, B_A = sbuf(ph, "p6A", [128, NA * T6], BF16)
                mix, B_mix = sbuf(ph, "p6mix", [128, KD, T6], BF16)
                mT, B_mT = sbuf(ph, "p6mT", [128, KD, T6], BF16)
                WS = max(XC, RC, KD) * 128
                wsl = [sbuf(ph, "p6w%d" % i, [128, WS], BF16) for i in range(2)]
                gch = [sbuf(ph, "p6g%d" % i, [128, T6], BF16) for i in range(2)]
                tmp = [sbuf(ph, "p6t%d" % i, [128, T6]) for i in range(2)]
                sq = [sbuf(ph, "p6sq%d" % i, [128, T6], BF16) for i in range(2)]
                hch = [sbuf(ph, "p6h%d" % i, [128, T6]) for i in range(3)]
                rstd, B_rstd = sbuf(ph, "p6rstd", [128, T6])
                pss = [psum(ph, "p6ps%d" % i, [128, 512]) for i in range(8)]
                st6 = {'ps': 0, 'w': 0, 'g': 0}
                A3 = bufA[:].rearrange("p (k t) -> p k t", t=T6)
                h2b = bufA[:].bitcast(F32).rearrange("p (k t) -> p k t", t=T6)

                def nps():
                    r = pss[st6['ps'] % 8]
                    st6['ps'] += 1
                    return r

                def nw():
                    r = wsl[st6['w'] % 2]
                    st6['w'] += 1
                    return r

                def ng_():
                    r = gch[st6['g'] % 2]
                    st6['g'] += 1
                    return r
                B_c6 = B_cache
                co, B_co = cf["co_1"]
                s1, B_s1 = cf["s1_2"]
                s2, B_s2 = cf["s2_2"]
                for t0 in range(0, SEQ, T6):
                    T = min(T6, SEQ - t0)
                    P.dma('sp', A3[:, 0:XC, 0:T], ynT[:, t0:t0 + T].rearrange("(k p) t -> p k t", p=128), B_A, B_ynT)
                    for dc in range(KD):
                        wt, B_wt = nw()
                        if (t0 == 0 or not CACHE_W) and not (PRECAST and upto >= 4):
                            P.dma('pool', wt[:, 0:XC * 128], wso_t[dc], B_wt, B_w, max_dma_last_dim=4096)
                            if CACHE_W and SEQ > T6:
                                P.dma('sp', wsoc[dc], wt[:, 0:XC * 128], B_c6["wsoc"], B_wt)
                        else:
                            P.dma('pool', wt[:, 0:XC * 128], wsoc[dc], B_wt, B_c6["wsoc"])
                        g_, B_g = ng_()
                        P.dma('sp', g_[:, 0:T], pgt[dc * 128:(dc + 1) * 128, t0:t0 + T], B_g, B_pgt)
                        pt, B_pt = nps()
                        for kc in range(XC):
                            P.op('pe', 'matmul', KW(pt[:, 0:T], lhsT=wt[:, kc * 128:(kc + 1) * 128], rhs=A3[:, kc, 0:T], start=(kc == 0), stop=(kc == XC - 1)),
                                 reads=[B_wt, B_A], writes=[B_pt], inc=(kc == XC - 1))
                        P.op('dve', 'tensor_tensor', KW(out=mix[:, dc, 0:T], in0=pt[:, 0:T], in1=g_[:, 0:T], op=ALU.mult), reads=[B_pt, B_g], writes=[B_mix])
                    P.dma('sp', A3[:, 0:RC, 0:T], rgyT[:, t0:t0 + T].rearrange("(k p) t -> p k t", p=128), B_A, B_rgyT)
                    for dc in range(KD):
                        wt, B_wt = nw()
                        if (t0 == 0 or not CACHE_W) and not (PRECAST and upto >= 4):
                            P.dma('pool', wt[:, 0:RC * 128], wro_t[dc], B_wt, B_w, max_dma_last_dim=4096)
                            if CACHE_W and SEQ > T6:
                                P.dma('sp', wroc[dc], wt[:, 0:RC * 128], B_c6["wroc"], B_wt)
                        else:
                            P.dma('pool', wt[:, 0:RC * 128], wroc[dc], B_wt, B_c6["wroc"])
                        g_, B_g = ng_()
                        P.dma('sp', g_[:, 0:T], pgt[(KD + dc) * 128:(KD + dc + 1) * 128, t0:t0 + T], B_g, B_pgt)
                        pt, B_pt = nps()
                        for kc in range(RC):
                            P.op('pe', 'matmul', KW(pt[:, 0:T], lhsT=wt[:, kc * 128:(kc + 1) * 128], rhs=A3[:, kc, 0:T], start=(kc == 0), stop=(kc == RC - 1)),
                                 reads=[B_wt, B_A], writes=[B_pt], inc=(kc == RC - 1))
                        t_, B_t = tmp[dc % 2]
                        P.op('dve', 'tensor_tensor', KW(out=t_[:, 0:T], in0=pt[:, 0:T], in1=g_[:, 0:T], op=ALU.mult), reads=[B_pt, B_g], writes=[B_t])
                        P.op('pool', 'tensor_tensor', KW(out=mix[:, dc, 0:T], in0=mix[:, dc, 0:T], in1=t_[:, 0:T], op=ALU.add), reads=[B_mix, B_t], writes=[B_mix])
                    for dc in range(KD):
                        wt, B_wt = nw()
                        if (t0 == 0 or not CACHE_W) and not (PRECAST and upto >= 4):
                            P.dma('pool', wt[:, 0:KD * 128], wo_t[dc], B_wt, B_w, max_dma_last_dim=4096)
                            if CACHE_W and SEQ > T6:
                                P.dma('sp', woc[dc], wt[:, 0:KD * 128], B_c6["woc"], B_wt)
                        else:
                            P.dma('pool', wt[:, 0:KD * 128], woc[dc], B_wt, B_c6["woc"])
                        pt, B_pt = nps()
                        for kc in range(KD):
                            P.op('pe', 'matmul', KW(pt[:, 0:T], lhsT=wt[:, kc * 128:(kc + 1) * 128], rhs=mix[:, kc, 0:T], start=(kc == 0), stop=(kc == KD - 1)),
                                 reads=[B_wt, B_mix], writes=[B_pt], inc=(kc == KD - 1))
                        P.op('act', 'activation', KW(out=mT[:, dc, 0:T], in_=pt[:, 0:T], func=AF.Copy), reads=[B_pt], writes=[B_mT])
                    rms_rstd(nps, sq, rstd, B_rstd, lambda kc: mT[:, kc, 0:T], [B_mT], KD, D, T)
                    for kc in range(KD):
                        hc_, B_hc = hch[kc % 3]
                        P.dma('sp', hc_[:, 0:T], h1T[kc * 128:(kc + 1) * 128, t0:t0 + T], B_hc, B_h1T)
                        t_, B_t = tmp[kc % 2]
                        P.op('dve', 'tensor_tensor', KW(out=t_[:, 0:T], in0=mT[:, kc, 0:T], in1=rstd[:, 0:T], op=ALU.mult), reads=[B_mT, B_rstd], writes=[B_t])
                        P.op('dve', 'scalar_tensor_tensor', KW(out=h2b[:, kc, 0:T], in0=t_[:, 0:T], scalar=co[:, kc, 0:1], in1=hc_[:, 0:T],
                                                               op0=ALU.mult, op1=ALU.add), reads=[B_t, B_co, B_hc], writes=[B_A])
                    P.dma('sp', h2T[:, t0:t0 + T].rearrange("(k p) t -> p k t", p=128), h2b[:, 0:KD, 0:T], B_h2T, B_A)
                    rms_rstd(nps, sq, rstd, B_rstd, lambda kc: h2b[:, kc, 0:T], [B_A], KD, D, T)
                    for kc in range(KD):
                        t_, B_t = tmp[kc % 2]
                        P.op('dve', 'tensor_tensor', KW(out=t_[:, 0:T], in0=h2b[:, kc, 0:T], in1=rstd[:, 0:T], op=ALU.mult), reads=[B_A, B_rstd], writes=[B_t])
                        P.op('act', 'activation', KW(out=mT[:, kc, 0:T], in_=t_[:, 0:T], func=AF.Identity, scale=s1[:, kc, 0:1], bias=s2[:, kc, 0:1]),
                             reads=[B_t, B_s1, B_s2], writes=[B_mT])
                    P.dma('sp', u2T[:, t0:t0 + T].rearrange("(k p) t -> p k t", p=128), mT[:, :, 0:T], B_u2T, B_mT)
                P.barrier()
        if 'h2' in dbg:
            o = dbgout("h2T", [D, SEQ])
            P.dma('sp', o[:, :], h2T[:, :], Buf("dbgh2"), B_h2T)

        if upto >= 7:
            ffn_phase("f2", 1, h2T, B_h2T, make_passes(C, SEQ, 0, TP), (u2T, B_u2T), outT, B_outT, None, None, 2, None)
        P.barrier()
        P.emit()
    return nc, dbg_out


def _tile_w(W):
    K, N = W.shape
    KC, OC = K // 128, N // 128
    return np.ascontiguousarray(W.reshape(KC, 128, OC, 128).transpose(2, 1, 0, 3)).reshape(OC, 128, KC * 128)


def _colvec(v, n=None):
    v = np.asarray(v, np.float32)
    return np.ascontiguousarray(v.reshape(-1, 128).T)


def _pad_rg(v, C, fill=0.0):
    v = np.asarray(v, np.float32)
    lead = v.shape[:-1]
    vb = v.reshape(lead + (16, C.RGB))
    out = np.full(lead + (16, C.NBC * 128), fill, np.float32)
    out[..., :C.RGB] = vb
    return out.reshape(lead + (C.RC * 128,))


def prep_shared(C, inp):
    S = {}
    S["w_ada_t"] = _tile_w(inp["w_ada"][0])
    S["b_adaT"] = _colvec(inp["b_ada"][0])
    S["normgT"] = np.ascontiguousarray(np.concatenate([_colvec(inp["norm_g"][0, j]) for j in range(6)], axis=1))
    for i in range(2):
        S["wup%d_t" % i] = _tile_w(inp["ffn_w_up"][0, i])
        S["wdn%d_t" % i] = _tile_w(inp["ffn_w_down"][0, i])
    w_in = inp["w_in"][0]
    D = C.D
    cols = []
    zpad = np.zeros((D, 1), np.float32)

    def take(idx):
        return w_in[:, idx]
    parts = []
    parts.append(w_in[:, 0:C.S1])
    parts.append(w_in[:, C.S1:C.S1 + C.DI])
    parts.append(w_in[:, C.S1 + C.DI:C.S1 + C.DI + C.GN])
    parts.append(w_in[:, C.S1 + C.DI + C.GN:C.S2])
    for d in range(2):
        blk = np.zeros((D, 128), np.float32)
        blk[:, :C.H] = w_in[:, C.S2 + d * C.H:C.S2 + (d + 1) * C.H]
        parts.append(blk)
    for s in (C.S3, C.S4):
        blk = np.zeros((D, 16, C.NBC * 128), np.float32)
        blk[:, :, :C.RGB] = w_in[:, s:s + C.RGW].reshape(D, 16, C.RGB)
        parts.append(blk.reshape(D, C.RC * 128))
    parts.append(w_in[:, C.S5:])
    S["win_t"] = _tile_w(np.concatenate(parts, axis=1))
    vec = np.zeros((128, C.NV), np.float32)
    cw = inp["ssd_conv_w"][0]
    nxc = C.XC + 16
    vec[:, C.V["cw"]:C.V["cw"] + nxc * 4] = cw.T.reshape(nxc, 128, 4).transpose(1, 0, 2).reshape(128, nxc * 4)
    vec[:, C.V["cb"]:C.V["cb"] + nxc] = _colvec(inp["ssd_conv_b"][0])
    for d in range(2):
        vec[:C.H, C.V["dtb"] + d] = inp["ssd_dt_bias"][0][d * C.H:(d + 1) * C.H]
        vec[:C.H, C.V["alog"] + d] = inp["ssd_a_log"][0, d]
    vec[:, C.V["sng"]:C.V["sng"] + C.XC] = _colvec(inp["ssd_norm_g"][0])
    rcw = _pad_rg(inp["rg_conv_w"][0], C)
    vec[:, C.V["rcw"]:C.V["rcw"] + C.RC * 4] = rcw.T.reshape(C.RC, 128, 4).transpose(1, 0, 2).reshape(128, C.RC * 4)
    vec[:, C.V["rcb"]:C.V["rcb"] + C.RC] = _colvec(_pad_rg(inp["rg_conv_b"][0], C))
    for d in range(2):
        vec[:, C.V["rba"] + d * C.RC:C.V["rba"] + (d + 1) * C.RC] = _colvec(_pad_rg(inp["rg_b_a"][0, d], C))
        vec[:, C.V["rbx"] + d * C.RC:C.V["rbx"] + (d + 1) * C.RC] = _colvec(_pad_rg(inp["rg_b_x"][0, d], C))
        vec[:, C.V["rlam"] + d * C.RC:C.V["rlam"] + (d + 1) * C.RC] = _colvec(_pad_rg(inp["rg_lam"][0, d], C))
    S["vecs"] = vec
    dr = np.zeros((1, 128), np.float32)
    dr[0, :C.H] = inp["ssd_d"][0]
    S["drow"] = dr
    NB = C.NBC
    rg = np.zeros((2, 2, 16, NB, 128, NB, 128), np.float32)
    for d in range(2):
        for ty, nm in enumerate(("rg_w_a", "rg_w_x")):
            w = np.zeros((16, NB * 128, NB * 128), np.float32)
            w[:, :C.RGB, :C.RGB] = inp[nm][0, d]
            rg[d, ty] = w.reshape(16, NB, 128, NB, 128).transpose(0, 3, 2, 1, 4)
    S["rgw_t"] = np.ascontiguousarray(rg.reshape(64 * NB, 128, NB * 128))
    S["wso_t"] = _tile_w(inp["w_ssd_out"][0])
    wro = np.zeros((16, NB * 128, D), np.float32)
    wro[:, :C.RGB, :] = inp["w_rg_out"][0].reshape(16, C.RGB, D)
    S["wro_t"] = _tile_w(wro.reshape(C.RC * 128, D))
    S["wo_t"] = _tile_w(inp["w_out"][0])
    return S


def prep_core(C, inp, b):
    m = {}
    m["xT"] = np.ascontiguousarray(np.concatenate([inp["x"][b].T, inp["ctx"][b].T], axis=1).astype(np.float32))
    sc = np.stack([_colvec(inp["c"][b]), _colvec(inp["c_ctx"])], axis=2)
    m["scT"] = np.ascontiguousarray(sc.reshape(128, C.KD * 2))
    return m


_CACHE = {}


def kernel(**inputs):
    C = Cfg()
    inp = {k: np.asarray(v) for k, v in inputs.items()}
    nb = inp["x"].shape[0]
    S = prep_shared(C, inp)
    if "nc" not in _CACHE:
        _CACHE["nc"] = build(C)[0]
    nc = _CACHE["nc"]
    in_maps = []
    for b in range(nb):
        m = dict(S)
        m.update(prep_core(C, inp, b))
        in_maps.append(m)
    res = run_bass_kernel_spmd(nc, in_maps, core_ids=list(range(nb)))
    out = np.stack([np.ascontiguousarray(r["outT"].T) for r in res.results], axis=0)
    return out.astype(np.float32)
```
